# Optimizing a Trainium2 kernel written in Bass

```python
import math
import jax, jax.numpy as jnp
from jax import lax
import numpy as np

D_MODEL = 1024
BATCH = 16
SEQ = 2048
DEPTH = 1
DEC_BATCH = 8
DEC_SEQ = 32
PAST_LEN = 1024

CHUNK = 64
PLE_DIM = 256
EPS = 1e-6
NEG_INF = -1e30
N_HEADS_A = 16
HEAD_DIM_A = 64
ATT_WIDTH = N_HEADS_A * HEAD_DIM_A
LEFT_CHUNKS = 8
ATT_PAST = LEFT_CHUNKS * CHUNK
BAND = ATT_PAST + CHUNK
REL_CLIP = 128
SSM_EXPAND = 2
D_INNER = SSM_EXPAND * D_MODEL
SSM_HEAD_DIM = 64
N_HEADS_S = D_INNER // SSM_HEAD_DIM
N_GROUPS = 4
HEADS_PER_GROUP = N_HEADS_S // N_GROUPS
D_STATE = 128
SSM_CONV = 4
CONV_DIM = D_INNER + 2 * N_GROUPS * D_STATE
D_FF = 3 * D_MODEL
FFN_CONV = 3
IN_SIZES = (ATT_WIDTH, ATT_WIDTH, ATT_WIDTH, D_INNER, CONV_DIM, N_HEADS_S, 2 * D_MODEL)
N_IN = 3 * ATT_WIDTH + D_INNER + CONV_DIM + N_HEADS_S + 2 * D_MODEL

kernel_name = 'hybrid_streaming_band_attn_ssd_convffn_step'


def rmsnorm(x, g):
    xf = x.astype(jnp.float32)
    y = xf * lax.rsqrt(jnp.mean(xf * xf, axis=-1, keepdims=True) + EPS)
    return (y * g.astype(jnp.float32)).astype(x.dtype)


def causal_dwconv(x_ext, w, b):
    width = w.shape[0]
    l = x_ext.shape[1] - width + 1
    out = b
    for i in range(width):
        out = out + x_ext[:, i:i + l] * w[i]
    return out


def rel_bias_lookup(table, rel):
    return table[:, jnp.clip(rel, -REL_CLIP, REL_CLIP) + REL_CLIP].astype(jnp.float32)


def attend(q, k, v, bias):
    logits = jnp.einsum('bihd,bjhd->bhij', q.astype(jnp.float32), k.astype(jnp.float32)) * (HEAD_DIM_A ** -0.5)
    probs = jax.nn.softmax(logits + bias[None], axis=-1)
    return jnp.einsum('bhij,bjhd->bihd', probs, v.astype(jnp.float32))


def band_attention(q, k, v, table):
    b, l = q.shape[:2]
    nc = l // CHUNK
    pad = ((0, 0), (ATT_PAST, 0), (0, 0), (0, 0))
    kp = jnp.pad(k, pad)
    vp = jnp.pad(v, pad)
    qc = q.reshape(b, nc, CHUNK, N_HEADS_A, HEAD_DIM_A)
    rel = (ATT_PAST + jnp.arange(CHUNK))[:, None] - jnp.arange(BAND)[None, :]
    bias = rel_bias_lookup(table, rel)

    def one_chunk(c):
        qi = lax.dynamic_index_in_dim(qc, c, axis=1, keepdims=False)
        kb = lax.dynamic_slice_in_dim(kp, c * CHUNK, BAND, axis=1)
        vb = lax.dynamic_slice_in_dim(vp, c * CHUNK, BAND, axis=1)
        valid = (c * CHUNK - ATT_PAST + jnp.arange(BAND)) >= 0
        return attend(qi, kb, vb, jnp.where(valid[None, None, :], bias, NEG_INF))

    out = lax.map(one_chunk, jnp.arange(nc))
    return jnp.moveaxis(out, 0, 1).reshape(b, l, ATT_WIDTH)


def cached_attention(q, k, v, ck, cv, table):
    b, t = q.shape[:2]
    lc = ck.shape[1]
    kk = jnp.concatenate([ck.astype(k.dtype), k], axis=1)
    vv = jnp.concatenate([cv.astype(v.dtype), v], axis=1)
    rel = (lc + jnp.arange(t))[:, None] - jnp.arange(lc + t)[None, :]
    out = attend(q, kk, vv, rel_bias_lookup(table, rel))
    return out.reshape(b, t, ATT_WIDTH)


def ssd(xs, dt, a_log, bm, cm, h0):
    b, l = xs.shape[:2]
    q = min(CHUNK, l)
    nc = l // q
    a_neg = -jnp.exp(a_log.astype(jnp.float32)).reshape(N_GROUPS, HEADS_PER_GROUP)
    xg = xs.reshape(b, l, N_GROUPS, HEADS_PER_GROUP, SSM_HEAD_DIM)
    dtg = dt.reshape(b, l, N_GROUPS, HEADS_PER_GROUP)
    ag = dtg * a_neg
    tri = (jnp.arange(q)[:, None] >= jnp.arange(q)[None, :])[None, :, :, None, None]

    def chunks(t):
        return jnp.moveaxis(t.reshape((b, nc, q) + t.shape[2:]), 1, 0)

    def step(hc, inp):
        xc, dtc, ac, bc, cc = inp
        acs = jnp.cumsum(ac, axis=1)
        seg = acs[:, :, None] - acs[:, None, :]
        decay = jnp.where(tri, jnp.exp(jnp.where(tri, seg, 0.0)), 0.0)
        cb = jnp.einsum('bign,bjgn->bijg', cc, bc)
        dx = dtc[..., None] * xc
        y = jnp.einsum('bijg,bijgr,bjgrp->bigrp', cb, decay, dx)
        y = y + jnp.einsum('bign,bgrpn->bigrp', cc, hc) * jnp.exp(acs)[..., None]
        last = acs[:, -1]
        wj = jnp.exp(last[:, None] - acs)[..., None] * dx
        h_new = hc * jnp.exp(last)[..., None, None] + jnp.einsum('bjgn,bjgrp->bgrpn', bc, wj)
        return h_new, y

    h_init = h0.reshape(b, N_GROUPS, HEADS_PER_GROUP, SSM_HEAD_DIM, D_STATE)
    h_fin, ys = lax.scan(step, h_init, (chunks(xg), chunks(dtg), chunks(ag), chunks(bm), chunks(cm)))
    y = jnp.moveaxis(ys, 0, 1).reshape(b, l, N_HEADS_S, SSM_HEAD_DIM)
    return y, h_fin.reshape(b, N_HEADS_S, SSM_HEAD_DIM, D_STATE)


def layer(x, pe, kv_cache, ssm_ctx, h0, ffn_ctx, prm):
    (g_mix, w_in, b_gate, g_q, g_k, rel_bias, conv_ssm_w, conv_ssm_b, dt_bias, a_log, d_skip, g_ssm,
     w_proj_a, w_proj_b, w_out, g_ffn, w_up, ffn_conv_w, ffn_conv_b, w_down, g_ple, w_ple_gate, w_ple) = prm
    b, l, _ = x.shape
    h = rmsnorm(x, g_mix)
    proj = h @ w_in
    split_at = []
    acc = 0
    for s in IN_SIZES[:-1]:
        acc += s
        split_at.append(acc)
    q, k, v, z, xbc, dt_raw, gates = jnp.split(proj, split_at, axis=-1)

    q = rmsnorm(q.reshape(b, l, N_HEADS_A, HEAD_DIM_A), g_q)
    k = rmsnorm(k.reshape(b, l, N_HEADS_A, HEAD_DIM_A), g_k)
    v = v.reshape(b, l, N_HEADS_A, HEAD_DIM_A)
    if kv_cache is None:
        ya = band_attention(q, k, v, rel_bias)
        new_k, new_v = k[:, -ATT_PAST:], v[:, -ATT_PAST:]
    else:
        ya = cached_attention(q, k, v, kv_cache[0], kv_cache[1], rel_bias)
        new_k, new_v = k, v
    ya = ya.astype(x.dtype)

    xbc_ext = jnp.concatenate([ssm_ctx.astype(xbc.dtype), xbc], axis=1)
    new_ssm_ctx = xbc_ext[:, -(SSM_CONV - 1):]
    xbc_c = jax.nn.silu(causal_dwconv(xbc_ext, conv_ssm_w, conv_ssm_b)).astype(jnp.float32)
    xs, bm, cm = jnp.split(xbc_c, [D_INNER, D_INNER + N_GROUPS * D_STATE], axis=-1)
    xs = xs.reshape(b, l, N_HEADS_S, SSM_HEAD_DIM)
    dt = jax.nn.softplus(dt_raw.astype(jnp.float32) + dt_bias.astype(jnp.float32))
    y, h_fin = ssd(xs, dt, a_log, bm.reshape(b, l, N_GROUPS, D_STATE),
                   cm.reshape(b, l, N_GROUPS, D_STATE), h0.astype(jnp.float32))
    y = (y + d_skip.astype(jnp.float32)[:, None] * xs).reshape(b, l, D_INNER)
    yb = rmsnorm(y * jax.nn.silu(z.astype(jnp.float32)), g_ssm).astype(x.dtype)

    gate_a, gate_b = jnp.split(jax.nn.sigmoid(gates + b_gate), 2, axis=-1)
    x = x + (gate_a * (ya @ w_proj_a) + gate_b * (yb @ w_proj_b)) @ w_out

    u = rmsnorm(x, g_ffn) @ w_up
    u_ext = jnp.concatenate([ffn_ctx.astype(u.dtype), u], axis=1)
    new_ffn_ctx = u_ext[:, -(FFN_CONV - 1):]
    ug, uv = jnp.split(causal_dwconv(u_ext, ffn_conv_w, ffn_conv_b), 2, axis=-1)
    x = x + (jax.nn.gelu(ug, approximate=True) * uv) @ w_down

    x = x + jax.nn.sigmoid(rmsnorm(x, g_ple) @ w_ple_gate) * (pe.astype(x.dtype) @ w_ple)
    return x, (new_k, new_v, h_fin, new_ssm_ctx, new_ffn_ctx)


def setup_inputs(seed: int = 0) -> dict:
    key = jax.random.key(seed)
    ks = jax.random.split(key, 40)
    f32 = jnp.float32

    def nrm(k, shape, s):
        return s * jax.random.normal(k, shape, f32)

    l_att = min(ATT_PAST, PAST_LEN)
    dt0 = jnp.exp(jax.random.uniform(ks[20], (DEPTH, N_HEADS_S), f32, math.log(1e-3), math.log(1e-1)))
    dt_bias = dt0 + jnp.log(-jnp.expm1(-dt0))
    a_log = jnp.log(jax.random.uniform(ks[21], (DEPTH, N_HEADS_S), f32, 1.0, 16.0))
    return {
        'x_prompt': nrm(ks[0], (BATCH, SEQ, D_MODEL), 1.0),
        'x_sample': nrm(ks[1], (DEC_BATCH, DEC_SEQ, D_MODEL), 1.0),
        'cache_k': nrm(ks[2], (DEPTH, DEC_BATCH, l_att, N_HEADS_A, HEAD_DIM_A), 1.0),
        'cache_v': nrm(ks[3], (DEPTH, DEC_BATCH, l_att, N_HEADS_A, HEAD_DIM_A), 1.0),
        'state_ssm': nrm(ks[4], (DEPTH, DEC_BATCH, N_HEADS_S, SSM_HEAD_DIM, D_STATE), 0.1),
        'state_conv_ssm': nrm(ks[5], (DEPTH, DEC_BATCH, SSM_CONV - 1, CONV_DIM), 1.0),
        'state_conv_ffn': nrm(ks[6], (DEPTH, DEC_BATCH, FFN_CONV - 1, 2 * D_FF), 1.0),
        'p_prompt': nrm(ks[7], (DEPTH, BATCH, SEQ, PLE_DIM), 1.0),
        'p_sample': nrm(ks[8], (DEPTH, DEC_BATCH, DEC_SEQ, PLE_DIM), 1.0),
        'g_mix': 1.0 + nrm(ks[9], (DEPTH, D_MODEL), 0.05),
        'w_in': nrm(ks[10], (DEPTH, D_MODEL, N_IN), D_MODEL ** -0.5),
        'b_gate': nrm(ks[11], (DEPTH, 2 * D_MODEL), 0.01),
        'g_q': 1.0 + nrm(ks[12], (DEPTH, HEAD_DIM_A), 0.05),
        'g_k': 1.0 + nrm(ks[13], (DEPTH, HEAD_DIM_A), 0.05),
        'rel_bias': nrm(ks[14], (DEPTH, N_HEADS_A, 2 * REL_CLIP + 1), 0.1),
        'conv_ssm_w': nrm(ks[15], (DEPTH, SSM_CONV, CONV_DIM), SSM_CONV ** -0.5),
        'conv_ssm_b': nrm(ks[16], (DEPTH, CONV_DIM), 0.01),
        'dt_bias': dt_bias,
        'a_log': a_log,
        'd_skip': 1.0 + nrm(ks[17], (DEPTH, N_HEADS_S), 0.1),
        'g_ssm': 1.0 + nrm(ks[18], (DEPTH, D_INNER), 0.05),
        'w_proj_a': nrm(ks[19], (DEPTH, ATT_WIDTH, D_MODEL), ATT_WIDTH ** -0.5),
        'w_proj_b': nrm(ks[22], (DEPTH, D_INNER, D_MODEL), D_INNER ** -0.5),
        'w_out': nrm(ks[23], (DEPTH, D_MODEL, D_MODEL), D_MODEL ** -0.5),
        'g_ffn': 1.0 + nrm(ks[24], (DEPTH, D_MODEL), 0.05),
        'w_up': nrm(ks[25], (DEPTH, D_MODEL, 2 * D_FF), D_MODEL ** -0.5),
        'ffn_conv_w': nrm(ks[26], (DEPTH, FFN_CONV, 2 * D_FF), FFN_CONV ** -0.5),
        'ffn_conv_b': nrm(ks[27], (DEPTH, 2 * D_FF), 0.01),
        'w_down': nrm(ks[28], (DEPTH, D_FF, D_MODEL), D_FF ** -0.5),
        'g_ple': 1.0 + nrm(ks[29], (DEPTH, D_MODEL), 0.05),
        'w_ple_gate': nrm(ks[30], (DEPTH, D_MODEL, D_MODEL), D_MODEL ** -0.5),
        'w_ple': nrm(ks[31], (DEPTH, PLE_DIM, D_MODEL), PLE_DIM ** -0.5),
    }


def reference(x_prompt, x_sample, cache_k, cache_v, state_ssm, state_conv_ssm, state_conv_ffn,
              p_prompt, p_sample, g_mix, w_in, b_gate, g_q, g_k, rel_bias, conv_ssm_w, conv_ssm_b,
              dt_bias, a_log, d_skip, g_ssm, w_proj_a, w_proj_b, w_out, g_ffn, w_up, ffn_conv_w,
              ffn_conv_b, w_down, g_ple, w_ple_gate, w_ple):
    bp = x_prompt.shape[0]
    yp, ys = x_prompt, x_sample
    sp, ss = [], []
    for i in range(DEPTH):
        prm = (g_mix[i], w_in[i], b_gate[i], g_q[i], g_k[i], rel_bias[i], conv_ssm_w[i], conv_ssm_b[i],
               dt_bias[i], a_log[i], d_skip[i], g_ssm[i], w_proj_a[i], w_proj_b[i], w_out[i], g_ffn[i],
               w_up[i], ffn_conv_w[i], ffn_conv_b[i], w_down[i], g_ple[i], w_ple_gate[i], w_ple[i])
        yp, st_p = layer(yp, p_prompt[i], None,
                         jnp.zeros((bp, SSM_CONV - 1, CONV_DIM), yp.dtype),
                         jnp.zeros((bp, N_HEADS_S, SSM_HEAD_DIM, D_STATE), jnp.float32),
                         jnp.zeros((bp, FFN_CONV - 1, 2 * D_FF), yp.dtype), prm)
        ys, st_s = layer(ys, p_sample[i], (cache_k[i], cache_v[i]), state_conv_ssm[i], state_ssm[i],
                         state_conv_ffn[i], prm)
        sp.append(st_p)
        ss.append(st_s)

    def stack(lst, j):
        return jnp.stack([s[j] for s in lst], axis=0)

    return (yp, ys,
            stack(sp, 0), stack(sp, 1), stack(sp, 2), stack(sp, 3), stack(sp, 4),
            stack(ss, 0), stack(ss, 1), stack(ss, 2), stack(ss, 3), stack(ss, 4))
```

```python
import numpy as np
from contextlib import ExitStack
import concourse.bass as bass
import concourse.mybir as mybir
from concourse.bass_utils import run_bass_kernel_spmd

F32 = mybir.dt.float32
BF16 = mybir.dt.bfloat16
AF = mybir.ActivationFunctionType
ALU = mybir.AluOpType
AX = mybir.AxisListType
ENGS = ('pe', 'act', 'dve', 'pool', 'sp')
RHS_ENG = 'pool'
POOL_CONV = True
WSCRATCH = True
WLOAD_ENG = 'sp'
EPS = 1e-6
D = 1024
NIN = 10272
NSLOT = 3


class StopBuild(Exception):
    pass


class Sched:
    def __init__(self):
        self.ops = []
        self.last_w = {}
        self.readers = {}
        self.eng_count = {e: 0 for e in ENGS}
        self.dma_count = {}
        self.last_dma = {}

    def add(self, eng, fn, reads=(), writes=(), dma=None):
        writes = list(writes) + [r for r in reads if r.startswith('ps')]
        reads = [r for r in reads if not r.startswith('ps')]
        idx = len(self.ops)
        deps = set()
        for r in reads:
            w = self.last_w.get(r)
            if w is not None:
                deps.add(w)
        for w_ in writes:
            w = self.last_w.get(w_)
            if w is not None:
                deps.add(w)
            deps.update(self.readers.get(w_, ()))
        if fn is None:
            assert not writes
            sig = None
        elif dma is None:
            self.eng_count[eng] += 1
            sig = ('e:' + eng, self.eng_count[eng])
        else:
            self.dma_count[dma] = self.dma_count.get(dma, 0) + 1
            sig = ('d:' + dma, 16 * self.dma_count[dma])
            prev = self.last_dma.get(dma)
            if prev is not None:
                deps.add(prev)
            self.last_dma[dma] = idx
        deps.discard(idx)
        self.ops.append(dict(eng=eng, fn=fn, deps=sorted(deps), sig=sig, dma=dma))
        for r in reads:
            self.readers.setdefault(r, []).append(idx)
        for w_ in writes:
            self.last_w[w_] = idx
            self.readers[w_] = []
        return idx

    def emit(self, nc, stack):
        names = ['e:' + e for e in ENGS] + ['d:' + k for k in self.dma_count]
        sems = {}
        for n in names:
            sems[n] = stack.enter_context(nc.semaphore(n.replace(':', '_')))
        know = {e: {} for e in ENGS}
        opknow = [None] * len(self.ops)
        waits = [None] * len(self.ops)
        for i, op in enumerate(self.ops):
            e = op['eng']
            k = know[e]
            m = {}
            for d in op['deps']:
                dop = self.ops[d]
                sn, sv = dop['sig']
                if k.get(sn, 0) >= sv:
                    continue
                if sn == 'e:pe' and e == 'pe':
                    continue
                m[sn] = max(m.get(sn, 0), sv)
                for a, b in opknow[d].items():
                    if k.get(a, 0) < b:
                        k[a] = b
                if k.get(sn, 0) < sv:
                    k[sn] = sv
            waits[i] = list(m.items())
            opknow[i] = dict(k)
        block = stack.enter_context(nc.Block())
        per_eng = {e: [i for i, op in enumerate(self.ops) if op['eng'] == e] for e in ENGS}

        def make(e):
            def body(engine):
                for i in per_eng[e]:
                    op = self.ops[i]
                    for sn, sv in waits[i]:
                        engine.wait_ge(sems[sn], sv)
                    if op['fn'] is None:
                        continue
                    ins = op['fn'](engine)
                    sn, sv = op['sig']
                    ins.then_inc(sems[sn], 16 if op['dma'] is not None else 1)
            return body

        block.tensor(make('pe'))
        block.scalar(make('act'))
        block.vector(make('dve'))
        block.gpsimd(make('pool'))
        block.sync(make('sp'))


_off = {}
_n = 0
for _name, _w in [('gmix', 8), ('gffn', 8), ('gple', 8), ('gq', 1), ('gk', 1), ('bgate', 16), ('cw', 96), ('cb', 24),
                  ('fw', 144), ('fb', 48), ('dtb', 32), ('alog', 32), ('dskip', 32), ('gssm', 16), ('ch', 16)]:
    _off[_name] = _n
    _n += _w
NCST = _n


def build(SEQ, NSEQ=2, SAMPLE=True, DBG=False, STOP=99):
    NT = SEQ // 512
    LK = min(512, SEQ)
    nc = bass.Bass("TRN2", target_bir_lowering=False)
    S = Sched()
    A = S.add

    def din(name, shape):
        return nc.dram_tensor(name, list(shape), F32, kind="ExternalInput").ap()

    def dout(name, shape):
        return nc.dram_tensor(name, list(shape), F32, kind="ExternalOutput").ap()

    xp = din("xp", [NSEQ, SEQ, D])
    ppT = din("ppT", [NSEQ, 256, SEQ])
    xsm = din("xsm", [32, D])
    psT = din("psT", [256, 32])
    ckT = din("ckT", [128, 8, 512])
    cv = din("cv", [512, D])
    st_ssmT = din("st_ssmT", [128, 2048])
    st_convT = din("st_convT", [128, 24, 3])
    st_ffnT = din("st_ffnT", [128, 48, 2])
    cst_d = din("cst", [128, NCST])
    biasx = din("biasx", [128, 16 * 256])
    mk_d = din("mk", [128, 5 * 128])
    w_in = din("w_in", [D, NIN])
    w_pa = din("w_pa", [D, D])
    w_pb = din("w_pb", [2048, D])
    w_out = din("w_out", [D, D])
    w_up = din("w_up", [D, 6144])
    w_down = din("w_down", [3072, D])
    w_pg = din("w_pg", [D, D])
    w_ple = din("w_ple", [256, D])

    yp = dout("yp", [NSEQ, SEQ, D])
    kp = dout("kp", [NSEQ, LK, D])
    vp = dout("vp", [NSEQ, LK, D])
    ssmp = dout("ssmp", [NSEQ, 2048, 128])
    cssmp = dout("cssmp", [NSEQ, 3, 3072])
    cffnp = dout("cffnp", [NSEQ, 2, 6144])
    ysm = dout("ysm", [32, D])
    ksm = dout("ksm", [32, D])
    vsm = dout("vsm", [32, D])
    ssms = dout("ssms", [1, 2048, 128])
    cssms = dout("cssms", [1, 3, 3072])
    cffns = dout("cffns", [1, 2, 6144])
    out_res = []
    if DBG:
        ntl = NSEQ * (SEQ // 512) + 1
        d_ya = nc.dram_tensor("d_ya", [ntl, 128, 8 * 512], BF16, kind="ExternalOutput").ap()
        d_yb = nc.dram_tensor("d_yb", [ntl, 128, 16 * 512], BF16, kind="ExternalOutput").ap()
        d_x = nc.dram_tensor("d_x", [ntl, 128, 4 * 1024], F32, kind="ExternalOutput").ap()
        d_x2 = nc.dram_tensor("d_x2", [ntl, 128, 4 * 1024], F32, kind="ExternalOutput").ap()
        d_m = nc.dram_tensor("d_m", [ntl, 128, 8 * 512], BF16, kind="ExternalOutput").ap()
        d_g = nc.dram_tensor("d_g", [ntl, 128, 16 * 512], BF16, kind="ExternalOutput").ap()
    tcount = [0]

    with ExitStack() as st:
        def sb(name, shape, dt):
            return st.enter_context(nc.sbuf_tensor(name, list(shape), dt))

        xt = sb("xt", [128, 4, D], F32)
        hT = sb("hT", [128, 8, 512], BF16)
        R3 = sb("R3", [128, 8192], BF16)
        kTb = [sb("kTb%d" % i, [128, 8, 512], BF16) for i in range(2)]
        vb = [sb("vb%d" % i, [128, 4, D], BF16) for i in range(2)]
        yaT = sb("yaT", [128, 8, 512], BF16)
        R2 = sb("R2", [128, 8192], BF16)
        R1 = sb("R1", [128, 24, 512], BF16)
        SCR = sb("SCR", [128, 6656], F32)
        hst = sb("hst", [128, 2048], F32)
        hbf = sb("hbf", [128, 2048], BF16)
        Et = sb("Et", [128, 16, 256], BF16)
        wsl = [sb("wsl%d" % i, [128, 8, 512], BF16) for i in range(NSLOT)]
        xhalo = sb("xhalo", [128, 24, 3], F32)
        uhalo = sb("uhalo", [128, 48, 2], F32)
        cst = sb("cst_sb", [128, NCST], F32)
        mk = sb("mk_sb", [128, 5, 128], BF16)
        identf = sb("identf", [128, 128], F32)
        aneg = sb("aneg", [128, 32], F32)
        xn = sb("xn", [128, D], BF16)
        stg = sb("stg", [128, 2, 512], F32)
        pet = sb("pet", [128, 2, 512], BF16)
        dtr = sb("dtr", [128, 4, 32], F32)
        ssb = sb("ssb", [128, 16], F32)
        PS = st.enter_context(nc.psum_tensor("PS", [128, 4096], F32))

        def bank(i):
            return PS[:, i * 512:(i + 1) * 512]

        def bankb(i):
            return PS[:, i * 512:(i + 1) * 512].bitcast(BF16)

        identb = mk[:, 0, :]
        Vm = mk[:, 1, :]
        Um = mk[:, 2, :]
        onesb = mk[:, 3, :]
        blk1 = mk[:, 4, :]

        def C(name, i=0, n=1):
            o = _off[name] + i
            return cst[:, o:o + n]

        def scr_f(off_b, ncols):
            return SCR[:, off_b // 4: off_b // 4 + ncols]

        def scr_b(off_b, ncols):
            return SCR[:, off_b // 4: off_b // 4 + ncols // 2].bitcast(BF16)

        qT = R3[:, 0:4096].rearrange("p (c t) -> p c t", c=8)
        ybT = R3[:, :].rearrange("p (c t) -> p c t", c=16)
        pT = [R3[:, 4096 + i * 1024: 4096 + i * 1024 + 640] for i in range(3)]
        rden = [R3[:, 7168 + i * 512: 7168 + i * 512 + 256].bitcast(F32) for i in range(2)]
        zs = R2[:, :].rearrange("p (b c) -> p b c", b=4)
        t1 = R2[:, :].bitcast(F32).rearrange("p (c t) -> p c t", c=8)
        tmpE = R2[:, :].bitcast(F32).rearrange("p (h t) -> p h t", h=16)
        xin = R2[:, :].bitcast(F32).rearrange("p (b d) -> p b d", b=4)
        mT = yaT

        def r3q(hp):
            return 'R3_%d' % hp

        def r3pt(i):
            return ['R3_%d' % (8 + 2 * i), 'R3_%d' % (9 + 2 * i)]

        def r3rd(i):
            return ['R3_%d' % (14 + i)]

        SCRN = ['scr%d' % i for i in range(12)]

        dummy = sb("dummy_sb", [128, 8], F32)
        rhs_x = sb("rhs_x", [128, 2048], BF16)
        dg = sb("dg", [128, 128], BF16)

        def phase_barrier():
            A('dve', lambda e: e.memset(dummy[0:1, 0:1], 0.0), writes=SCRN)

        wplan = []

        def tile_plan():
            p = []
            for i in range(16):
                p.append((w_in, 0, 8, 512 * i, 512))
            p.append((w_in, 0, 8, 8192, 32))
            for i in range(4):
                p.append((w_in, 0, 8, 8224 + 512 * i, 512))
            for i in range(2):
                p.append((w_pa, 0, 8, 512 * i, 512))
            for i in range(2):
                for kg in range(2):
                    p.append((w_pb, kg * 1024, 8, 512 * i, 512))
            for i in range(2):
                p.append((w_out, 0, 8, 512 * i, 512))
            for i in range(6):
                p.append((w_up, 0, 8, 512 * i, 512))
                p.append((w_up, 0, 8, 3072 + 512 * i, 512))
            for i in range(2):
                for kg in range(3):
                    p.append((w_down, kg * 1024, 8, 512 * i, 512))
            for i in range(2):
                p.append((w_pg, 0, 8, 512 * i, 512))
                p.append((w_ple, 0, 2, 512 * i, 512))
            return p

        ntiles = NSEQ * NT + (1 if SAMPLE else 0)
        for _ in range(ntiles):
            wplan.extend(tile_plan())
        wstate = dict(issued=0, used=0)

        NPLAN = len(tile_plan())
        wscr = nc.dram_tensor("wscr", [NPLAN, 128, 4096], BF16, kind="Internal").ap() if WSCRATCH else None

        def w_issue(upto):
            while wstate['issued'] < min(upto, len(wplan)):
                i = wstate['issued']
                wap, r0, nk, c0, ncol = wplan[i]
                s = i % NSLOT
                j = i % NPLAN
                if WSCRATCH and i >= NPLAN:
                    src = wscr[j, :, 0:nk * ncol].rearrange("p (k n) -> p k n", k=nk)
                    A(WLOAD_ENG, lambda e, s=s, nk=nk, ncol=ncol, src=src: e.dma_start(out=wsl[s][:, 0:nk, 0:ncol], in_=src),
                      reads=['wscr%d' % j], writes=['w%d' % s], dma=('wh%d' if WLOAD_ENG == 'sp' else 'w%d') % s)
                else:
                    src = wap[r0:r0 + nk * 128, c0:c0 + ncol].rearrange("(k p) n -> p k n", p=128)
                    A('pool', lambda e, s=s, nk=nk, ncol=ncol, src=src: e.dma_start(out=wsl[s][:, 0:nk, 0:ncol], in_=src),
                      writes=['w%d' % s], dma='w%d' % s)
                    if WSCRATCH and ntiles > 1:
                        dst = wscr[j, :, 0:nk * ncol].rearrange("p (k n) -> p k n", k=nk)
                        A('sp', lambda e, s=s, nk=nk, ncol=ncol, dst=dst: e.dma_start(out=dst, in_=wsl[s][:, 0:nk, 0:ncol]),
                          reads=['w%d' % s], writes=['wscr%d' % j], dma='ws%d' % s)
                wstate['issued'] += 1

        def w_get(wap, c0, prefetch=True):
            i = wstate['used']
            assert wplan[i][0] is wap and wplan[i][3] == c0, (i, c0)
            w_issue(i + NSLOT if prefetch else i + 1)
            wstate['used'] += 1
            s = i % NSLOT
            return wsl[s], 'w%d' % s

        A('sp', lambda e: e.dma_start(out=cst[:], in_=cst_d), writes=['cst'], dma='c0')
        A('pool', lambda e: e.dma_start(out=mk[:].rearrange("p a b -> p (a b)"), in_=mk_d), writes=['mk'], dma='c1')
        A('sp', lambda e: e.dma_start(out=identf[:], in_=mk_d[:, 0:128]), writes=['identf'], dma='c2')
        A('sp', lambda e: e.dma_start(out=tmpE.rearrange("p h t -> p (h t)"), in_=biasx), writes=['tmpE'], dma='c3')
        A('dve', lambda e: e.tensor_tensor(out=tmpE, in0=tmpE, in1=C('ch', 0, 16).unsqueeze(2).to_broadcast([128, 16, 256]),
                                           op=ALU.subtract), reads=['cst', 'tmpE'], writes=['tmpE'])
        A('act', lambda e: e.activation(out=Et[:], in_=tmpE, func=AF.Exp), reads=['tmpE'], writes=['Et'])
        A('dve', lambda e: e.memset(Et[64:128, :, 128:192], 0.0), writes=['Et'])
        A('act', lambda e: e.activation(out=aneg[:], in_=C('alog', 0, 32), func=AF.Exp), reads=['cst'], writes=['aneg'])
        A('dve', lambda e: e.tensor_scalar(out=aneg[:], in0=aneg[:], scalar1=-1.0, scalar2=None, op0=ALU.mult),
          reads=['aneg'], writes=['aneg'])
        A('dve', lambda e: e.memset(dummy[0:1, 1:2], 0.0), reads=['tmpE', 'Et'], writes=['R2_%d' % i for i in range(16)])

        xnb = [xn, scr_b(24576, 1024)]

        def rms_A(b, bs, junk, from_xin=False):
            k = b % 2
            xb = (xin if from_xin else xt)[0:bs, b, :]
            xr = 'xt%d' % b
            sr_ = 'ssb%d' % k
            sc_ = ssb[:, 4 * k:4 * k + 4]
            xrs = ['R2_%d' % (4 * b + j_) for j_ in range(4)] if from_xin else [xr]
            A('act', lambda e: e.activation(out=junk[0:bs, :], in_=xb, func=AF.Square, accum_out=sc_[0:bs, 0:1]),
              reads=xrs, writes=['scr0', sr_])
            A('act', lambda e: e.activation(out=sc_[0:bs, 1:2], in_=sc_[0:bs, 0:1], func=AF.Sqrt, scale=1.0 / D, bias=EPS),
              reads=[sr_], writes=[sr_])
            A('dve', lambda e: e.reciprocal(out=sc_[0:bs, 2:3], in_=sc_[0:bs, 1:2]), reads=[sr_], writes=[sr_])
            A('dve', lambda e: e.tensor_scalar(out=xnb[k][0:bs, :], in0=xb, scalar1=sc_[0:bs, 2:3], scalar2=None, op0=ALU.mult),
              reads=xrs + [sr_], writes=['xn%d' % k])

        def rms_B(b, bs, gname):
            k = b % 2
            pb = 3

            def tr(e):
                ins = None
                for kc in range(8):
                    ins = e.transpose(out=bankb(pb)[:, kc * 128: kc * 128 + bs], in_=xnb[k][0:bs, kc * 128:(kc + 1) * 128],
                                      identity=identb[0:bs, 0:bs])
                return ins
            A('pe', tr, reads=['xn%d' % k, 'mk'], writes=['ps%d' % pb])
            A('dve', lambda e: e.tensor_tensor(
                out=hT[:, :, b * 128: b * 128 + bs],
                in0=bankb(pb).rearrange("p (k t) -> p k t", k=8)[:, :, 0:bs],
                in1=C(gname, 0, 8).unsqueeze(2).to_broadcast([128, 8, bs]), op=ALU.mult),
              reads=['ps%d' % pb, 'cst'], writes=['hT'])

        def rmsnorm_T(nb, bs, gname, junk, from_xin=False):
            for b in range(nb):
                rms_A(b, bs, junk, from_xin)
                if b >= 1:
                    rms_B(b - 1, bs, gname)
            rms_B(nb - 1, bs, gname)

        def mm_fm(e, out, slot, c, srcT, nk, ncol, koff=0, start=True, stop=True):
            ins = None
            for kc in range(nk):
                ins = e.matmul(out, lhsT=slot[:, kc, c * 128:(c + 1) * 128], rhs=srcT[:, koff + kc, 0:ncol],
                               start=(start and kc == 0), stop=(stop and kc == nk - 1))
            return ins

        def mm_tm(e, out, slot, srcT, b, bs, nk, ncols, koff=0, start=True, stop=True):
            ins = None
            for kc in range(nk):
                ins = e.matmul(out, lhsT=srcT[:, koff + kc, b * 128: b * 128 + bs], rhs=slot[:, kc, 0:ncols],
                               start=(start and kc == 0), stop=(stop and kc == nk - 1))
            return ins

        def out_dma(dst, src, reads, key):
            rn = 'out%d' % len(out_res)
            out_res.append(rn)
            A('sp', lambda e: e.dma_start(out=dst, in_=src), reads=reads, writes=[rn], dma=key)

        stgc = [0]

        def stage():
            i = stgc[0] % 2
            stgc[0] += 1
            return stg[:, i, :], 'stg%d' % i

        stopped = [False]

        def chk(x):
            if STOP <= x:
                raise StopBuild()

        tiles = [('p', s_, ti_) for s_ in range(NSEQ) for ti_ in range(NT)] + ([('s', 0, 0)] if SAMPLE else [])

        def emit_xprefetch(tidx):
            kind_, s_, ti_ = tiles[tidx]
            if kind_ == 's':
                A('sp', lambda e: e.dma_start(out=xin[0:32, 0, :], in_=xsm), writes=['R2_%d' % j_ for j_ in range(4)], dma='xin0')
            else:
                for b_ in range(4):
                    src = xp[s_, ti_ * 512 + b_ * 128: ti_ * 512 + (b_ + 1) * 128, :]
                    A('sp', lambda e, b_=b_, src=src: e.dma_start(out=xin[:, b_, :], in_=src),
                      writes=['R2_%d' % (4 * b_ + j_) for j_ in range(4)], dma='xin%d' % b_)

        def do_tile(kind, s, ti):
            if stopped[0]:
                return
            tidx = tiles.index((kind, s, ti))
            sample = (kind == 's')
            bs = 32 if sample else 128
            nb = 1 if sample else 4
            ncol = nb * bs
            first = sample or ti == 0
            last = sample or ti == NT - 1
            t0 = 0 if sample else ti * 512
            cur = 0 if sample else ti % 2
            prv = 1 - cur
            xsrc = xsm if sample else xp[s, t0:t0 + ncol, :]
            ydst = ysm if sample else yp[s, t0:t0 + ncol, :]
            k_dst = ksm if sample else kp[s]
            v_dst = vsm if sample else vp[s]
            kv_out = sample or (SEQ - (t0 + ncol) < LK)
            kv_row0 = 0 if sample else t0 - (SEQ - LK)
            xres = ['xt%d' % b for b in range(nb)]

            def cols(b):
                return slice(b * 128, b * 128 + bs)

            if first:
                if sample:
                    A('sp', lambda e: e.dma_start(out=hst[:], in_=st_ssmT), writes=['hst%d' % g for g in range(4)], dma='c0')
                    A('act', lambda e: e.activation(out=hbf[:], in_=hst[:], func=AF.Copy),
                      reads=['hst%d' % g for g in range(4)], writes=['hbf%d' % g for g in range(4)])
                    A('sp', lambda e: e.dma_start(out=xhalo[:], in_=st_convT), writes=['xhalo%d' % k for k in range(24)], dma='c2')
                    A('sp', lambda e: e.dma_start(out=uhalo[:], in_=st_ffnT), writes=['uhalo%d' % k for k in range(48)], dma='c3')
                    A('pool', lambda e: e.dma_start(out=kTb[1][:], in_=ckT), writes=['kT1_%d' % h for h in range(8)], dma='c1')
                    A('pool', lambda e: e.dma_start(out=vb[1][:], in_=cv.rearrange("(b p) d -> p b d", p=128)),
                      writes=['v1_%d' % b for b in range(4)], dma='c4')
                else:
                    A('dve', lambda e: e.memset(hst[:], 0.0), writes=['hst%d' % g for g in range(4)])
                    A('dve', lambda e: e.memset(hbf[:], 0.0), writes=['hbf%d' % g for g in range(4)])
                    A('dve', lambda e: e.memset(xhalo[:], 0.0), writes=['xhalo%d' % k for k in range(24)])
                    A('dve', lambda e: e.memset(uhalo[:], 0.0), writes=['uhalo%d' % k for k in range(48)])

            chk(0.1)
            if tidx == 0:
                emit_xprefetch(0)
            if sample:
                A('pool', lambda e: e.dma_start(out=pet[:, :, 0:32], in_=psT.rearrange("(c p) t -> p c t", p=128)),
                  writes=['pet'], dma='pet')
            else:
                A('pool', lambda e: e.dma_start(out=pet[:], in_=ppT[s, :, t0:t0 + 512].rearrange("(c p) t -> p c t", p=128)),
                  writes=['pet'], dma='pet')

            chk(0.2)
            phase_barrier()
            junk = scr_b(0, 1024)
            sqb = [scr_b(2048 + i * 1024, 512) for i in range(2)]
            qraw = [scr_f(4096 + i * 2048, 512) for i in range(2)]
            rs = [scr_f(8192 + i * 2048, 512) for i in range(2)]
            knf = scr_f(12288, 512)
            xraw = [scr_f(14336 + i * 2064, 516) for i in range(2)]
            cacc = [scr_f(18464 + i * 2048, 512) for i in range(2)]
            rmsnorm_T(nb, bs, 'gmix', junk, from_xin=True)

            chk(0.3)
            pcount = [0]

            def nextbank():
                i = pcount[0] % 2
                pcount[0] += 1
                return i

            def qk_chunk(slot, wr, c, hp, is_k):
                pb = nextbank()
                i = hp % 2
                A('pe', lambda e: mm_fm(e, bank(pb)[:, 0:ncol], slot, c, hT, 8, ncol), reads=[wr, 'hT'], writes=['ps%d' % pb])
                A('act', lambda e: e.activation(out=sqb[i][:, 0:ncol], in_=bank(pb)[:, 0:ncol], func=AF.Square),
                  reads=['ps%d' % pb], writes=['scr%d' % (1 + i)])
                gcol = C('gk') if is_k else C('gq')
                A('dve', lambda e: e.tensor_scalar(out=qraw[i][:, 0:ncol], in0=bank(pb)[:, 0:ncol], scalar1=gcol, scalar2=None, op0=ALU.mult),
                  reads=['ps%d' % pb, 'cst'], writes=['scr%d' % (3 + i)])
                return lambda: qk_chunk_B(pb, i, hp, is_k)

            def qk_chunk_B(pb, i, hp, is_k):
                A('pe', lambda e: e.matmul(bank(2)[:, 0:ncol], lhsT=blk1, rhs=sqb[i][:, 0:ncol], start=True, stop=True),
                  reads=['scr%d' % (1 + i), 'mk'], writes=['ps2'])
                A('act', lambda e: e.activation(out=rs[i][:, 0:ncol], in_=bank(2)[:, 0:ncol], func=AF.Ln, scale=1.0 / 64, bias=EPS),
                  reads=['ps2'], writes=['scr%d' % (5 + i)])
                A('act', lambda e: e.activation(out=rs[i][:, 0:ncol], in_=rs[i][:, 0:ncol], func=AF.Exp, scale=-0.5),
                  reads=['scr%d' % (5 + i)], writes=['scr%d' % (5 + i)])
                if not is_k:
                    A('dve', lambda e: e.tensor_tensor(out=qT[:, hp, 0:ncol], in0=qraw[i][:, 0:ncol], in1=rs[i][:, 0:ncol], op=ALU.mult),
                      reads=['scr%d' % (3 + i), 'scr%d' % (5 + i)], writes=[r3q(hp)])
                else:
                    A('dve', lambda e: e.tensor_tensor(out=kTb[cur][:, hp, 0:ncol], in0=qraw[i][:, 0:ncol], in1=rs[i][:, 0:ncol], op=ALU.mult),
                      reads=['scr%d' % (3 + i), 'scr%d' % (5 + i)], writes=['kT%d_%d' % (cur, hp)])
                    if kv_out:
                        A('dve', lambda e: e.tensor_tensor(out=knf[:, 0:ncol], in0=qraw[i][:, 0:ncol], in1=rs[i][:, 0:ncol], op=ALU.mult),
                          reads=['scr%d' % (3 + i), 'scr%d' % (5 + i)], writes=['scr7'])
                        sg, sr = stage()

                        def trk(e):
                            ins = None
                            for b in range(nb):
                                ins = e.transpose(out=bank(4)[0:bs, b * 128:(b + 1) * 128], in_=knf[:, cols(b)], identity=identf[:, :])
                            return ins
                        A('pe', trk, reads=['scr7', 'identf'], writes=['ps4'])
                        A('act', lambda e: e.activation(out=sg[0:bs, 0:nb * 128], in_=bank(4)[0:bs, 0:nb * 128], func=AF.Copy),
                          reads=['ps4'], writes=[sr])
                        dst = k_dst[kv_row0:kv_row0 + ncol, hp * 128:(hp + 1) * 128].rearrange("(b p) d -> p b d", p=bs)
                        out_dma(dst, sg[0:bs, 0:nb * 128].rearrange("p (b d) -> p b d", b=nb), [sr], sr)

            pend = []
            for is_k_ in (False, True):
                for blk in range(2):
                    slot, wr = w_get(w_in, (1024 if is_k_ else 0) + 512 * blk)
                    for c in range(4):
                        pB = qk_chunk(slot, wr, c, blk * 4 + c, is_k_)
                        if pend:
                            pend.pop(0)()
                        pend.append(pB)
            while pend:
                pend.pop(0)()
            chk(0.5)
            for blk in range(2):
                slot, wr = w_get(w_in, 2048 + 512 * blk)
                for b in range(nb):
                    pb = nextbank()
                    A('pe', lambda e, b=b, pb=pb, slot=slot: mm_tm(e, bank(pb)[0:bs, :], slot, hT, b, bs, 8, 512), reads=[wr, 'hT'], writes=['ps%d' % pb])
                    A('act', lambda e, b=b, pb=pb, blk=blk: e.activation(out=vb[cur][0:bs, b, blk * 512:(blk + 1) * 512], in_=bank(pb)[0:bs, :], func=AF.Copy),
                      reads=['ps%d' % pb], writes=['v%d_%d' % (cur, b)])
                    if kv_out:
                        sg, sr = stage()
                        A('dve', lambda e, pb=pb, sg=sg: e.tensor_copy(out=sg[0:bs, :], in_=bank(pb)[0:bs, :]), reads=['ps%d' % pb], writes=[sr])
                        r0 = kv_row0 + b * 128
                        out_dma(v_dst[r0:r0 + bs, blk * 512:(blk + 1) * 512], sg[0:bs, :], [sr], sr)
            chk(0.6)
            for b_ in range(nb):
                A('act', lambda e, b_=b_: e.activation(out=xt[0:bs, b_, :], in_=xin[0:bs, b_, :], func=AF.Copy),
                  reads=['R2_%d' % (4 * b_ + j_) for j_ in range(4)], writes=['xt%d' % b_])
            for blk in range(4):
                slot, wr = w_get(w_in, 3072 + 512 * blk)
                for b in range(nb):
                    pb = nextbank()
                    A('pe', lambda e, b=b, pb=pb, slot=slot: mm_tm(e, bank(pb)[0:bs, :], slot, hT, b, bs, 8, 512), reads=[wr, 'hT'], writes=['ps%d' % pb])
                    A('act', lambda e, b=b, pb=pb, blk=blk: e.activation(out=zs[0:bs, b, blk * 512:(blk + 1) * 512], in_=bank(pb)[0:bs, :], func=AF.Silu),
                      reads=['ps%d' % pb], writes=['R2_%d' % (b * 4 + blk)])
            chk(0.7)
            pend = []
            for blk in range(6):
                slot, wr = w_get(w_in, 5120 + 512 * blk)
                for c in range(4):
                    def xbc_A(c=c, cc=blk * 4 + c, slot=slot, wr=wr):
                        i = cc % 2
                        pb = nextbank()
                        xr = 'scr%d' % (8 + i)
                        ar = 'scr%d' % (10 + i)
                        hr = 'xhalo%d' % cc
                        A('pe', lambda e: mm_fm(e, bank(pb)[:, 0:ncol], slot, c, hT, 8, ncol), reads=[wr, 'hT'], writes=['ps%d' % pb])
                        A('act', lambda e: e.activation(out=xraw[i][:, 3:3 + ncol], in_=bank(pb)[:, 0:ncol], func=AF.Copy),
                          reads=['ps%d' % pb], writes=[xr])
                        xh_ = 'xrh%d' % i
                        A('act', lambda e: e.activation(out=cacc[i][:, 0:ncol], in_=bank(pb)[:, 0:ncol], func=AF.Identity,
                                                        scale=C('cw', cc * 4 + 3), bias=C('cb', cc)),
                          reads=['ps%d' % pb, 'cst'], writes=[ar])
                        A('dve', lambda e: e.tensor_copy(out=xraw[i][:, 0:3], in_=xhalo[:, cc, :]), reads=[hr], writes=[xh_])
                        for tap in range(0, 3):
                            A('dve', lambda e, tap=tap: e.scalar_tensor_tensor(
                                out=cacc[i][:, 0:ncol], in0=xraw[i][:, tap:tap + ncol], scalar=C('cw', cc * 4 + tap),
                                in1=cacc[i][:, 0:ncol], op0=ALU.mult, op1=ALU.add), reads=[xr, xh_, ar, 'cst'], writes=[ar])
                        A('act', lambda e: e.activation(out=xhalo[:, cc, :], in_=xraw[i][:, ncol:ncol + 3], func=AF.Copy),
                          reads=[xr], writes=[hr])

                        def xbc_B():
                            A('act', lambda e: e.activation(out=R1[:, cc, 0:ncol], in_=cacc[i][:, 0:ncol], func=AF.Silu),
                              reads=[ar], writes=['R1_%d' % cc])
                        return xbc_B
                    pB = xbc_A()
                    if pend:
                        pend.pop(0)()
                    pend.append(pB)
            while pend:
                pend.pop(0)()
            chk(0.8)
            slot, wr = w_get(w_in, 8192)
            for b in range(nb):
                pb = nextbank()
                A('pe', lambda e, b=b, pb=pb, slot=slot: mm_tm(e, bank(pb)[0:bs, 0:32], slot, hT, b, bs, 8, 32), reads=[wr, 'hT'], writes=['ps%d' % pb])
                A('dve', lambda e, b=b, pb=pb: e.tensor_tensor(out=dtr[0:bs, b, :], in0=bank(pb)[0:bs, 0:32], in1=C('dtb', 0, 32)[0:bs, :], op=ALU.add),
                  reads=['ps%d' % pb, 'cst'], writes=['dtr'])

            if STOP <= 1:
                stopped[0] = True
                return
            LA = 2
            jobs = []
            for qb in range(nb):
                gb = (0 if sample else ti * 4) + qb
                if sample:
                    kl = [(1, t, t, 128) for t in range(4)] + [(0, 0, 4, 32)]
                else:
                    kl = []
                    for t in range(5):
                        kbi = gb - 4 + t
                        if kbi < 0:
                            continue
                        kl.append(((kbi // 4) % 2, kbi % 4, t, 128))
                for hp in range(8):
                    for half in range(2):
                        jobs.append((qb, hp, half, kl))
            nq = bs

            def att_sc(j):
                qb, hp, half, kl = jobs[j]
                si = j % 3
                bA, bB = 2 * si, 2 * si + 1
                p0, p1 = half * 64, half * 64 + 64

                def sc(e):
                    ins = None
                    for (sl, bk, t, nk) in kl:
                        o = bank(bA)[0:nk, t * 128: t * 128 + nq] if t < 4 else bank(bB)[0:nk, 0:nq]
                        ins = e.matmul(o, lhsT=kTb[sl][p0:p1, hp, bk * 128: bk * 128 + nk], rhs=qT[p0:p1, hp, qb * 128: qb * 128 + nq],
                                       start=True, stop=True)
                    return ins
                rds = [r3q(hp)] + ['kT%d_%d' % (sl, hp) for (sl, bk, t, nk) in kl]
                A('pe', sc, reads=rds, writes=['ps%d' % bA, 'ps%d' % bB])

            def att_rest(j):
                qb, hp, half, kl = jobs[j]
                si = j % 3
                h = hp * 2 + half
                bA, bB = 2 * si, 2 * si + 1
                p0, p1 = half * 64, half * 64 + 64
                pp = (j // 2) % 2
                nc0 = pp * 128
                far = [x for x in kl if x[2] < 4]
                if far:
                    tlo = far[0][2]
                    A('act', lambda e: e.activation(
                        out=pT[si][:, tlo * 128:512].rearrange("p (t q) -> p t q", q=128)[:, :, 0:nq],
                        in_=bank(bA)[:, tlo * 128:512].rearrange("p (t q) -> p t q", q=128)[:, :, 0:nq],
                        func=AF.Exp, scale=0.125), reads=['ps%d' % bA], writes=r3pt(si))
                nk4 = kl[-1][3]
                A('act', lambda e: e.activation(out=pT[si][0:nk4, 512:512 + nq], in_=bank(bB)[0:nk4, 0:nq],
                                                func=AF.Exp, scale=0.125), reads=['ps%d' % bB], writes=r3pt(si))
                has3 = any(x[2] == 3 for x in kl)
                if has3 and nk4 == 128:
                    A('dve', lambda e: e.tensor_tensor(
                        out=pT[si][:, 384:640].rearrange("p (t q) -> p t q", q=128)[:, :, 0:nq],
                        in0=pT[si][:, 384:640].rearrange("p (t q) -> p t q", q=128)[:, :, 0:nq],
                        in1=Et[:, h, :].rearrange("p (t q) -> p t q", q=128)[:, :, 0:nq], op=ALU.mult),
                      reads=r3pt(si) + ['Et'], writes=r3pt(si))
                else:
                    if has3:
                        A('dve', lambda e: e.tensor_tensor(out=pT[si][:, 384:384 + nq], in0=pT[si][:, 384:384 + nq],
                                                           in1=Et[:, h, 0:nq], op=ALU.mult),
                          reads=r3pt(si) + ['Et'], writes=r3pt(si))
                    A('dve', lambda e: e.tensor_tensor(out=pT[si][0:nk4, 512:512 + nq], in0=pT[si][0:nk4, 512:512 + nq],
                                                       in1=Et[0:nk4, h, 128:128 + nq], op=ALU.mult),
                      reads=r3pt(si) + ['Et'], writes=r3pt(si))
                if any(x[2] == 0 for x in kl) and nq > 64:
                    A('dve', lambda e: e.memset(pT[si][0:64, 64:128], 0.0), reads=r3pt(si), writes=r3pt(si))

                def pv(e):
                    ins = None
                    n = len(kl)
                    for jj, (sl, bk, t, nk) in enumerate(kl):
                        e.matmul(bank(6 + pp)[p0:p1, 0:nq], lhsT=vb[sl][0:nk, bk, h * 64:(h + 1) * 64], rhs=pT[si][0:nk, t * 128: t * 128 + nq],
                                 start=(jj == 0), stop=(jj == n - 1), skip_group_check=True)
                        ins = e.matmul(bank(6 + pp)[p0:p1, 128:128 + nq], lhsT=onesb[0:nk, 0:64], rhs=pT[si][0:nk, t * 128: t * 128 + nq],
                                       start=False, stop=(jj == n - 1), skip_group_check=True)
                    return ins
                A('pe', pv, reads=r3pt(si) + ['mk'] + ['v%d_%d' % (sl, bk) for (sl, bk, t, nk) in kl],
                  writes=['ps%d' % (6 + pp)])
                if half == 1:
                    fins.append(lambda: att_fin(qb, hp, pp, nc0))

            def att_fin(qb, hp, pp, nc0):
                if True:
                    ri = pp
                    A('act', lambda e: e.activation(out=rden[ri][:, 0:nq], in_=bank(6 + pp)[:, 128:128 + nq], func=AF.Ln),
                      reads=['ps%d' % (6 + pp)], writes=r3rd(ri))
                    A('act', lambda e: e.activation(out=rden[ri][:, 0:nq], in_=rden[ri][:, 0:nq], func=AF.Exp, scale=-1.0),
                      reads=r3rd(ri), writes=r3rd(ri))
                    A('dve', lambda e: e.tensor_tensor(out=yaT[:, hp, qb * 128: qb * 128 + nq], in0=bank(6 + pp)[:, 0:nq],
                                                       in1=rden[ri][:, 0:nq], op=ALU.mult),
                      reads=['ps%d' % (6 + pp)] + r3rd(ri), writes=['yaT'])

            nj = len(jobs)
            fins = []
            for j in range(min(LA, nj)):
                att_sc(j)
            for j in range(nj):
                if j + LA < nj:
                    att_sc(j + LA)
                npend = len(fins)
                att_rest(j)
                for _ in range(npend):
                    fins.pop(0)()
            while fins:
                fins.pop(0)()

            if STOP <= 2:
                stopped[0] = True
                return
            tl = tcount[0]
            tcount[0] += 1
            if DBG:
                out_dma(d_ya[tl], yaT[:].rearrange("p a b -> p (a b)"), ['yaT'], 'dbg')
            phase_barrier()
            rhs_hi = [scr_b(0, 1024), rhs_x[:, 0:1024]]
            rhs_lo = [scr_b(2048, 1024), rhs_x[:, 1024:2048]]
            dec = [scr_b(4096 + i * 2048, 1024) for i in range(2)]
            dxb = [scr_b(8192 + i * 1024, 512) for i in range(2)]
            xsd = [scr_b(10240 + i * 1024, 512) for i in range(2)]
            xwb = [scr_b(12288 + i * 1024, 512) for i in range(2)]
            bm_tm = scr_b(14336, 512)
            cbm = scr_b(15360, 512)
            yg = scr_b(16384, 2048)
            tmpS = scr_f(20480, 512)
            yv = scr_f(22528, 512)
            sm = scr_f(24576, 512)
            exo_all = sm[:, 0:384].rearrange("p (b c) -> p b c", c=96)
            a_hi_all = sm[:, 384:448].bitcast(BF16)
            a_lo_all = sm[:, 448:512].bitcast(BF16)
            ssy = ssb[:, 8:16]
            nt = bs
            ex1v = yv[0:nt, 0:nb * 32].rearrange("p (b c) -> p b c", c=32)
            a_fv = tmpS[0:nt, 0:nb * 32].rearrange("p (b c) -> p b c", c=32)
            A('act', lambda e: e.activation(out=ex1v, in_=dtr[0:nt, 0:nb, :], func=AF.Exp), reads=['dtr'], writes=['yv'])
            A('act', lambda e: e.activation(out=dtr[0:nt, 0:nb, :], in_=ex1v, func=AF.Ln, bias=1.0), reads=['yv'], writes=['sm1', 'dtr'])
            A('dve', lambda e: e.tensor_tensor(out=a_fv, in0=dtr[0:nt, 0:nb, :], in1=aneg[0:nt, :].unsqueeze(1).to_broadcast([nt, nb, 32]), op=ALU.mult),
              reads=['sm1', 'aneg'], writes=['tmpS'])
            A('dve', lambda e: e.tensor_copy(out=a_hi_all[0:nt, 0:nb * 32], in_=tmpS[0:nt, 0:nb * 32]), reads=['tmpS'], writes=['sm3'])
            A('dve', lambda e: e.tensor_tensor(out=a_lo_all[0:nt, 0:nb * 32], in0=tmpS[0:nt, 0:nb * 32], in1=a_hi_all[0:nt, 0:nb * 32], op=ALU.subtract),
              reads=['tmpS', 'sm3'], writes=['sm4'])

            def cums(e):
                ins = None
                for b in range(nb):
                    for (m, c0, M) in ((Vm, 0, nt), (Um, 32, nt), (onesb, 64, 128)):
                        o = bank(0)[0:M, b * 96 + c0: b * 96 + c0 + 32]
                        e.matmul(o, lhsT=m[0:nt, 0:M], rhs=a_hi_all[0:nt, b * 32:(b + 1) * 32], start=True, stop=False, skip_group_check=True)
                        ins = e.matmul(o, lhsT=m[0:nt, 0:M], rhs=a_lo_all[0:nt, b * 32:(b + 1) * 32], start=False, stop=True, skip_group_check=True)
                return ins
            A('pe', cums, reads=['sm3', 'sm4', 'mk'], writes=['ps0'])
            A('act', lambda e: e.activation(out=sm[0:nt, 0:nb * 96], in_=bank(0)[0:nt, 0:nb * 96], func=AF.Exp), reads=['ps0'], writes=['sm5', 'sm6'])
            if nt < 128:
                A('act', lambda e: e.activation(out=exo_all[:, 0:nb, 64:96], in_=bank(0)[:, 0:nb * 96].rearrange("p (b c) -> p b c", c=96)[:, :, 64:96],
                                                func=AF.Exp), reads=['ps0'], writes=['sm6'])
            A('dve', lambda e: e.tensor_tensor(out=exo_all[0:nt, 0:nb, 32:64], in0=exo_all[0:nt, 0:nb, 32:64], in1=dtr[0:nt, 0:nb, :], op=ALU.mult),
              reads=['sm5', 'sm1'], writes=['sm7', 'sm5'])
            blkf = []
            for b in range(nb):
                cb_ = cols(b)
                dt_ = dtr[:, b, :]
                exo = exo_all[:, b, :]
                w2 = exo_all[:, b, 32:64]
                a_hi = a_hi_all[:, b * 32:(b + 1) * 32]
                a_lo = a_lo_all[:, b * 32:(b + 1) * 32]
                def ssd_pre(b=b, cb_=cb_):

                    def cbf(e, cb_=cb_):
                        ins = None
                        for g in range(4):
                            ins = e.matmul(bank(3)[0:nt, g * 128: g * 128 + nt], lhsT=R1[:, 16 + g, cb_], rhs=R1[:, 20 + g, cb_], start=True, stop=True)
                        return ins
                    A('pe', cbf, reads=['R1_%d' % c for c in range(16, 24)], writes=['ps3'])
                    A('dve', lambda e: e.tensor_tensor(out=cbm[0:nt, :].rearrange("p (g i) -> p g i", g=4)[:, :, 0:nt],
                                                       in0=bank(3)[0:nt, :].rearrange("p (g i) -> p g i", g=4)[:, :, 0:nt],
                                                       in1=Vm[0:nt, 0:nt].unsqueeze(1).to_broadcast([nt, 4, nt]), op=ALU.mult),
                      reads=['ps3', 'mk'], writes=['cbm'])

                    def trb(e, cb_=cb_):
                        ins = None
                        for g in range(4):
                            ins = e.transpose(out=bankb(4)[0:nt, g * 128:(g + 1) * 128], in_=R1[:, 16 + g, cb_], identity=identb)
                        return ins
                    A('pe', trb, reads=['R1_%d' % c for c in range(16, 20)] + ['mk'], writes=['ps4'])
                    A('act', lambda e: e.activation(out=bm_tm[0:nt, :], in_=bankb(4)[0:nt, 0:512], func=AF.Copy), reads=['ps4'], writes=['bm_tm'])
                def ssd_stage1(g, b=b, cb_=cb_, dt_=dt_, exo=exo, w2=w2, a_hi=a_hi, a_lo=a_lo):
                    i = g % 2
                    hs = slice(8 * g, 8 * g + 8)
                    W8 = 8 * nt
                    sb0, sb1 = ((1, 2), (0, 3))[i]
                    A(RHS_ENG, lambda e: e.tensor_tensor(out=rhs_hi[i][0:nt, 0:8 * nt].rearrange("p (h i) -> p h i", h=8),
                                                         in0=Vm[0:nt, 0:nt].unsqueeze(1).to_broadcast([nt, 8, nt]),
                                                         in1=a_hi[0:nt, hs].unsqueeze(2).to_broadcast([nt, 8, nt]), op=ALU.mult),
                      reads=['sm3', 'mk'], writes=['rhs_hi%d' % i])
                    A(RHS_ENG, lambda e: e.tensor_tensor(out=rhs_lo[i][0:nt, 0:8 * nt].rearrange("p (h i) -> p h i", h=8),
                                                         in0=Vm[0:nt, 0:nt].unsqueeze(1).to_broadcast([nt, 8, nt]),
                                                         in1=a_lo[0:nt, hs].unsqueeze(2).to_broadcast([nt, 8, nt]), op=ALU.mult),
                      reads=['sm4', 'mk'], writes=['rhs_lo%d' % i])

                    def segf(e):
                        ins = None
                        for k_, c0 in enumerate(range(0, W8, 512)):
                            c1 = min(W8, c0 + 512)
                            o = bank((sb0, sb1)[k_])[0:nt, 0:c1 - c0]
                            e.matmul(o, lhsT=Um[0:nt, 0:nt], rhs=rhs_hi[i][0:nt, c0:c1], start=True, stop=False)
                            ins = e.matmul(o, lhsT=Um[0:nt, 0:nt], rhs=rhs_lo[i][0:nt, c0:c1], start=False, stop=True)
                        return ins
                    A('pe', segf, reads=['rhs_hi%d' % i, 'rhs_lo%d' % i, 'mk'], writes=['ps%d' % sb0, 'ps%d' % sb1])
                    for k_, c0 in enumerate(range(0, W8, 512)):
                        c1 = min(W8, c0 + 512)
                        bk_ = (sb0, sb1)[k_]
                        A('act', lambda e, c0=c0, c1=c1, bk_=bk_: e.activation(out=dec[i][0:nt, c0:c1], in_=bank(bk_)[0:nt, 0:c1 - c0], func=AF.Exp),
                          reads=['ps%d' % bk_], writes=['dec%d' % i])

                    def trx(e):
                        ins = None
                        for j in range(4):
                            ins = e.transpose(out=bankb(4)[0:nt, i * 512 + j * 128: i * 512 + (j + 1) * 128], in_=R1[:, 4 * g + j, cb_], identity=identb)
                        return ins
                    A('pe', trx, reads=['R1_%d' % (4 * g + j) for j in range(4)] + ['mk'], writes=['ps4'])
                    xsv = bankb(4)[0:nt, i * 512:(i + 1) * 512].rearrange("p (h d) -> p h d", h=8)
                    for (dst, nm, sc_ap, rr) in ((dxb[i], 'dx%d' % i, dt_[0:nt, hs], 'sm1'), (xsd[i], 'xsd%d' % i, C('dskip', 0, 32)[0:nt, hs], 'cst'),
                                                 (xwb[i], 'xw%d' % i, w2[0:nt, hs], 'sm7')):
                        A('dve', lambda e, dst=dst, sc_ap=sc_ap: e.tensor_tensor(
                            out=dst[0:nt, :].rearrange("p (h d) -> p h d", h=8), in0=xsv,
                            in1=sc_ap.unsqueeze(2).to_broadcast([nt, 8, 64]), op=ALU.mult), reads=['ps4', rr], writes=[nm])
                    A('dve', lambda e: e.tensor_tensor(
                        out=dec[i][0:nt, 0:W8].rearrange("p (h i) -> p h i", h=8), in0=dec[i][0:nt, 0:W8].rearrange("p (h i) -> p h i", h=8),
                        in1=cbm[0:nt, g * 128: g * 128 + nt].unsqueeze(1).to_broadcast([nt, 8, nt]), op=ALU.mult),
                      reads=['dec%d' % i, 'cbm'], writes=['dec%d' % i])

                def ssd_stage2(g, b=b, cb_=cb_, dt_=dt_, exo=exo, w2=w2, a_hi=a_hi, a_lo=a_lo):
                    i = g % 2
                    hs = slice(8 * g, 8 * g + 8)

                    def yf(e):
                        e.matmul(bank(5)[0:nt, :], lhsT=identb[0:nt, 0:nt], rhs=xsd[i][0:nt, :], start=True, stop=False, skip_group_check=True)
                        ins = None
                        for hh in range(8):
                            ins = e.matmul(bank(5)[0:nt, hh * 64:(hh + 1) * 64], lhsT=dec[i][0:nt, hh * nt:(hh + 1) * nt],
                                           rhs=dxb[i][0:nt, hh * 64:(hh + 1) * 64], start=False, stop=(hh == 7), skip_group_check=True)
                        return ins
                    A('pe', yf, reads=['dec%d' % i, 'dx%d' % i, 'xsd%d' % i, 'mk'], writes=['ps5'])
                    A('pe', lambda e: e.matmul(bank(6)[0:nt, :], lhsT=R1[:, 20 + g, cb_], rhs=hbf[:, g * 512:(g + 1) * 512], start=True, stop=True),
                      reads=['R1_%d' % (20 + g), 'hbf%d' % g], writes=['ps6'])
                    A('pe', lambda e: e.matmul(bank(7)[:, :], lhsT=bm_tm[0:nt, g * 128:(g + 1) * 128], rhs=xwb[i][0:nt, :], start=True, stop=True),
                      reads=['bm_tm', 'xw%d' % i], writes=['ps7'])
                    A('dve', lambda e: e.tensor_tensor(out=tmpS[0:nt, :].rearrange("p (h d) -> p h d", h=8),
                                                       in0=bank(6)[0:nt, :].rearrange("p (h d) -> p h d", h=8),
                                                       in1=exo[0:nt, hs].unsqueeze(2).to_broadcast([nt, 8, 64]), op=ALU.mult),
                      reads=['ps6', 'sm5'], writes=['tmpS'])
                    A('dve', lambda e: e.tensor_tensor(out=yv[0:nt, :], in0=bank(5)[0:nt, :], in1=tmpS[0:nt, :], op=ALU.add),
                      reads=['ps5', 'tmpS'], writes=['yv'])
                    A('dve', lambda e: e.tensor_tensor(out=yv[0:nt, :], in0=yv[0:nt, :], in1=zs[0:nt, b, g * 512:(g + 1) * 512], op=ALU.mult),
                      reads=['yv', 'R2_%d' % (b * 4 + g)], writes=['yv'])
                    A('act', lambda e: e.activation(out=yg[0:nt, g * 512:(g + 1) * 512], in_=yv[0:nt, :], func=AF.Copy),
                      reads=['yv'], writes=['yg'])
                    A('act', lambda e: e.activation(out=tmpS[0:nt, :], in_=yv[0:nt, :], func=AF.Square, accum_out=ssy[0:nt, g:g + 1]),
                      reads=['yv', 'tmpS'], writes=['tmpS', 'ssy'])
                    hv = hst[:, g * 512:(g + 1) * 512]
                    A('dve', lambda e: e.tensor_tensor(out=hv.rearrange("p (h d) -> p h d", h=8), in0=hv.rearrange("p (h d) -> p h d", h=8),
                                                         in1=exo[:, 64:96][:, hs].unsqueeze(2).to_broadcast([128, 8, 64]), op=ALU.mult),
                      reads=['hst%d' % g, 'sm6'], writes=['hst%d' % g])
                    A('dve', lambda e: e.tensor_tensor(out=hv, in0=hv, in1=bank(7)[:, :], op=ALU.add), reads=['hst%d' % g, 'ps7'], writes=['hst%d' % g])
                    A('act', lambda e: e.activation(out=hbf[:, g * 512:(g + 1) * 512], in_=hv, func=AF.Copy),
                      reads=['hst%d' % g], writes=['hbf%d' % g])

                def ssd_post(b=b, cb_=cb_):
                    A('dve', lambda e: e.reduce_sum(out=ssy[0:nt, 4:5], in_=ssy[0:nt, 0:4], axis=AX.X), reads=['ssy'], writes=['ssy'])
                    A('act', lambda e: e.activation(out=ssy[0:nt, 5:6], in_=ssy[0:nt, 4:5], func=AF.Ln, scale=1.0 / 2048, bias=EPS), reads=['ssy'], writes=['ssy'])
                    A('act', lambda e: e.activation(out=ssy[0:nt, 6:7], in_=ssy[0:nt, 5:6], func=AF.Exp, scale=-0.5), reads=['ssy'], writes=['ssy'])
                    A('dve', lambda e: e.tensor_scalar(out=dg[0:nt, 0:nt], in0=identb[0:nt, 0:nt], scalar1=ssy[0:nt, 6:7], scalar2=None, op0=ALU.mult),
                      reads=['ssy', 'mk'], writes=['dg'])
                    for j in range(4):
                        bk_ = 5 + (j % 3)

                        def tryb(e, j=j, bk_=bk_):
                            ins = None
                            for c in range(4):
                                cc = 4 * j + c
                                ins = e.matmul(bank(bk_)[:, c * 128: c * 128 + nt], lhsT=yg[0:nt, cc * 128:(cc + 1) * 128], rhs=dg[0:nt, 0:nt],
                                               start=True, stop=True)
                            return ins
                        A('pe', tryb, reads=['yg', 'dg'], writes=['ps%d' % bk_])
                        A('dve', lambda e, j=j, bk_=bk_: e.tensor_tensor(out=ybT[:, 4 * j:4 * j + 4, cb_],
                                                                        in0=bank(bk_).rearrange("p (k t) -> p k t", k=4)[:, :, 0:nt],
                                                                        in1=C('gssm', 4 * j, 4).unsqueeze(2).to_broadcast([128, 4, nt]), op=ALU.mult),
                          reads=['ps%d' % bk_, 'cst'], writes=['R3_%d' % c for c in range(4 * j, 4 * j + 4)])

                blkf.append((ssd_pre, ssd_stage1, ssd_stage2, ssd_post))

            blkf[0][0]()
            blkf[0][1](0)
            blkf[0][1](1)
            for b in range(nb):
                pre_, s1_, s2_, post_ = blkf[b]
                s2_(0)
                s1_(2)
                s2_(1)
                s1_(3)
                s2_(2)
                s2_(3)
                if b + 1 < nb:
                    blkf[b + 1][0]()
                    blkf[b + 1][1](0)
                    blkf[b + 1][1](1)
                post_()

            if STOP <= 3:
                stopped[0] = True
                return
            if DBG:
                out_dma(d_yb[tl], R3[:, :], ['R3_%d' % k for k in range(16)], 'dbg')
            phase_barrier()
            tmpB = [scr_f(i * 2048, 512) for i in range(2)]
            for blk in range(4):
                slot, wr = w_get(w_in, 8224 + 512 * blk)
                for c in range(4):
                    cc = blk * 4 + c
                    pb = nextbank()
                    A('pe', lambda e, c=c, pb=pb, slot=slot: mm_fm(e, bank(pb)[:, 0:ncol], slot, c, hT, 8, ncol), reads=[wr, 'hT'], writes=['ps%d' % pb])
                    A('act', lambda e, cc=cc, pb=pb: e.activation(out=R1[:, cc, 0:ncol], in_=bank(pb)[:, 0:ncol], func=AF.Sigmoid, bias=C('bgate', cc)),
                      reads=['ps%d' % pb, 'cst'], writes=['R1_%d' % cc])
            for blk in range(2):
                slot, wr = w_get(w_pa, 512 * blk)
                for c in range(4):
                    cc = blk * 4 + c
                    pb = nextbank()
                    A('pe', lambda e, c=c, pb=pb, slot=slot: mm_fm(e, bank(pb)[:, 0:ncol], slot, c, yaT, 8, ncol), reads=[wr, 'yaT'], writes=['ps%d' % pb])
                    A('dve', lambda e, cc=cc, pb=pb: e.tensor_tensor(out=t1[:, cc, 0:ncol], in0=bank(pb)[:, 0:ncol], in1=R1[:, cc, 0:ncol], op=ALU.mult),
                      reads=['ps%d' % pb, 'R1_%d' % cc], writes=['R2_%d' % (2 * cc), 'R2_%d' % (2 * cc + 1)])
            for blk in range(2):
                slot0, wr0 = w_get(w_pb, 512 * blk)
                slot1, wr1 = w_get(w_pb, 512 * blk, prefetch=False)
                for c in range(4):
                    cc = blk * 4 + c
                    pb = nextbank()
                    i = cc % 2

                    def pbf(e, c=c, pb=pb, slot0=slot0, slot1=slot1):
                        mm_fm(e, bank(pb)[:, 0:ncol], slot0, c, ybT, 8, ncol, koff=0, start=True, stop=False)
                        return mm_fm(e, bank(pb)[:, 0:ncol], slot1, c, ybT, 8, ncol, koff=8, start=False, stop=True)
                    A('pe', pbf, reads=[wr0, wr1] + ['R3_%d' % k for k in range(16)], writes=['ps%d' % pb])
                    A('dve', lambda e, cc=cc, pb=pb, i=i: e.tensor_tensor(out=tmpB[i][:, 0:ncol], in0=bank(pb)[:, 0:ncol], in1=R1[:, 8 + cc, 0:ncol], op=ALU.mult),
                      reads=['ps%d' % pb, 'R1_%d' % (8 + cc)], writes=['scr%d' % i])
                    A('dve', lambda e, cc=cc, i=i: e.tensor_tensor(out=mT[:, cc, 0:ncol], in0=t1[:, cc, 0:ncol], in1=tmpB[i][:, 0:ncol], op=ALU.add),
                      reads=['scr%d' % i, 'R2_%d' % (2 * cc), 'R2_%d' % (2 * cc + 1)], writes=['yaT'])
            if DBG:
                out_dma(d_m[tl], yaT[:].rearrange("p a b -> p (a b)"), ['yaT'], 'dbg')
                out_dma(d_g[tl], R1[:, 0:16, :].rearrange("p a b -> p (a b)"), ['R1_%d' % k for k in range(16)], 'dbg')
            for half in range(2):
                slot, wr = w_get(w_out, 512 * half)
                for b in range(nb):
                    pb = nextbank()
                    A('pe', lambda e, b=b, pb=pb, slot=slot: mm_tm(e, bank(pb)[0:bs, :], slot, mT, b, bs, 8, 512), reads=[wr, 'yaT'], writes=['ps%d' % pb])
                    A('dve', lambda e, b=b, pb=pb, half=half: e.tensor_tensor(out=xt[0:bs, b, half * 512:(half + 1) * 512], in0=xt[0:bs, b, half * 512:(half + 1) * 512],
                                                                   in1=bank(pb)[0:bs, :], op=ALU.add),
                      reads=['ps%d' % pb, 'xt%d' % b], writes=['xt%d' % b])
                    if half == 1:
                        if b >= 2:
                            rms_B(b - 2, bs, 'gffn')
                        rms_A(b, bs, scr_b(0, 1024))
                if half == 1:
                    for b in range(max(0, nb - 2), nb):
                        rms_B(b, bs, 'gffn')

            if STOP <= 4:
                stopped[0] = True
                return
            if DBG:
                out_dma(d_x[tl], xt[:].rearrange("p a b -> p (a b)"), xres, 'dbg')
            if tidx + 1 < len(tiles):
                emit_xprefetch(tidx + 1)
            phase_barrier()
            junk = scr_b(0, 1024)
            ugr = [scr_f(2048 + i * 2064, 516) for i in range(2)]
            uvr = [scr_f(6176 + i * 2064, 516) for i in range(2)]
            cga = [scr_f(10304 + k * 2048, 512) for k in range(2)]
            cva = [scr_f(14400 + k * 2048, 512) for k in range(2)]
            gl = [scr_f(18496 + k * 2048, 512) for k in range(2)]
            pend = []
            for ub in range(6):
                slotg, wrg = w_get(w_up, 512 * ub)
                slotv, wrv = w_get(w_up, 3072 + 512 * ub, prefetch=False)
                for c in range(4):
                    def up_A(c=c, cg_=ub * 4 + c, slotg=slotg, wrg=wrg, slotv=slotv, wrv=wrv):
                        i = cg_ % 2
                        for (slot, wr, raw, rn, chn, acc, an, pb) in ((slotg, wrg, ugr[i], 'scr%d' % (1 + i), cg_, cga[i], 'scr%d' % (5 + i), 2 * i),
                                                                       (slotv, wrv, uvr[i], 'scr%d' % (3 + i), 24 + cg_, cva[i], 'scr%d' % (7 + i), 2 * i + 1)):
                            hr = 'uhalo%d' % chn
                            A('pe', lambda e, pb=pb, slot=slot: mm_fm(e, bank(pb)[:, 0:ncol], slot, c, hT, 8, ncol), reads=[wr, 'hT'], writes=['ps%d' % pb])
                            A('act', lambda e, raw=raw, pb=pb: e.activation(out=raw[:, 2:2 + ncol], in_=bank(pb)[:, 0:ncol], func=AF.Copy),
                              reads=['ps%d' % pb], writes=[rn])
                            rh_ = rn + 'h'
                            A('act', lambda e, chn=chn, acc=acc, pb=pb: e.activation(out=acc[:, 0:ncol], in_=bank(pb)[:, 0:ncol], func=AF.Identity,
                                                                                    scale=C('fw', chn * 3 + 2), bias=C('fb', chn)),
                              reads=['ps%d' % pb, 'cst'], writes=[an])
                            A('dve', lambda e, raw=raw, chn=chn: e.tensor_copy(out=raw[:, 0:2], in_=uhalo[:, chn, :]), reads=[hr], writes=[rh_])
                            for tap in range(0, 2):
                                A('dve', lambda e, raw=raw, chn=chn, acc=acc, tap=tap: e.scalar_tensor_tensor(
                                    out=acc[:, 0:ncol], in0=raw[:, tap:tap + ncol], scalar=C('fw', chn * 3 + tap), in1=acc[:, 0:ncol],
                                    op0=ALU.mult, op1=ALU.add), reads=[rn, rh_, an, 'cst'], writes=[an])
                            A('act', lambda e, raw=raw, chn=chn: e.activation(out=uhalo[:, chn, :], in_=raw[:, ncol:ncol + 2], func=AF.Copy),
                              reads=[rn], writes=[hr])

                        def up_B():
                            A('act', lambda e: e.activation(out=gl[i][:, 0:ncol], in_=cga[i][:, 0:ncol], func=AF.Gelu_apprx_tanh),
                              reads=['scr%d' % (5 + i)], writes=['scr%d' % (9 + i)])
                            A('dve', lambda e: e.tensor_tensor(out=R1[:, cg_, 0:ncol], in0=gl[i][:, 0:ncol], in1=cva[i][:, 0:ncol], op=ALU.mult),
                              reads=['scr%d' % (9 + i), 'scr%d' % (7 + i)], writes=['R1_%d' % cg_])
                        return up_B
                    pB = up_A()
                    if pend:
                        pend.pop(0)()
                    pend.append(pB)
            while pend:
                pend.pop(0)()
            for half in range(2):
                for kg in range(3):
                    slot, wr = w_get(w_down, 512 * half)
                    for b in range(nb):
                        A('pe', lambda e, b=b, slot=slot, kg=kg: mm_tm(e, bank(4 + b)[0:bs, :], slot, R1, b, bs, 8, 512, koff=8 * kg,
                                                                        start=(kg == 0), stop=(kg == 2)),
                          reads=[wr] + ['R1_%d' % k for k in range(8 * kg, 8 * kg + 8)], writes=['ps%d' % (4 + b)])
                for b in range(nb):
                    A('dve', lambda e, b=b, half=half: e.tensor_tensor(out=xt[0:bs, b, half * 512:(half + 1) * 512], in0=xt[0:bs, b, half * 512:(half + 1) * 512],
                                                                       in1=bank(4 + b)[0:bs, :], op=ALU.add),
                      reads=['ps%d' % (4 + b), 'xt%d' % b], writes=['xt%d' % b])
                    if half == 1:
                        if b >= 2:
                            rms_B(b - 2, bs, 'gple')
                        rms_A(b, bs, scr_b(0, 1024))
                if half == 1:
                    for b in range(max(0, nb - 2), nb):
                        rms_B(b, bs, 'gple')

            if STOP <= 5:
                stopped[0] = True
                return
            if DBG:
                out_dma(d_x2[tl], xt[:].rearrange("p a b -> p (a b)"), xres, 'dbg')
            phase_barrier()
            junk = scr_b(0, 1024)
            sgt = [scr_f(2048 + i * 2048, 512) for i in range(2)]
            for half in range(2):
                slotg, wrg = w_get(w_pg, 512 * half)
                slotp, wrp = w_get(w_ple, 512 * half, prefetch=False)
                for b in range(nb):
                    i = b % 2
                    bg, bp_ = 2 * i, 2 * i + 1
                    A('pe', lambda e, b=b, slotg=slotg, bg=bg: mm_tm(e, bank(bg)[0:bs, :], slotg, hT, b, bs, 8, 512), reads=[wrg, 'hT'], writes=['ps%d' % bg])
                    A('pe', lambda e, b=b, slotp=slotp, bp_=bp_: mm_tm(e, bank(bp_)[0:bs, :], slotp, pet, b, bs, 2, 512), reads=[wrp, 'pet'], writes=['ps%d' % bp_])
                    A('act', lambda e, i=i, bg=bg: e.activation(out=sgt[i][0:bs, :], in_=bank(bg)[0:bs, :], func=AF.Sigmoid), reads=['ps%d' % bg], writes=['scr%d' % (1 + i)])
                    A('dve', lambda e, i=i, bp_=bp_: e.tensor_tensor(out=sgt[i][0:bs, :], in0=sgt[i][0:bs, :], in1=bank(bp_)[0:bs, :], op=ALU.mult),
                      reads=['ps%d' % bp_, 'scr%d' % (1 + i)], writes=['scr%d' % (1 + i)])
                    A('dve', lambda e, b=b, i=i, half=half: e.tensor_tensor(out=xt[0:bs, b, half * 512:(half + 1) * 512],
                                                                            in0=xt[0:bs, b, half * 512:(half + 1) * 512], in1=sgt[i][0:bs, :], op=ALU.add),
                      reads=['scr%d' % (1 + i), 'xt%d' % b], writes=['xt%d' % b])
                    if half == 1 and not sample:
                        out_dma(ydst[b * 128:(b + 1) * 128, :], xt[:, b, :], ['xt%d' % b], 'yout%d' % b)
            if sample:
                out_dma(ydst, xt[0:32, 0, :], xres, 'yout0')

            if last:
                sd = ssms[0] if sample else ssmp[s]
                for q in range(4):
                    def trs(e, q=q):
                        ins = None
                        for j in range(4):
                            c = 4 * q + j
                            ins = e.transpose(out=bank(0)[:, j * 128:(j + 1) * 128], in_=hst[:, c * 128:(c + 1) * 128], identity=identf[:, :])
                        return ins
                    A('pe', trs, reads=['hst%d' % q, 'identf'], writes=['ps0'])
                    sg, sr = stage()
                    A('act', lambda e, sg=sg: e.activation(out=sg[:, :], in_=bank(0)[:, :], func=AF.Copy), reads=['ps0'], writes=[sr])
                    out_dma(sd[q * 512:(q + 1) * 512, :].rearrange("(j p) n -> p j n", p=128), sg[:, :].rearrange("p (j n) -> p j n", j=4), [sr], sr)
                cd = cssms[0] if sample else cssmp[s]
                for q in range(6):
                    def trc(e, q=q):
                        ins = None
                        for j in range(4):
                            ins = e.transpose(out=bank(1)[0:3, j * 128:(j + 1) * 128], in_=xhalo[:, 4 * q + j, :], identity=identf[:, :])
                        return ins
                    A('pe', trc, reads=['xhalo%d' % (4 * q + j) for j in range(4)] + ['identf'], writes=['ps1'])
                    sg, sr = stage()
                    A('act', lambda e, sg=sg: e.activation(out=sg[0:3, :], in_=bank(1)[0:3, :], func=AF.Copy), reads=['ps1'], writes=[sr])
                    out_dma(cd[:, q * 512:(q + 1) * 512], sg[0:3, :], [sr], sr)
                fd = cffns[0] if sample else cffnp[s]
                for q in range(12):
                    def trf(e, q=q):
                        ins = None
                        for j in range(4):
                            ins = e.transpose(out=bank(1)[0:2, j * 128:(j + 1) * 128], in_=uhalo[:, 4 * q + j, :], identity=identf[:, :])
                        return ins
                    A('pe', trf, reads=['uhalo%d' % (4 * q + j) for j in range(4)] + ['identf'], writes=['ps1'])
                    sg, sr = stage()
                    A('act', lambda e, sg=sg: e.activation(out=sg[0:2, :], in_=bank(1)[0:2, :], func=AF.Copy), reads=['ps1'], writes=[sr])
                    out_dma(fd[:, q * 512:(q + 1) * 512], sg[0:2, :], [sr], sr)

        try:
            chk(0.05)
            for s in range(NSEQ):
                for ti in range(NT):
                    do_tile('p', s, ti)
            if SAMPLE:
                do_tile('s', 0, 0)
        except StopBuild:
            pass
        A('sp', None, reads=list(out_res))
        S.emit(nc, st)
    return nc


def _consts(g_mix, g_ffn, g_ple, g_q, g_k, b_gate, conv_ssm_w, conv_ssm_b, ffn_conv_w, ffn_conv_b, dt_bias, a_log, d_skip, g_ssm, rel_bias):
    cst = np.zeros((128, NCST), np.float32)

    def fm(v, nch):
        return np.ascontiguousarray(np.asarray(v, np.float32).reshape(nch, 128).T)

    cst[:, _off['gmix']:_off['gmix'] + 8] = fm(g_mix, 8)
    cst[:, _off['gffn']:_off['gffn'] + 8] = fm(g_ffn, 8)
    cst[:, _off['gple']:_off['gple'] + 8] = fm(g_ple, 8)
    cst[:, _off['gq']] = np.tile(np.asarray(g_q, np.float32), 2)
    cst[:, _off['gk']] = np.tile(np.asarray(g_k, np.float32), 2)
    cst[:, _off['bgate']:_off['bgate'] + 16] = fm(b_gate, 16)
    cw = np.asarray(conv_ssm_w, np.float32)
    cst[:, _off['cw']:_off['cw'] + 96] = cw.T.reshape(24, 128, 4).transpose(1, 0, 2).reshape(128, 96)
    cst[:, _off['cb']:_off['cb'] + 24] = fm(conv_ssm_b, 24)
    fw = np.asarray(ffn_conv_w, np.float32)
    cst[:, _off['fw']:_off['fw'] + 144] = fw.T.reshape(48, 128, 3).transpose(1, 0, 2).reshape(128, 144)
    cst[:, _off['fb']:_off['fb'] + 48] = fm(ffn_conv_b, 48)
    cst[:, _off['dtb']:_off['dtb'] + 32] = np.asarray(dt_bias, np.float32)[None, :]
    cst[:, _off['alog']:_off['alog'] + 32] = np.asarray(a_log, np.float32)[None, :]
    cst[:, _off['dskip']:_off['dskip'] + 32] = np.asarray(d_skip, np.float32)[None, :]
    cst[:, _off['gssm']:_off['gssm'] + 16] = fm(g_ssm, 16)
    tab = np.asarray(rel_bias, np.float32)
    cst[:, _off['ch']:_off['ch'] + 16] = tab[:, 256][None, :]
    j = np.arange(128)[:, None]
    i = np.arange(128)[None, :]
    bx = np.zeros((128, 16, 2, 128), np.float32)
    for t, base in enumerate((128, 0)):
        rel = np.clip(base + i - j, -128, 128) + 128
        bx[:, :, t, :] = tab[:, rel].transpose(1, 0, 2)
    mk = np.zeros((128, 5, 128), np.float32)
    mk[:, 0, :] = np.eye(128)
    mk[:, 1, :] = (j <= i)
    mk[:, 2, :] = (j > i)
    mk[:, 3, :] = 1.0
    mk[:, 4, :] = ((j // 64) == (i // 64))
    return cst, bx.reshape(128, 16 * 256), mk.reshape(128, 5 * 128)


_NC_CACHE = {}


def _run(inputs, SEQ, n_cores, SAMPLE=True):
    f = lambda k: np.asarray(inputs[k], np.float32)
    cst, bx, mk = _consts(f('g_mix')[0], f('g_ffn')[0], f('g_ple')[0], f('g_q')[0], f('g_k')[0], f('b_gate')[0], f('conv_ssm_w')[0],
                          f('conv_ssm_b')[0], f('ffn_conv_w')[0], f('ffn_conv_b')[0], f('dt_bias')[0], f('a_log')[0], f('d_skip')[0],
                          f('g_ssm')[0], f('rel_bias')[0])
    key = (SEQ, SAMPLE)
    if key not in _NC_CACHE:
        _NC_CACHE[key] = build(SEQ, 2, SAMPLE)
    nc = _NC_CACHE[key]
    shared = dict(cst=cst, biasx=bx, mk=mk, w_in=np.ascontiguousarray(f('w_in')[0]), w_pa=np.ascontiguousarray(f('w_proj_a')[0]),
                  w_pb=np.ascontiguousarray(f('w_proj_b')[0]), w_out=np.ascontiguousarray(f('w_out')[0]), w_up=np.ascontiguousarray(f('w_up')[0]),
                  w_down=np.ascontiguousarray(f('w_down')[0]), w_pg=np.ascontiguousarray(f('w_ple_gate')[0]), w_ple=np.ascontiguousarray(f('w_ple')[0]))
    xpr, ppr = f('x_prompt'), f('p_prompt')[0]
    in_maps = []
    for c in range(n_cores):
        m = dict(shared)
        m['xp'] = np.ascontiguousarray(xpr[2 * c:2 * c + 2])
        m['ppT'] = np.ascontiguousarray(ppr[2 * c:2 * c + 2].transpose(0, 2, 1))
        m['xsm'] = np.ascontiguousarray(f('x_sample')[c])
        m['psT'] = np.ascontiguousarray(f('p_sample')[0, c].T)
        ck = f('cache_k')[0, c]
        m['ckT'] = np.ascontiguousarray(ck.transpose(1, 2, 0).reshape(8, 128, 512).transpose(1, 0, 2))
        m['cv'] = np.ascontiguousarray(f('cache_v')[0, c].reshape(512, 1024))
        m['st_ssmT'] = np.ascontiguousarray(f('state_ssm')[0, c].reshape(2048, 128).T)
        m['st_convT'] = np.ascontiguousarray(f('state_conv_ssm')[0, c].T.reshape(24, 128, 3).transpose(1, 0, 2))
        m['st_ffnT'] = np.ascontiguousarray(f('state_conv_ffn')[0, c].T.reshape(48, 128, 2).transpose(1, 0, 2))
        in_maps.append(m)
    res = run_bass_kernel_spmd(nc, in_maps, core_ids=list(range(n_cores)))
    R = res.results
    LK = min(512, SEQ)
    cat = lambda k: np.concatenate([np.asarray(r[k]) for r in R], axis=0)
    stk = lambda k: np.stack([np.asarray(r[k]) for r in R], axis=0)
    yp = cat('yp')
    ys = stk('ysm')
    kp = cat('kp').reshape(-1, LK, 16, 64)[None]
    vp = cat('vp').reshape(-1, LK, 16, 64)[None]
    ssmp = cat('ssmp').reshape(-1, 32, 64, 128)[None]
    cssmp = cat('cssmp')[None]
    cffnp = cat('cffnp')[None]
    ks = stk('ksm').reshape(-1, 32, 16, 64)[None]
    vs = stk('vsm').reshape(-1, 32, 16, 64)[None]
    ssms = cat('ssms').reshape(-1, 32, 64, 128)[None]
    cssms = cat('cssms')[None]
    cffns = cat('cffns')[None]
    return tuple(np.ascontiguousarray(a, dtype=np.float32) for a in (yp, ys, kp, vp, ssmp, cssmp, cffnp, ks, vs, ssms, cssms, cffns))


def kernel(**inputs):
    SEQ = int(np.asarray(inputs['x_prompt']).shape[1])
    n_cores = int(np.asarray(inputs['x_prompt']).shape[0]) // 2
    return _run(inputs, SEQ, n_cores)
```

```python
import numpy as np
from contextlib import ExitStack
import concourse.bass as bass
import concourse.mybir as mybir
from concourse.bass_utils import run_bass_kernel_spmd

F32 = mybir.dt.float32
BF16 = mybir.dt.bfloat16
AF = mybir.ActivationFunctionType
ALU = mybir.AluOpType
AX = mybir.AxisListType
ENGS = ('pe', 'act', 'dve', 'pool', 'sp')
RHS_ENG = 'pool'
POOL_CONV = True
WSCRATCH = True
WLOAD_ENG = 'sp'
EPS = 1e-6
D = 1024
NIN = 10272
NSLOT = 3


class StopBuild(Exception):
    pass


class Sched:
    def __init__(self):
        self.ops = []
        self.last_w = {}
        self.readers = {}
        self.eng_count = {e: 0 for e in ENGS}
        self.dma_count = {}
        self.last_dma = {}

    def add(self, eng, fn, reads=(), writes=(), dma=None):
        writes = list(writes) + [r for r in reads if r.startswith('ps')]
        reads = [r for r in reads if not r.startswith('ps')]
        idx = len(self.ops)
        deps = set()
        for r in reads:
            w = self.last_w.get(r)
            if w is not None:
                deps.add(w)
        for w_ in writes:
            w = self.last_w.get(w_)
            if w is not None:
                deps.add(w)
            deps.update(self.readers.get(w_, ()))
        if fn is None:
            assert not writes
            sig = None
        elif dma is None:
            self.eng_count[eng] += 1
            sig = ('e:' + eng, self.eng_count[eng])
        else:
            self.dma_count[dma] = self.dma_count.get(dma, 0) + 1
            sig = ('d:' + dma, 16 * self.dma_count[dma])
            prev = self.last_dma.get(dma)
            if prev is not None:
                deps.add(prev)
            self.last_dma[dma] = idx
        deps.discard(idx)
        self.ops.append(dict(eng=eng, fn=fn, deps=sorted(deps), sig=sig, dma=dma))
        for r in reads:
            self.readers.setdefault(r, []).append(idx)
        for w_ in writes:
            self.last_w[w_] = idx
            self.readers[w_] = []
        return idx

    def emit(self, nc, stack):
        names = ['e:' + e for e in ENGS] + ['d:' + k for k in self.dma_count]
        sems = {}
        for n in names:
            sems[n] = stack.enter_context(nc.semaphore(n.replace(':', '_')))
        know = {e: {} for e in ENGS}
        opknow = [None] * len(self.ops)
        waits = [None] * len(self.ops)
        for i, op in enumerate(self.ops):
            e = op['eng']
            k = know[e]
            m = {}
            for d in op['deps']:
                dop = self.ops[d]
                sn, sv = dop['sig']
                if k.get(sn, 0) >= sv:
                    continue
                if sn == 'e:pe' and e == 'pe':
                    continue
                m[sn] = max(m.get(sn, 0), sv)
                for a, b in opknow[d].items():
                    if k.get(a, 0) < b:
                        k[a] = b
                if k.get(sn, 0) < sv:
                    k[sn] = sv
            waits[i] = list(m.items())
            opknow[i] = dict(k)
        block = stack.enter_context(nc.Block())
        per_eng = {e: [i for i, op in enumerate(self.ops) if op['eng'] == e] for e in ENGS}

        def make(e):
            def body(engine):
                for i in per_eng[e]:
                    op = self.ops[i]
                    for sn, sv in waits[i]:
                        engine.wait_ge(sems[sn], sv)
                    if op['fn'] is None:
                        continue
                    ins = op['fn'](engine)
                    sn, sv = op['sig']
                    ins.then_inc(sems[sn], 16 if op['dma'] is not None else 1)
            return body

        block.tensor(make('pe'))
        block.scalar(make('act'))
        block.vector(make('dve'))
        block.gpsimd(make('pool'))
        block.sync(make('sp'))


_off = {}
_n = 0
for _name, _w in [('gmix', 8), ('gffn', 8), ('gple', 8), ('gq', 1), ('gk', 1), ('bgate', 16), ('cw', 96), ('cb', 24),
                  ('fw', 144), ('fb', 48), ('dtb', 32), ('alog', 32), ('dskip', 32), ('gssm', 16), ('ch', 16)]:
    _off[_name] = _n
    _n += _w
NCST = _n


def build(SEQ, NSEQ=2, SAMPLE=True, DBG=False, STOP=99):
    NT = SEQ // 512
    LK = min(512, SEQ)
    nc = bass.Bass("TRN2", target_bir_lowering=False)
    S = Sched()
    A = S.add

    def din(name, shape):
        return nc.dram_tensor(name, list(shape), F32, kind="ExternalInput").ap()

    def dout(name, shape):
        return nc.dram_tensor(name, list(shape), F32, kind="ExternalOutput").ap()

    xp = din("xp", [NSEQ, SEQ, D])
    ppT = din("ppT", [NSEQ, 256, SEQ])
    xsm = din("xsm", [32, D])
    psT = din("psT", [256, 32])
    ckT = din("ckT", [128, 8, 512])
    cv = din("cv", [512, D])
    st_ssmT = din("st_ssmT", [128, 2048])
    st_convT = din("st_convT", [128, 24, 3])
    st_ffnT = din("st_ffnT", [128, 48, 2])
    cst_d = din("cst", [128, NCST])
    biasx = din("biasx", [128, 16 * 256])
    mk_d = din("mk", [128, 5 * 128])
    w_in = din("w_in", [D, NIN])
    w_pa = din("w_pa", [D, D])
    w_pb = din("w_pb", [2048, D])
    w_out = din("w_out", [D, D])
    w_up = din("w_up", [D, 6144])
    w_down = din("w_down", [3072, D])
    w_pg = din("w_pg", [D, D])
    w_ple = din("w_ple", [256, D])

    yp = dout("yp", [NSEQ, SEQ, D])
    kp = dout("kp", [NSEQ, LK, D])
    vp = dout("vp", [NSEQ, LK, D])
    ssmp = dout("ssmp", [NSEQ, 2048, 128])
    cssmp = dout("cssmp", [NSEQ, 3, 3072])
    cffnp = dout("cffnp", [NSEQ, 2, 6144])
    ysm = dout("ysm", [32, D])
    ksm = dout("ksm", [32, D])
    vsm = dout("vsm", [32, D])
    ssms = dout("ssms", [1, 2048, 128])
    cssms = dout("cssms", [1, 3, 3072])
    cffns = dout("cffns", [1, 2, 6144])
    out_res = []
    if DBG:
        ntl = NSEQ * (SEQ // 512) + 1
        d_ya = nc.dram_tensor("d_ya", [ntl, 128, 8 * 512], BF16, kind="ExternalOutput").ap()
        d_yb = nc.dram_tensor("d_yb", [ntl, 128, 16 * 512], BF16, kind="ExternalOutput").ap()
        d_x = nc.dram_tensor("d_x", [ntl, 128, 4 * 1024], F32, kind="ExternalOutput").ap()
        d_x2 = nc.dram_tensor("d_x2", [ntl, 128, 4 * 1024], F32, kind="ExternalOutput").ap()
        d_m = nc.dram_tensor("d_m", [ntl, 128, 8 * 512], BF16, kind="ExternalOutput").ap()
        d_g = nc.dram_tensor("d_g", [ntl, 128, 16 * 512], BF16, kind="ExternalOutput").ap()
    tcount = [0]

    with ExitStack() as st:
        def sb(name, shape, dt):
            return st.enter_context(nc.sbuf_tensor(name, list(shape), dt))

        xt = sb("xt", [128, 4, D], F32)
        hT = sb("hT", [128, 8, 512], BF16)
        R3 = sb("R3", [128, 8192], BF16)
        kTb = [sb("kTb%d" % i, [128, 8, 512], BF16) for i in range(2)]
        vb = [sb("vb%d" % i, [128, 4, D], BF16) for i in range(2)]
        yaT = sb("yaT", [128, 8, 512], BF16)
        R2 = sb("R2", [128, 8192], BF16)
        R1 = sb("R1", [128, 24, 512], BF16)
        SCR = sb("SCR", [128, 6656], F32)
        hst = sb("hst", [128, 2048], F32)
        hbf = sb("hbf", [128, 2048], BF16)
        Et = sb("Et", [128, 16, 256], BF16)
        wsl = [sb("wsl%d" % i, [128, 8, 512], BF16) for i in range(NSLOT)]
        xhalo = sb("xhalo", [128, 24, 3], F32)
        uhalo = sb("uhalo", [128, 48, 2], F32)
        cst = sb("cst_sb", [128, NCST], F32)
        mk = sb("mk_sb", [128, 5, 128], BF16)
        identf = sb("identf", [128, 128], F32)
        aneg = sb("aneg", [128, 32], F32)
        xn = sb("xn", [128, D], BF16)
        stg = sb("stg", [128, 2, 512], F32)
        pet = sb("pet", [128, 2, 512], BF16)
        dtr = sb("dtr", [128, 4, 32], F32)
        ssb = sb("ssb", [128, 16], F32)
        PS = st.enter_context(nc.psum_tensor("PS", [128, 4096], F32))

        def bank(i):
            return PS[:, i * 512:(i + 1) * 512]

        def bankb(i):
            return PS[:, i * 512:(i + 1) * 512].bitcast(BF16)

        identb = mk[:, 0, :]
        Vm = mk[:, 1, :]
        Um = mk[:, 2, :]
        onesb = mk[:, 3, :]
        blk1 = mk[:, 4, :]

        def C(name, i=0, n=1):
            o = _off[name] + i
            return cst[:, o:o + n]

        def scr_f(off_b, ncols):
            return SCR[:, off_b // 4: off_b // 4 + ncols]

        def scr_b(off_b, ncols):
            return SCR[:, off_b // 4: off_b // 4 + ncols // 2].bitcast(BF16)

        qT = R3[:, 0:4096].rearrange("p (c t) -> p c t", c=8)
        ybT = R3[:, :].rearrange("p (c t) -> p c t", c=16)
        pT = [R3[:, 4096 + i * 1024: 4096 + i * 1024 + 640] for i in range(3)]
        rden = [R3[:, 7168 + i * 512: 7168 + i * 512 + 256].bitcast(F32) for i in range(2)]
        zs = R2[:, :].rearrange("p (b c) -> p b c", b=4)
        t1 = R2[:, :].bitcast(F32).rearrange("p (c t) -> p c t", c=8)
        tmpE = R2[:, :].bitcast(F32).rearrange("p (h t) -> p h t", h=16)
        xin = R2[:, :].bitcast(F32).rearrange("p (b d) -> p b d", b=4)
        mT = yaT

        def r3q(hp):
            return 'R3_%d' % hp

        def r3pt(i):
            return ['R3_%d' % (8 + 2 * i), 'R3_%d' % (9 + 2 * i)]

        def r3rd(i):
            return ['R3_%d' % (14 + i)]

        SCRN = ['scr%d' % i for i in range(12)]

        dummy = sb("dummy_sb", [128, 8], F32)
        rhs_x = sb("rhs_x", [128, 2048], BF16)
        dg = sb("dg", [128, 128], BF16)

        def phase_barrier():
            A('dve', lambda e: e.memset(dummy[0:1, 0:1], 0.0), writes=SCRN)

        wplan = []

        def tile_plan():
            p = []
            for i in range(16):
                p.append((w_in, 0, 8, 512 * i, 512))
            p.append((w_in, 0, 8, 8192, 32))
            for i in range(4):
                p.append((w_in, 0, 8, 8224 + 512 * i, 512))
            for i in range(2):
                p.append((w_pa, 0, 8, 512 * i, 512))
            for i in range(2):
                for kg in range(2):
                    p.append((w_pb, kg * 1024, 8, 512 * i, 512))
            for i in range(2):
                p.append((w_out, 0, 8, 512 * i, 512))
            for i in range(6):
                p.append((w_up, 0, 8, 512 * i, 512))
                p.append((w_up, 0, 8, 3072 + 512 * i, 512))
            for i in range(2):
                for kg in range(3):
                    p.append((w_down, kg * 1024, 8, 512 * i, 512))
            for i in range(2):
                p.append((w_pg, 0, 8, 512 * i, 512))
                p.append((w_ple, 0, 2, 512 * i, 512))
            return p

        ntiles = NSEQ * NT + (1 if SAMPLE else 0)
        for _ in range(ntiles):
            wplan.extend(tile_plan())
        wstate = dict(issued=0, used=0)

        NPLAN = len(tile_plan())
        wscr = nc.dram_tensor("wscr", [NPLAN, 128, 4096], BF16, kind="Internal").ap() if WSCRATCH else None

        def w_issue(upto):
            while wstate['issued'] < min(upto, len(wplan)):
                i = wstate['issued']
                wap, r0, nk, c0, ncol = wplan[i]
                s = i % NSLOT
                j = i % NPLAN
                if WSCRATCH and i >= NPLAN:
                    src = wscr[j, :, 0:nk * ncol].rearrange("p (k n) -> p k n", k=nk)
                    A(WLOAD_ENG, lambda e, s=s, nk=nk, ncol=ncol, src=src: e.dma_start(out=wsl[s][:, 0:nk, 0:ncol], in_=src),
                      reads=['wscr%d' % j], writes=['w%d' % s], dma=('wh%d' if WLOAD_ENG == 'sp' else 'w%d') % s)
                else:
                    src = wap[r0:r0 + nk * 128, c0:c0 + ncol].rearrange("(k p) n -> p k n", p=128)
                    A('pool', lambda e, s=s, nk=nk, ncol=ncol, src=src: e.dma_start(out=wsl[s][:, 0:nk, 0:ncol], in_=src),
                      writes=['w%d' % s], dma='w%d' % s)
                    if WSCRATCH and ntiles > 1:
                        dst = wscr[j, :, 0:nk * ncol].rearrange("p (k n) -> p k n", k=nk)
                        A('sp', lambda e, s=s, nk=nk, ncol=ncol, dst=dst: e.dma_start(out=dst, in_=wsl[s][:, 0:nk, 0:ncol]),
                          reads=['w%d' % s], writes=['wscr%d' % j], dma='ws%d' % s)
                wstate['issued'] += 1

        def w_get(wap, c0, prefetch=True):
            i = wstate['used']
            assert wplan[i][0] is wap and wplan[i][3] == c0, (i, c0)
            w_issue(i + NSLOT if prefetch else i + 1)
            wstate['used'] += 1
            s = i % NSLOT
            return wsl[s], 'w%d' % s

        A('sp', lambda e: e.dma_start(out=cst[:], in_=cst_d), writes=['cst'], dma='c0')
        A('pool', lambda e: e.dma_start(out=mk[:].rearrange("p a b -> p (a b)"), in_=mk_d), writes=['mk'], dma='c1')
        A('sp', lambda e: e.dma_start(out=identf[:], in_=mk_d[:, 0:128]), writes=['identf'], dma='c2')
        A('sp', lambda e: e.dma_start(out=tmpE.rearrange("p h t -> p (h t)"), in_=biasx), writes=['tmpE'], dma='c3')
        A('dve', lambda e: e.tensor_tensor(out=tmpE, in0=tmpE, in1=C('ch', 0, 16).unsqueeze(2).to_broadcast([128, 16, 256]),
                                           op=ALU.subtract), reads=['cst', 'tmpE'], writes=['tmpE'])
        A('act', lambda e: e.activation(out=Et[:], in_=tmpE, func=AF.Exp), reads=['tmpE'], writes=['Et'])
        A('dve', lambda e: e.memset(Et[64:128, :, 128:192], 0.0), writes=['Et'])
        A('act', lambda e: e.activation(out=aneg[:], in_=C('alog', 0, 32), func=AF.Exp), reads=['cst'], writes=['aneg'])
        A('dve', lambda e: e.tensor_scalar(out=aneg[:], in0=aneg[:], scalar1=-1.0, scalar2=None, op0=ALU.mult),
          reads=['aneg'], writes=['aneg'])
        A('dve', lambda e: e.memset(dummy[0:1, 1:2], 0.0), reads=['tmpE', 'Et'], writes=['R2_%d' % i for i in range(16)])

        xnb = [xn, scr_b(24576, 1024)]

        def rms_A(b, bs, junk, from_xin=False):
            k = b % 2
            xb = (xin if from_xin else xt)[0:bs, b, :]
            xr = 'xt%d' % b
            sr_ = 'ssb%d' % k
            sc_ = ssb[:, 4 * k:4 * k + 4]
            xrs = ['R2_%d' % (4 * b + j_) for j_ in range(4)] if from_xin else [xr]
            A('act', lambda e: e.activation(out=junk[0:bs, :], in_=xb, func=AF.Square, accum_out=sc_[0:bs, 0:1]),
              reads=xrs, writes=['scr0', sr_])
            A('act', lambda e: e.activation(out=sc_[0:bs, 1:2], in_=sc_[0:bs, 0:1], func=AF.Sqrt, scale=1.0 / D, bias=EPS),
              reads=[sr_], writes=[sr_])
            A('dve', lambda e: e.reciprocal(out=sc_[0:bs, 2:3], in_=sc_[0:bs, 1:2]), reads=[sr_], writes=[sr_])
            A('dve', lambda e: e.tensor_scalar(out=xnb[k][0:bs, :], in0=xb, scalar1=sc_[0:bs, 2:3], scalar2=None, op0=ALU.mult),
              reads=xrs + [sr_], writes=['xn%d' % k])

        def rms_B(b, bs, gname):
            k = b % 2
            pb = 3

            def tr(e):
                ins = None
                for kc in range(8):
                    ins = e.transpose(out=bankb(pb)[:, kc * 128: kc * 128 + bs], in_=xnb[k][0:bs, kc * 128:(kc + 1) * 128],
                                      identity=identb[0:bs, 0:bs])
                return ins
            A('pe', tr, reads=['xn%d' % k, 'mk'], writes=['ps%d' % pb])
            A('dve', lambda e: e.tensor_tensor(
                out=hT[:, :, b * 128: b * 128 + bs],
                in0=bankb(pb).rearrange("p (k t) -> p k t", k=8)[:, :, 0:bs],
                in1=C(gname, 0, 8).unsqueeze(2).to_broadcast([128, 8, bs]), op=ALU.mult),
              reads=['ps%d' % pb, 'cst'], writes=['hT'])

        def rmsnorm_T(nb, bs, gname, junk, from_xin=False):
            for b in range(nb):
                rms_A(b, bs, junk, from_xin)
                if b >= 1:
                    rms_B(b - 1, bs, gname)
            rms_B(nb - 1, bs, gname)

        def mm_fm(e, out, slot, c, srcT, nk, ncol, koff=0, start=True, stop=True):
            ins = None
            for kc in range(nk):
                ins = e.matmul(out, lhsT=slot[:, kc, c * 128:(c + 1) * 128], rhs=srcT[:, koff + kc, 0:ncol],
                               start=(start and kc == 0), stop=(stop and kc == nk - 1))
            return ins

        def mm_tm(e, out, slot, srcT, b, bs, nk, ncols, koff=0, start=True, stop=True):
            ins = None
            for kc in range(nk):
                ins = e.matmul(out, lhsT=srcT[:, koff + kc, b * 128: b * 128 + bs], rhs=slot[:, kc, 0:ncols],
                               start=(start and kc == 0), stop=(stop and kc == nk - 1))
            return ins

        def out_dma(dst, src, reads, key):
            rn = 'out%d' % len(out_res)
            out_res.append(rn)
            A('sp', lambda e: e.dma_start(out=dst, in_=src), reads=reads, writes=[rn], dma=key)

        stgc = [0]

        def stage():
            i = stgc[0] % 2
            stgc[0] += 1
            return stg[:, i, :], 'stg%d' % i

        stopped = [False]

        def chk(x):
            if STOP <= x:
                raise StopBuild()

        tiles = [('p', s_, ti_) for s_ in range(NSEQ) for ti_ in range(NT)] + ([('s', 0, 0)] if SAMPLE else [])

        def emit_xprefetch(tidx):
            kind_, s_, ti_ = tiles[tidx]
            if kind_ == 's':
                A('sp', lambda e: e.dma_start(out=xin[0:32, 0, :], in_=xsm), writes=['R2_%d' % j_ for j_ in range(4)], dma='xin0')
            else:
                for b_ in range(4):
                    src = xp[s_, ti_ * 512 + b_ * 128: ti_ * 512 + (b_ + 1) * 128, :]
                    A('sp', lambda e, b_=b_, src=src: e.dma_start(out=xin[:, b_, :], in_=src),
                      writes=['R2_%d' % (4 * b_ + j_) for j_ in range(4)], dma='xin%d' % b_)

        def do_tile(kind, s, ti):
            if stopped[0]:
                return
            tidx = tiles.index((kind, s, ti))
            sample = (kind == 's')
            bs = 32 if sample else 128
            nb = 1 if sample else 4
            ncol = nb * bs
            first = sample or ti == 0
            last = sample or ti == NT - 1
            t0 = 0 if sample else ti * 512
            cur = 0 if sample else ti % 2
            prv = 1 - cur
            xsrc = xsm if sample else xp[s, t0:t0 + ncol, :]
            ydst = ysm if sample else yp[s, t0:t0 + ncol, :]
            k_dst = ksm if sample else kp[s]
            v_dst = vsm if sample else vp[s]
            kv_out = sample or (SEQ - (t0 + ncol) < LK)
            kv_row0 = 0 if sample else t0 - (SEQ - LK)
            xres = ['xt%d' % b for b in range(nb)]

            def cols(b):
                return slice(b * 128, b * 128 + bs)

            if first:
                if sample:
                    A('sp', lambda e: e.dma_start(out=hst[:], in_=st_ssmT), writes=['hst%d' % g for g in range(4)], dma='c0')
                    A('act', lambda e: e.activation(out=hbf[:], in_=hst[:], func=AF.Copy),
                      reads=['hst%d' % g for g in range(4)], writes=['hbf%d' % g for g in range(4)])
                    A('sp', lambda e: e.dma_start(out=xhalo[:], in_=st_convT), writes=['xhalo%d' % k for k in range(24)], dma='c2')
                    A('sp', lambda e: e.dma_start(out=uhalo[:], in_=st_ffnT), writes=['uhalo%d' % k for k in range(48)], dma='c3')
                    A('pool', lambda e: e.dma_start(out=kTb[1][:], in_=ckT), writes=['kT1_%d' % h for h in range(8)], dma='c1')
                    A('pool', lambda e: e.dma_start(out=vb[1][:], in_=cv.rearrange("(b p) d -> p b d", p=128)),
                      writes=['v1_%d' % b for b in range(4)], dma='c4')
                else:
                    A('dve', lambda e: e.memset(hst[:], 0.0), writes=['hst%d' % g for g in range(4)])
                    A('dve', lambda e: e.memset(hbf[:], 0.0), writes=['hbf%d' % g for g in range(4)])
                    A('dve', lambda e: e.memset(xhalo[:], 0.0), writes=['xhalo%d' % k for k in range(24)])
                    A('dve', lambda e: e.memset(uhalo[:], 0.0), writes=['uhalo%d' % k for k in range(48)])

            chk(0.1)
            if tidx == 0:
                emit_xprefetch(0)
            if sample:
                A('pool', lambda e: e.dma_start(out=pet[:, :, 0:32], in_=psT.rearrange("(c p) t -> p c t", p=128)),
                  writes=['pet'], dma='pet')
            else:
                A('pool', lambda e: e.dma_start(out=pet[:], in_=ppT[s, :, t0:t0 + 512].rearrange("(c p) t -> p c t", p=128)),
                  writes=['pet'], dma='pet')

            chk(0.2)
            phase_barrier()
            junk = scr_b(0, 1024)
            sqb = [scr_b(2048 + i * 1024, 512) for i in range(2)]
            qraw = [scr_f(4096 + i * 2048, 512) for i in range(2)]
            rs = [scr_f(8192 + i * 2048, 512) for i in range(2)]
            knf = scr_f(12288, 512)
            xraw = [scr_f(14336 + i * 2064, 516) for i in range(2)]
            cacc = [scr_f(18464 + i * 2048, 512) for i in range(2)]
            rmsnorm_T(nb, bs, 'gmix', junk, from_xin=True)

            chk(0.3)
            pcount = [0]

            PBANKS = (0, 1, 5, 6, 7)

            def nextbank():
                i = PBANKS[pcount[0] % len(PBANKS)]
                pcount[0] += 1
                return i

            def qk_chunk(slot, wr, c, hp, is_k):
                pb = nextbank()
                i = hp % 2
                A('pe', lambda e: mm_fm(e, bank(pb)[:, 0:ncol], slot, c, hT, 8, ncol), reads=[wr, 'hT'], writes=['ps%d' % pb])
                A('act', lambda e: e.activation(out=sqb[i][:, 0:ncol], in_=bank(pb)[:, 0:ncol], func=AF.Square),
                  reads=['ps%d' % pb], writes=['scr%d' % (1 + i)])
                gcol = C('gk') if is_k else C('gq')
                A('dve', lambda e: e.tensor_scalar(out=qraw[i][:, 0:ncol], in0=bank(pb)[:, 0:ncol], scalar1=gcol, scalar2=None, op0=ALU.mult),
                  reads=['ps%d' % pb, 'cst'], writes=['scr%d' % (3 + i)])
                return lambda: qk_chunk_B(pb, i, hp, is_k)

            def qk_chunk_B(pb, i, hp, is_k):
                A('pe', lambda e: e.matmul(bank(2)[:, 0:ncol], lhsT=blk1, rhs=sqb[i][:, 0:ncol], start=True, stop=True),
                  reads=['scr%d' % (1 + i), 'mk'], writes=['ps2'])
                A('act', lambda e: e.activation(out=rs[i][:, 0:ncol], in_=bank(2)[:, 0:ncol], func=AF.Ln, scale=1.0 / 64, bias=EPS),
                  reads=['ps2'], writes=['scr%d' % (5 + i)])
                A('act', lambda e: e.activation(out=rs[i][:, 0:ncol], in_=rs[i][:, 0:ncol], func=AF.Exp, scale=-0.5),
                  reads=['scr%d' % (5 + i)], writes=['scr%d' % (5 + i)])
                if not is_k:
                    A('dve', lambda e: e.tensor_tensor(out=qT[:, hp, 0:ncol], in0=qraw[i][:, 0:ncol], in1=rs[i][:, 0:ncol], op=ALU.mult),
                      reads=['scr%d' % (3 + i), 'scr%d' % (5 + i)], writes=[r3q(hp)])
                else:
                    A('dve', lambda e: e.tensor_tensor(out=kTb[cur][:, hp, 0:ncol], in0=qraw[i][:, 0:ncol], in1=rs[i][:, 0:ncol], op=ALU.mult),
                      reads=['scr%d' % (3 + i), 'scr%d' % (5 + i)], writes=['kT%d_%d' % (cur, hp)])
                    if kv_out:
                        A('dve', lambda e: e.tensor_tensor(out=knf[:, 0:ncol], in0=qraw[i][:, 0:ncol], in1=rs[i][:, 0:ncol], op=ALU.mult),
                          reads=['scr%d' % (3 + i), 'scr%d' % (5 + i)], writes=['scr7'])
                        sg, sr = stage()

                        def trk(e):
                            ins = None
                            for b in range(nb):
                                ins = e.transpose(out=bank(4)[0:bs, b * 128:(b + 1) * 128], in_=knf[:, cols(b)], identity=identf[:, :])
                            return ins
                        A('pe', trk, reads=['scr7', 'identf'], writes=['ps4'])
                        A('act', lambda e: e.activation(out=sg[0:bs, 0:nb * 128], in_=bank(4)[0:bs, 0:nb * 128], func=AF.Copy),
                          reads=['ps4'], writes=[sr])
                        dst = k_dst[kv_row0:kv_row0 + ncol, hp * 128:(hp + 1) * 128].rearrange("(b p) d -> p b d", p=bs)
                        out_dma(dst, sg[0:bs, 0:nb * 128].rearrange("p (b d) -> p b d", b=nb), [sr], sr)

            pend = []
            for is_k_ in (False, True):
                for blk in range(2):
                    slot, wr = w_get(w_in, (1024 if is_k_ else 0) + 512 * blk)
                    for c in range(4):
                        pB = qk_chunk(slot, wr, c, blk * 4 + c, is_k_)
                        if pend:
                            pend.pop(0)()
                        pend.append(pB)
            while pend:
                pend.pop(0)()
            chk(0.5)
            for blk in range(2):
                slot, wr = w_get(w_in, 2048 + 512 * blk)
                for b in range(nb):
                    pb = nextbank()
                    A('pe', lambda e, b=b, pb=pb, slot=slot: mm_tm(e, bank(pb)[0:bs, :], slot, hT, b, bs, 8, 512), reads=[wr, 'hT'], writes=['ps%d' % pb])
                    A('act', lambda e, b=b, pb=pb, blk=blk: e.activation(out=vb[cur][0:bs, b, blk * 512:(blk + 1) * 512], in_=bank(pb)[0:bs, :], func=AF.Copy),
                      reads=['ps%d' % pb], writes=['v%d_%d' % (cur, b)])
                    if kv_out:
                        sg, sr = stage()
                        A('dve', lambda e, pb=pb, sg=sg: e.tensor_copy(out=sg[0:bs, :], in_=bank(pb)[0:bs, :]), reads=['ps%d' % pb], writes=[sr])
                        r0 = kv_row0 + b * 128
                        out_dma(v_dst[r0:r0 + bs, blk * 512:(blk + 1) * 512], sg[0:bs, :], [sr], sr)
            chk(0.6)
            for b_ in range(nb):
                A('act', lambda e, b_=b_: e.activation(out=xt[0:bs, b_, :], in_=xin[0:bs, b_, :], func=AF.Copy),
                  reads=['R2_%d' % (4 * b_ + j_) for j_ in range(4)], writes=['xt%d' % b_])
            for blk in range(4):
                slot, wr = w_get(w_in, 3072 + 512 * blk)
                for b in range(nb):
                    pb = nextbank()
                    A('pe', lambda e, b=b, pb=pb, slot=slot: mm_tm(e, bank(pb)[0:bs, :], slot, hT, b, bs, 8, 512), reads=[wr, 'hT'], writes=['ps%d' % pb])
                    A('act', lambda e, b=b, pb=pb, blk=blk: e.activation(out=zs[0:bs, b, blk * 512:(blk + 1) * 512], in_=bank(pb)[0:bs, :], func=AF.Silu),
                      reads=['ps%d' % pb], writes=['R2_%d' % (b * 4 + blk)])
            chk(0.7)
            pend = []
            for blk in range(6):
                slot, wr = w_get(w_in, 5120 + 512 * blk)
                for c in range(4):
                    def xbc_A(c=c, cc=blk * 4 + c, slot=slot, wr=wr):
                        i = cc % 2
                        pb = nextbank()
                        xr = 'scr%d' % (8 + i)
                        ar = 'scr%d' % (10 + i)
                        hr = 'xhalo%d' % cc
                        A('pe', lambda e: mm_fm(e, bank(pb)[:, 0:ncol], slot, c, hT, 8, ncol), reads=[wr, 'hT'], writes=['ps%d' % pb])
                        A('act', lambda e: e.activation(out=xraw[i][:, 3:3 + ncol], in_=bank(pb)[:, 0:ncol], func=AF.Copy),
                          reads=['ps%d' % pb], writes=[xr])
                        xh_ = 'xrh%d' % i
                        A('act', lambda e: e.activation(out=cacc[i][:, 0:ncol], in_=bank(pb)[:, 0:ncol], func=AF.Identity,
                                                        scale=C('cw', cc * 4 + 3), bias=C('cb', cc)),
                          reads=['ps%d' % pb, 'cst'], writes=[ar])
                        A('dve', lambda e: e.tensor_copy(out=xraw[i][:, 0:3], in_=xhalo[:, cc, :]), reads=[hr], writes=[xh_])
                        for tap in range(0, 3):
                            A('dve', lambda e, tap=tap: e.scalar_tensor_tensor(
                                out=cacc[i][:, 0:ncol], in0=xraw[i][:, tap:tap + ncol], scalar=C('cw', cc * 4 + tap),
                                in1=cacc[i][:, 0:ncol], op0=ALU.mult, op1=ALU.add), reads=[xr, xh_, ar, 'cst'], writes=[ar])
                        A('act', lambda e: e.activation(out=xhalo[:, cc, :], in_=xraw[i][:, ncol:ncol + 3], func=AF.Copy),
                          reads=[xr], writes=[hr])

                        def xbc_B():
                            A('act', lambda e: e.activation(out=R1[:, cc, 0:ncol], in_=cacc[i][:, 0:ncol], func=AF.Silu),
                              reads=[ar], writes=['R1_%d' % cc])
                        return xbc_B
                    pB = xbc_A()
                    if pend:
                        pend.pop(0)()
                    pend.append(pB)
            while pend:
                pend.pop(0)()
            chk(0.8)
            slot, wr = w_get(w_in, 8192)
            for b in range(nb):
                pb = nextbank()
                A('pe', lambda e, b=b, pb=pb, slot=slot: mm_tm(e, bank(pb)[0:bs, 0:32], slot, hT, b, bs, 8, 32), reads=[wr, 'hT'], writes=['ps%d' % pb])
                A('dve', lambda e, b=b, pb=pb: e.tensor_tensor(out=dtr[0:bs, b, :], in0=bank(pb)[0:bs, 0:32], in1=C('dtb', 0, 32)[0:bs, :], op=ALU.add),
                  reads=['ps%d' % pb, 'cst'], writes=['dtr'])

            if STOP <= 1:
                stopped[0] = True
                return
            LA = 2
            jobs = []
            for qb in range(nb):
                gb = (0 if sample else ti * 4) + qb
                if sample:
                    kl = [(1, t, t, 128) for t in range(4)] + [(0, 0, 4, 32)]
                else:
                    kl = []
                    for t in range(5):
                        kbi = gb - 4 + t
                        if kbi < 0:
                            continue
                        kl.append(((kbi // 4) % 2, kbi % 4, t, 128))
                for hp in range(8):
                    for half in range(2):
                        jobs.append((qb, hp, half, kl))
            nq = bs

            def att_sc(j):
                qb, hp, half, kl = jobs[j]
                si = j % 3
                bA, bB = 2 * si, 2 * si + 1
                p0, p1 = half * 64, half * 64 + 64

                def sc(e):
                    ins = None
                    for (sl, bk, t, nk) in kl:
                        o = bank(bA)[0:nk, t * 128: t * 128 + nq] if t < 4 else bank(bB)[0:nk, 0:nq]
                        ins = e.matmul(o, lhsT=kTb[sl][p0:p1, hp, bk * 128: bk * 128 + nk], rhs=qT[p0:p1, hp, qb * 128: qb * 128 + nq],
                                       start=True, stop=True)
                    return ins
                rds = [r3q(hp)] + ['kT%d_%d' % (sl, hp) for (sl, bk, t, nk) in kl]
                A('pe', sc, reads=rds, writes=['ps%d' % bA, 'ps%d' % bB])

            def att_rest(j):
                qb, hp, half, kl = jobs[j]
                si = j % 3
                h = hp * 2 + half
                bA, bB = 2 * si, 2 * si + 1
                p0, p1 = half * 64, half * 64 + 64
                pp = (j // 2) % 2
                nc0 = pp * 128
                far = [x for x in kl if x[2] < 4]
                if far:
                    tlo = far[0][2]
                    A('act', lambda e: e.activation(
                        out=pT[si][:, tlo * 128:512].rearrange("p (t q) -> p t q", q=128)[:, :, 0:nq],
                        in_=bank(bA)[:, tlo * 128:512].rearrange("p (t q) -> p t q", q=128)[:, :, 0:nq],
                        func=AF.Exp, scale=0.125), reads=['ps%d' % bA], writes=r3pt(si))
                nk4 = kl[-1][3]
                A('act', lambda e: e.activation(out=pT[si][0:nk4, 512:512 + nq], in_=bank(bB)[0:nk4, 0:nq],
                                                func=AF.Exp, scale=0.125), reads=['ps%d' % bB], writes=r3pt(si))
                has3 = any(x[2] == 3 for x in kl)
                if has3 and nk4 == 128:
                    A('dve', lambda e: e.tensor_tensor(
                        out=pT[si][:, 384:640].rearrange("p (t q) -> p t q", q=128)[:, :, 0:nq],
                        in0=pT[si][:, 384:640].rearrange("p (t q) -> p t q", q=128)[:, :, 0:nq],
                        in1=Et[:, h, :].rearrange("p (t q) -> p t q", q=128)[:, :, 0:nq], op=ALU.mult),
                      reads=r3pt(si) + ['Et'], writes=r3pt(si))
                else:
                    if has3:
                        A('dve', lambda e: e.tensor_tensor(out=pT[si][:, 384:384 + nq], in0=pT[si][:, 384:384 + nq],
                                                           in1=Et[:, h, 0:nq], op=ALU.mult),
                          reads=r3pt(si) + ['Et'], writes=r3pt(si))
                    A('dve', lambda e: e.tensor_tensor(out=pT[si][0:nk4, 512:512 + nq], in0=pT[si][0:nk4, 512:512 + nq],
                                                       in1=Et[0:nk4, h, 128:128 + nq], op=ALU.mult),
                      reads=r3pt(si) + ['Et'], writes=r3pt(si))
                if any(x[2] == 0 for x in kl) and nq > 64:
                    A('dve', lambda e: e.memset(pT[si][0:64, 64:128], 0.0), reads=r3pt(si), writes=r3pt(si))

                def pv(e):
                    ins = None
                    n = len(kl)
                    for jj, (sl, bk, t, nk) in enumerate(kl):
                        e.matmul(bank(6 + pp)[p0:p1, 0:nq], lhsT=vb[sl][0:nk, bk, h * 64:(h + 1) * 64], rhs=pT[si][0:nk, t * 128: t * 128 + nq],
                                 start=(jj == 0), stop=(jj == n - 1), skip_group_check=True)
                        ins = e.matmul(bank(6 + pp)[p0:p1, 128:128 + nq], lhsT=onesb[0:nk, 0:64], rhs=pT[si][0:nk, t * 128: t * 128 + nq],
                                       start=False, stop=(jj == n - 1), skip_group_check=True)
                    return ins
                A('pe', pv, reads=r3pt(si) + ['mk'] + ['v%d_%d' % (sl, bk) for (sl, bk, t, nk) in kl],
                  writes=['ps%d' % (6 + pp)])
                if half == 1:
                    fins.append(lambda: att_fin(qb, hp, pp, nc0))

            def att_fin(qb, hp, pp, nc0):
                if True:
                    ri = pp
                    A('act', lambda e: e.activation(out=rden[ri][:, 0:nq], in_=bank(6 + pp)[:, 128:128 + nq], func=AF.Ln),
                      reads=['ps%d' % (6 + pp)], writes=r3rd(ri))
                    A('act', lambda e: e.activation(out=rden[ri][:, 0:nq], in_=rden[ri][:, 0:nq], func=AF.Exp, scale=-1.0),
                      reads=r3rd(ri), writes=r3rd(ri))
                    A('dve', lambda e: e.tensor_tensor(out=yaT[:, hp, qb * 128: qb * 128 + nq], in0=bank(6 + pp)[:, 0:nq],
                                                       in1=rden[ri][:, 0:nq], op=ALU.mult),
                      reads=['ps%d' % (6 + pp)] + r3rd(ri), writes=['yaT'])

            nj = len(jobs)
            fins = []
            for j in range(min(LA, nj)):
                att_sc(j)
            for j in range(nj):
                if j + LA < nj:
                    att_sc(j + LA)
                npend = len(fins)
                att_rest(j)
                for _ in range(npend):
                    fins.pop(0)()
            while fins:
                fins.pop(0)()

            if STOP <= 2:
                stopped[0] = True
                return
            tl = tcount[0]
            tcount[0] += 1
            if DBG:
                out_dma(d_ya[tl], yaT[:].rearrange("p a b -> p (a b)"), ['yaT'], 'dbg')
            phase_barrier()
            rhs_hi = [scr_b(0, 1024), rhs_x[:, 0:1024]]
            rhs_lo = [scr_b(2048, 1024), rhs_x[:, 1024:2048]]
            dec = [scr_b(4096 + i * 2048, 1024) for i in range(2)]
            dxb = [scr_b(8192 + i * 1024, 512) for i in range(2)]
            xsd = [scr_b(10240 + i * 1024, 512) for i in range(2)]
            xwb = [scr_b(12288 + i * 1024, 512) for i in range(2)]
            bm_tm = scr_b(14336, 512)
            cbm = scr_b(15360, 512)
            yg = scr_b(16384, 2048)
            tmpS = scr_f(20480, 512)
            yv = scr_f(22528, 512)
            sm = scr_f(24576, 512)
            exo_all = sm[:, 0:384].rearrange("p (b c) -> p b c", c=96)
            a_hi_all = sm[:, 384:448].bitcast(BF16)
            a_lo_all = sm[:, 448:512].bitcast(BF16)
            ssy = ssb[:, 8:16]
            nt = bs
            ex1v = yv[0:nt, 0:nb * 32].rearrange("p (b c) -> p b c", c=32)
            a_fv = tmpS[0:nt, 0:nb * 32].rearrange("p (b c) -> p b c", c=32)
            A('act', lambda e: e.activation(out=ex1v, in_=dtr[0:nt, 0:nb, :], func=AF.Exp), reads=['dtr'], writes=['yv'])
            A('act', lambda e: e.activation(out=dtr[0:nt, 0:nb, :], in_=ex1v, func=AF.Ln, bias=1.0), reads=['yv'], writes=['sm1', 'dtr'])
            A('dve', lambda e: e.tensor_tensor(out=a_fv, in0=dtr[0:nt, 0:nb, :], in1=aneg[0:nt, :].unsqueeze(1).to_broadcast([nt, nb, 32]), op=ALU.mult),
              reads=['sm1', 'aneg'], writes=['tmpS'])
            A('dve', lambda e: e.tensor_copy(out=a_hi_all[0:nt, 0:nb * 32], in_=tmpS[0:nt, 0:nb * 32]), reads=['tmpS'], writes=['sm3'])
            A('dve', lambda e: e.tensor_tensor(out=a_lo_all[0:nt, 0:nb * 32], in0=tmpS[0:nt, 0:nb * 32], in1=a_hi_all[0:nt, 0:nb * 32], op=ALU.subtract),
              reads=['tmpS', 'sm3'], writes=['sm4'])

            def cums(e):
                ins = None
                for b in range(nb):
                    for (m, c0, M) in ((Vm, 0, nt), (Um, 32, nt), (onesb, 64, 128)):
                        o = bank(0)[0:M, b * 96 + c0: b * 96 + c0 + 32]
                        e.matmul(o, lhsT=m[0:nt, 0:M], rhs=a_hi_all[0:nt, b * 32:(b + 1) * 32], start=True, stop=False, skip_group_check=True)
                        ins = e.matmul(o, lhsT=m[0:nt, 0:M], rhs=a_lo_all[0:nt, b * 32:(b + 1) * 32], start=False, stop=True, skip_group_check=True)
                return ins
            A('pe', cums, reads=['sm3', 'sm4', 'mk'], writes=['ps0'])
            A('act', lambda e: e.activation(out=sm[0:nt, 0:nb * 96], in_=bank(0)[0:nt, 0:nb * 96], func=AF.Exp), reads=['ps0'], writes=['sm5', 'sm6'])
            if nt < 128:
                A('act', lambda e: e.activation(out=exo_all[:, 0:nb, 64:96], in_=bank(0)[:, 0:nb * 96].rearrange("p (b c) -> p b c", c=96)[:, :, 64:96],
                                                func=AF.Exp), reads=['ps0'], writes=['sm6'])
            A('dve', lambda e: e.tensor_tensor(out=exo_all[0:nt, 0:nb, 32:64], in0=exo_all[0:nt, 0:nb, 32:64], in1=dtr[0:nt, 0:nb, :], op=ALU.mult),
              reads=['sm5', 'sm1'], writes=['sm7', 'sm5'])
            blkf = []
            for b in range(nb):
                cb_ = cols(b)
                dt_ = dtr[:, b, :]
                exo = exo_all[:, b, :]
                w2 = exo_all[:, b, 32:64]
                a_hi = a_hi_all[:, b * 32:(b + 1) * 32]
                a_lo = a_lo_all[:, b * 32:(b + 1) * 32]
                def ssd_pre(b=b, cb_=cb_):

                    def cbf(e, cb_=cb_):
                        ins = None
                        for g in range(4):
                            ins = e.matmul(bank(3)[0:nt, g * 128: g * 128 + nt], lhsT=R1[:, 16 + g, cb_], rhs=R1[:, 20 + g, cb_], start=True, stop=True)
                        return ins
                    A('pe', cbf, reads=['R1_%d' % c for c in range(16, 24)], writes=['ps3'])
                    A('dve', lambda e: e.tensor_tensor(out=cbm[0:nt, :].rearrange("p (g i) -> p g i", g=4)[:, :, 0:nt],
                                                       in0=bank(3)[0:nt, :].rearrange("p (g i) -> p g i", g=4)[:, :, 0:nt],
                                                       in1=Vm[0:nt, 0:nt].unsqueeze(1).to_broadcast([nt, 4, nt]), op=ALU.mult),
                      reads=['ps3', 'mk'], writes=['cbm'])

                    def trb(e, cb_=cb_):
                        ins = None
                        for g in range(4):
                            ins = e.transpose(out=bankb(4)[0:nt, g * 128:(g + 1) * 128], in_=R1[:, 16 + g, cb_], identity=identb)
                        return ins
                    A('pe', trb, reads=['R1_%d' % c for c in range(16, 20)] + ['mk'], writes=['ps4'])
                    A('act', lambda e: e.activation(out=bm_tm[0:nt, :], in_=bankb(4)[0:nt, 0:512], func=AF.Copy), reads=['ps4'], writes=['bm_tm'])
                def ssd_stage1(g, b=b, cb_=cb_, dt_=dt_, exo=exo, w2=w2, a_hi=a_hi, a_lo=a_lo):
                    i = g % 2
                    hs = slice(8 * g, 8 * g + 8)
                    W8 = 8 * nt
                    sb0, sb1 = ((1, 2), (0, 3))[i]
                    A(RHS_ENG, lambda e: e.tensor_tensor(out=rhs_hi[i][0:nt, 0:8 * nt].rearrange("p (h i) -> p h i", h=8),
                                                         in0=Vm[0:nt, 0:nt].unsqueeze(1).to_broadcast([nt, 8, nt]),
                                                         in1=a_hi[0:nt, hs].unsqueeze(2).to_broadcast([nt, 8, nt]), op=ALU.mult),
                      reads=['sm3', 'mk'], writes=['rhs_hi%d' % i])
                    A(RHS_ENG, lambda e: e.tensor_tensor(out=rhs_lo[i][0:nt, 0:8 * nt].rearrange("p (h i) -> p h i", h=8),
                                                         in0=Vm[0:nt, 0:nt].unsqueeze(1).to_broadcast([nt, 8, nt]),
                                                         in1=a_lo[0:nt, hs].unsqueeze(2).to_broadcast([nt, 8, nt]), op=ALU.mult),
                      reads=['sm4', 'mk'], writes=['rhs_lo%d' % i])

                    def segf(e):
                        ins = None
                        for k_, c0 in enumerate(range(0, W8, 512)):
                            c1 = min(W8, c0 + 512)
                            o = bank((sb0, sb1)[k_])[0:nt, 0:c1 - c0]
                            e.matmul(o, lhsT=Um[0:nt, 0:nt], rhs=rhs_hi[i][0:nt, c0:c1], start=True, stop=False)
                            ins = e.matmul(o, lhsT=Um[0:nt, 0:nt], rhs=rhs_lo[i][0:nt, c0:c1], start=False, stop=True)
                        return ins
                    A('pe', segf, reads=['rhs_hi%d' % i, 'rhs_lo%d' % i, 'mk'], writes=['ps%d' % sb0, 'ps%d' % sb1])
                    for k_, c0 in enumerate(range(0, W8, 512)):
                        c1 = min(W8, c0 + 512)
                        bk_ = (sb0, sb1)[k_]
                        A('act', lambda e, c0=c0, c1=c1, bk_=bk_: e.activation(out=dec[i][0:nt, c0:c1], in_=bank(bk_)[0:nt, 0:c1 - c0], func=AF.Exp),
                          reads=['ps%d' % bk_], writes=['dec%d' % i])

                    def trx(e):
                        ins = None
                        for j in range(4):
                            ins = e.transpose(out=bankb(4)[0:nt, i * 512 + j * 128: i * 512 + (j + 1) * 128], in_=R1[:, 4 * g + j, cb_], identity=identb)
                        return ins
                    A('pe', trx, reads=['R1_%d' % (4 * g + j) for j in range(4)] + ['mk'], writes=['ps4'])
                    xsv = bankb(4)[0:nt, i * 512:(i + 1) * 512].rearrange("p (h d) -> p h d", h=8)
                    for (dst, nm, sc_ap, rr) in ((dxb[i], 'dx%d' % i, dt_[0:nt, hs], 'sm1'), (xsd[i], 'xsd%d' % i, C('dskip', 0, 32)[0:nt, hs], 'cst'),
                                                 (xwb[i], 'xw%d' % i, w2[0:nt, hs], 'sm7')):
                        A('dve', lambda e, dst=dst, sc_ap=sc_ap: e.tensor_tensor(
                            out=dst[0:nt, :].rearrange("p (h d) -> p h d", h=8), in0=xsv,
                            in1=sc_ap.unsqueeze(2).to_broadcast([nt, 8, 64]), op=ALU.mult), reads=['ps4', rr], writes=[nm])
                    A('dve', lambda e: e.tensor_tensor(
                        out=dec[i][0:nt, 0:W8].rearrange("p (h i) -> p h i", h=8), in0=dec[i][0:nt, 0:W8].rearrange("p (h i) -> p h i", h=8),
                        in1=cbm[0:nt, g * 128: g * 128 + nt].unsqueeze(1).to_broadcast([nt, 8, nt]), op=ALU.mult),
                      reads=['dec%d' % i, 'cbm'], writes=['dec%d' % i])

                def ssd_stage2(g, b=b, cb_=cb_, dt_=dt_, exo=exo, w2=w2, a_hi=a_hi, a_lo=a_lo):
                    i = g % 2
                    hs = slice(8 * g, 8 * g + 8)

                    def yf(e):
                        e.matmul(bank(5)[0:nt, :], lhsT=identb[0:nt, 0:nt], rhs=xsd[i][0:nt, :], start=True, stop=False, skip_group_check=True)
                        ins = None
                        for hh in range(8):
                            ins = e.matmul(bank(5)[0:nt, hh * 64:(hh + 1) * 64], lhsT=dec[i][0:nt, hh * nt:(hh + 1) * nt],
                                           rhs=dxb[i][0:nt, hh * 64:(hh + 1) * 64], start=False, stop=(hh == 7), skip_group_check=True)
                        return ins
                    A('pe', yf, reads=['dec%d' % i, 'dx%d' % i, 'xsd%d' % i, 'mk'], writes=['ps5'])
                    A('pe', lambda e: e.matmul(bank(6)[0:nt, :], lhsT=R1[:, 20 + g, cb_], rhs=hbf[:, g * 512:(g + 1) * 512], start=True, stop=True),
                      reads=['R1_%d' % (20 + g), 'hbf%d' % g], writes=['ps6'])
                    A('pe', lambda e: e.matmul(bank(7)[:, :], lhsT=bm_tm[0:nt, g * 128:(g + 1) * 128], rhs=xwb[i][0:nt, :], start=True, stop=True),
                      reads=['bm_tm', 'xw%d' % i], writes=['ps7'])
                    A('dve', lambda e: e.tensor_tensor(out=tmpS[0:nt, :].rearrange("p (h d) -> p h d", h=8),
                                                       in0=bank(6)[0:nt, :].rearrange("p (h d) -> p h d", h=8),
                                                       in1=exo[0:nt, hs].unsqueeze(2).to_broadcast([nt, 8, 64]), op=ALU.mult),
                      reads=['ps6', 'sm5'], writes=['tmpS'])
                    A('dve', lambda e: e.tensor_tensor(out=yv[0:nt, :], in0=bank(5)[0:nt, :], in1=tmpS[0:nt, :], op=ALU.add),
                      reads=['ps5', 'tmpS'], writes=['yv'])
                    A('dve', lambda e: e.tensor_tensor(out=yv[0:nt, :], in0=yv[0:nt, :], in1=zs[0:nt, b, g * 512:(g + 1) * 512], op=ALU.mult),
                      reads=['yv', 'R2_%d' % (b * 4 + g)], writes=['yv'])
                    A('act', lambda e: e.activation(out=yg[0:nt, g * 512:(g + 1) * 512], in_=yv[0:nt, :], func=AF.Copy),
                      reads=['yv'], writes=['yg'])
                    A('act', lambda e: e.activation(out=tmpS[0:nt, :], in_=yv[0:nt, :], func=AF.Square, accum_out=ssy[0:nt, g:g + 1]),
                      reads=['yv', 'tmpS'], writes=['tmpS', 'ssy'])
                    hv = hst[:, g * 512:(g + 1) * 512]
                    A('dve', lambda e: e.tensor_tensor(out=hv.rearrange("p (h d) -> p h d", h=8), in0=hv.rearrange("p (h d) -> p h d", h=8),
                                                         in1=exo[:, 64:96][:, hs].unsqueeze(2).to_broadcast([128, 8, 64]), op=ALU.mult),
                      reads=['hst%d' % g, 'sm6'], writes=['hst%d' % g])
                    A('dve', lambda e: e.tensor_tensor(out=hv, in0=hv, in1=bank(7)[:, :], op=ALU.add), reads=['hst%d' % g, 'ps7'], writes=['hst%d' % g])
                    A('act', lambda e: e.activation(out=hbf[:, g * 512:(g + 1) * 512], in_=hv, func=AF.Copy),
                      reads=['hst%d' % g], writes=['hbf%d' % g])

                def ssd_post(b=b, cb_=cb_):
                    A('dve', lambda e: e.reduce_sum(out=ssy[0:nt, 4:5], in_=ssy[0:nt, 0:4], axis=AX.X), reads=['ssy'], writes=['ssy'])
                    A('act', lambda e: e.activation(out=ssy[0:nt, 5:6], in_=ssy[0:nt, 4:5], func=AF.Ln, scale=1.0 / 2048, bias=EPS), reads=['ssy'], writes=['ssy'])
                    A('act', lambda e: e.activation(out=ssy[0:nt, 6:7], in_=ssy[0:nt, 5:6], func=AF.Exp, scale=-0.5), reads=['ssy'], writes=['ssy'])
                    A('dve', lambda e: e.tensor_scalar(out=dg[0:nt, 0:nt], in0=identb[0:nt, 0:nt], scalar1=ssy[0:nt, 6:7], scalar2=None, op0=ALU.mult),
                      reads=['ssy', 'mk'], writes=['dg'])
                    for j in range(4):
                        bk_ = 5 + (j % 3)

                        def tryb(e, j=j, bk_=bk_):
                            ins = None
                            for c in range(4):
                                cc = 4 * j + c
                                ins = e.matmul(bank(bk_)[:, c * 128: c * 128 + nt], lhsT=yg[0:nt, cc * 128:(cc + 1) * 128], rhs=dg[0:nt, 0:nt],
                                               start=True, stop=True)
                            return ins
                        A('pe', tryb, reads=['yg', 'dg'], writes=['ps%d' % bk_])
                        A('dve', lambda e, j=j, bk_=bk_: e.tensor_tensor(out=ybT[:, 4 * j:4 * j + 4, cb_],
                                                                        in0=bank(bk_).rearrange("p (k t) -> p k t", k=4)[:, :, 0:nt],
                                                                        in1=C('gssm', 4 * j, 4).unsqueeze(2).to_broadcast([128, 4, nt]), op=ALU.mult),
                          reads=['ps%d' % bk_, 'cst'], writes=['R3_%d' % c for c in range(4 * j, 4 * j + 4)])

                blkf.append((ssd_pre, ssd_stage1, ssd_stage2, ssd_post))

            blkf[0][0]()
            blkf[0][1](0)
            blkf[0][1](1)
            for b in range(nb):
                pre_, s1_, s2_, post_ = blkf[b]
                s2_(0)
                s1_(2)
                s2_(1)
                s1_(3)
                s2_(2)
                s2_(3)
                if b + 1 < nb:
                    blkf[b + 1][0]()
                    blkf[b + 1][1](0)
                    blkf[b + 1][1](1)
                post_()

            if STOP <= 3:
                stopped[0] = True
                return
            if DBG:
                out_dma(d_yb[tl], R3[:, :], ['R3_%d' % k for k in range(16)], 'dbg')
            phase_barrier()
            tmpB = [scr_f(i * 2048, 512) for i in range(2)]
            for blk in range(4):
                slot, wr = w_get(w_in, 8224 + 512 * blk)
                for c in range(4):
                    cc = blk * 4 + c
                    pb = nextbank()
                    A('pe', lambda e, c=c, pb=pb, slot=slot: mm_fm(e, bank(pb)[:, 0:ncol], slot, c, hT, 8, ncol), reads=[wr, 'hT'], writes=['ps%d' % pb])
                    A('act', lambda e, cc=cc, pb=pb: e.activation(out=R1[:, cc, 0:ncol], in_=bank(pb)[:, 0:ncol], func=AF.Sigmoid, bias=C('bgate', cc)),
                      reads=['ps%d' % pb, 'cst'], writes=['R1_%d' % cc])
            for blk in range(2):
                slot, wr = w_get(w_pa, 512 * blk)
                for c in range(4):
                    cc = blk * 4 + c
                    pb = nextbank()
                    A('pe', lambda e, c=c, pb=pb, slot=slot: mm_fm(e, bank(pb)[:, 0:ncol], slot, c, yaT, 8, ncol), reads=[wr, 'yaT'], writes=['ps%d' % pb])
                    A('dve', lambda e, cc=cc, pb=pb: e.tensor_tensor(out=t1[:, cc, 0:ncol], in0=bank(pb)[:, 0:ncol], in1=R1[:, cc, 0:ncol], op=ALU.mult),
                      reads=['ps%d' % pb, 'R1_%d' % cc], writes=['R2_%d' % (2 * cc), 'R2_%d' % (2 * cc + 1)])
            for blk in range(2):
                slot0, wr0 = w_get(w_pb, 512 * blk)
                slot1, wr1 = w_get(w_pb, 512 * blk, prefetch=False)
                for c in range(4):
                    cc = blk * 4 + c
                    pb = nextbank()
                    i = cc % 2

                    def pbf(e, c=c, pb=pb, slot0=slot0, slot1=slot1):
                        mm_fm(e, bank(pb)[:, 0:ncol], slot0, c, ybT, 8, ncol, koff=0, start=True, stop=False)
                        return mm_fm(e, bank(pb)[:, 0:ncol], slot1, c, ybT, 8, ncol, koff=8, start=False, stop=True)
                    A('pe', pbf, reads=[wr0, wr1] + ['R3_%d' % k for k in range(16)], writes=['ps%d' % pb])
                    A('dve', lambda e, cc=cc, pb=pb, i=i: e.tensor_tensor(out=tmpB[i][:, 0:ncol], in0=bank(pb)[:, 0:ncol], in1=R1[:, 8 + cc, 0:ncol], op=ALU.mult),
                      reads=['ps%d' % pb, 'R1_%d' % (8 + cc)], writes=['scr%d' % i])
                    A('dve', lambda e, cc=cc, i=i: e.tensor_tensor(out=mT[:, cc, 0:ncol], in0=t1[:, cc, 0:ncol], in1=tmpB[i][:, 0:ncol], op=ALU.add),
                      reads=['scr%d' % i, 'R2_%d' % (2 * cc), 'R2_%d' % (2 * cc + 1)], writes=['yaT'])
            if DBG:
                out_dma(d_m[tl], yaT[:].rearrange("p a b -> p (a b)"), ['yaT'], 'dbg')
                out_dma(d_g[tl], R1[:, 0:16, :].rearrange("p a b -> p (a b)"), ['R1_%d' % k for k in range(16)], 'dbg')
            for half in range(2):
                slot, wr = w_get(w_out, 512 * half)
                for b in range(nb):
                    pb = nextbank()
                    A('pe', lambda e, b=b, pb=pb, slot=slot: mm_tm(e, bank(pb)[0:bs, :], slot, mT, b, bs, 8, 512), reads=[wr, 'yaT'], writes=['ps%d' % pb])
                    A('dve', lambda e, b=b, pb=pb, half=half: e.tensor_tensor(out=xt[0:bs, b, half * 512:(half + 1) * 512], in0=xt[0:bs, b, half * 512:(half + 1) * 512],
                                                                   in1=bank(pb)[0:bs, :], op=ALU.add),
                      reads=['ps%d' % pb, 'xt%d' % b], writes=['xt%d' % b])
                    if half == 1:
                        if b >= 2:
                            rms_B(b - 2, bs, 'gffn')
                        rms_A(b, bs, scr_b(0, 1024))
                if half == 1:
                    for b in range(max(0, nb - 2), nb):
                        rms_B(b, bs, 'gffn')

            if STOP <= 4:
                stopped[0] = True
                return
            if DBG:
                out_dma(d_x[tl], xt[:].rearrange("p a b -> p (a b)"), xres, 'dbg')
            if tidx + 1 < len(tiles):
                emit_xprefetch(tidx + 1)
            phase_barrier()
            junk = scr_b(0, 1024)
            ugr = [scr_f(2048 + i * 2064, 516) for i in range(2)]
            uvr = [scr_f(6176 + i * 2064, 516) for i in range(2)]
            cga = [scr_f(10304 + k * 2048, 512) for k in range(2)]
            cva = [scr_f(14400 + k * 2048, 512) for k in range(2)]
            gl = [scr_f(18496 + k * 2048, 512) for k in range(2)]
            pend = []
            for ub in range(6):
                slotg, wrg = w_get(w_up, 512 * ub)
                slotv, wrv = w_get(w_up, 3072 + 512 * ub, prefetch=False)
                for c in range(4):
                    def up_A(c=c, cg_=ub * 4 + c, slotg=slotg, wrg=wrg, slotv=slotv, wrv=wrv):
                        i = cg_ % 2
                        for (slot, wr, raw, rn, chn, acc, an, pb) in ((slotg, wrg, ugr[i], 'scr%d' % (1 + i), cg_, cga[i], 'scr%d' % (5 + i), 2 * i),
                                                                       (slotv, wrv, uvr[i], 'scr%d' % (3 + i), 24 + cg_, cva[i], 'scr%d' % (7 + i), 2 * i + 1)):
                            hr = 'uhalo%d' % chn
                            A('pe', lambda e, pb=pb, slot=slot: mm_fm(e, bank(pb)[:, 0:ncol], slot, c, hT, 8, ncol), reads=[wr, 'hT'], writes=['ps%d' % pb])
                            A('act', lambda e, raw=raw, pb=pb: e.activation(out=raw[:, 2:2 + ncol], in_=bank(pb)[:, 0:ncol], func=AF.Copy),
                              reads=['ps%d' % pb], writes=[rn])
                            rh_ = rn + 'h'
                            A('act', lambda e, chn=chn, acc=acc, pb=pb: e.activation(out=acc[:, 0:ncol], in_=bank(pb)[:, 0:ncol], func=AF.Identity,
                                                                                    scale=C('fw', chn * 3 + 2), bias=C('fb', chn)),
                              reads=['ps%d' % pb, 'cst'], writes=[an])
                            A('dve', lambda e, raw=raw, chn=chn: e.tensor_copy(out=raw[:, 0:2], in_=uhalo[:, chn, :]), reads=[hr], writes=[rh_])
                            for tap in range(0, 2):
                                A('dve', lambda e, raw=raw, chn=chn, acc=acc, tap=tap: e.scalar_tensor_tensor(
                                    out=acc[:, 0:ncol], in0=raw[:, tap:tap + ncol], scalar=C('fw', chn * 3 + tap), in1=acc[:, 0:ncol],
                                    op0=ALU.mult, op1=ALU.add), reads=[rn, rh_, an, 'cst'], writes=[an])
                            A('act', lambda e, raw=raw, chn=chn: e.activation(out=uhalo[:, chn, :], in_=raw[:, ncol:ncol + 2], func=AF.Copy),
                              reads=[rn], writes=[hr])

                        def up_B():
                            A('act', lambda e: e.activation(out=gl[i][:, 0:ncol], in_=cga[i][:, 0:ncol], func=AF.Gelu_apprx_tanh),
                              reads=['scr%d' % (5 + i)], writes=['scr%d' % (9 + i)])
                            A('dve', lambda e: e.tensor_tensor(out=R1[:, cg_, 0:ncol], in0=gl[i][:, 0:ncol], in1=cva[i][:, 0:ncol], op=ALU.mult),
                              reads=['scr%d' % (9 + i), 'scr%d' % (7 + i)], writes=['R1_%d' % cg_])
                        return up_B
                    pB = up_A()
                    if pend:
                        pend.pop(0)()
                    pend.append(pB)
            while pend:
                pend.pop(0)()
            for half in range(2):
                for kg in range(3):
                    slot, wr = w_get(w_down, 512 * half)
                    for b in range(nb):
                        A('pe', lambda e, b=b, slot=slot, kg=kg: mm_tm(e, bank(4 + b)[0:bs, :], slot, R1, b, bs, 8, 512, koff=8 * kg,
                                                                        start=(kg == 0), stop=(kg == 2)),
                          reads=[wr] + ['R1_%d' % k for k in range(8 * kg, 8 * kg + 8)], writes=['ps%d' % (4 + b)])
                for b in range(nb):
                    A('dve', lambda e, b=b, half=half: e.tensor_tensor(out=xt[0:bs, b, half * 512:(half + 1) * 512], in0=xt[0:bs, b, half * 512:(half + 1) * 512],
                                                                       in1=bank(4 + b)[0:bs, :], op=ALU.add),
                      reads=['ps%d' % (4 + b), 'xt%d' % b], writes=['xt%d' % b])
                    if half == 1:
                        if b >= 2:
                            rms_B(b - 2, bs, 'gple')
                        rms_A(b, bs, scr_b(0, 1024))
                if half == 1:
                    for b in range(max(0, nb - 2), nb):
                        rms_B(b, bs, 'gple')

            if STOP <= 5:
                stopped[0] = True
                return
            if DBG:
                out_dma(d_x2[tl], xt[:].rearrange("p a b -> p (a b)"), xres, 'dbg')
            phase_barrier()
            junk = scr_b(0, 1024)
            sgt = [scr_f(2048 + i * 2048, 512) for i in range(2)]
            for half in range(2):
                slotg, wrg = w_get(w_pg, 512 * half)
                slotp, wrp = w_get(w_ple, 512 * half, prefetch=False)
                for b in range(nb):
                    i = b % 2
                    bg, bp_ = 2 * i, 2 * i + 1
                    A('pe', lambda e, b=b, slotg=slotg, bg=bg: mm_tm(e, bank(bg)[0:bs, :], slotg, hT, b, bs, 8, 512), reads=[wrg, 'hT'], writes=['ps%d' % bg])
                    A('pe', lambda e, b=b, slotp=slotp, bp_=bp_: mm_tm(e, bank(bp_)[0:bs, :], slotp, pet, b, bs, 2, 512), reads=[wrp, 'pet'], writes=['ps%d' % bp_])
                    A('act', lambda e, i=i, bg=bg: e.activation(out=sgt[i][0:bs, :], in_=bank(bg)[0:bs, :], func=AF.Sigmoid), reads=['ps%d' % bg], writes=['scr%d' % (1 + i)])
                    A('dve', lambda e, i=i, bp_=bp_: e.tensor_tensor(out=sgt[i][0:bs, :], in0=sgt[i][0:bs, :], in1=bank(bp_)[0:bs, :], op=ALU.mult),
                      reads=['ps%d' % bp_, 'scr%d' % (1 + i)], writes=['scr%d' % (1 + i)])
                    A('dve', lambda e, b=b, i=i, half=half: e.tensor_tensor(out=xt[0:bs, b, half * 512:(half + 1) * 512],
                                                                            in0=xt[0:bs, b, half * 512:(half + 1) * 512], in1=sgt[i][0:bs, :], op=ALU.add),
                      reads=['scr%d' % (1 + i), 'xt%d' % b], writes=['xt%d' % b])
                    if half == 1 and not sample:
                        out_dma(ydst[b * 128:(b + 1) * 128, :], xt[:, b, :], ['xt%d' % b], 'yout%d' % b)
            if sample:
                out_dma(ydst, xt[0:32, 0, :], xres, 'yout0')

            if last:
                sd = ssms[0] if sample else ssmp[s]
                for q in range(4):
                    def trs(e, q=q):
                        ins = None
                        for j in range(4):
                            c = 4 * q + j
                            ins = e.transpose(out=bank(0)[:, j * 128:(j + 1) * 128], in_=hst[:, c * 128:(c + 1) * 128], identity=identf[:, :])
                        return ins
                    A('pe', trs, reads=['hst%d' % q, 'identf'], writes=['ps0'])
                    sg, sr = stage()
                    A('act', lambda e, sg=sg: e.activation(out=sg[:, :], in_=bank(0)[:, :], func=AF.Copy), reads=['ps0'], writes=[sr])
                    out_dma(sd[q * 512:(q + 1) * 512, :].rearrange("(j p) n -> p j n", p=128), sg[:, :].rearrange("p (j n) -> p j n", j=4), [sr], sr)
                cd = cssms[0] if sample else cssmp[s]
                for q in range(6):
                    def trc(e, q=q):
                        ins = None
                        for j in range(4):
                            ins = e.transpose(out=bank(1)[0:3, j * 128:(j + 1) * 128], in_=xhalo[:, 4 * q + j, :], identity=identf[:, :])
                        return ins
                    A('pe', trc, reads=['xhalo%d' % (4 * q + j) for j in range(4)] + ['identf'], writes=['ps1'])
                    sg, sr = stage()
                    A('act', lambda e, sg=sg: e.activation(out=sg[0:3, :], in_=bank(1)[0:3, :], func=AF.Copy), reads=['ps1'], writes=[sr])
                    out_dma(cd[:, q * 512:(q + 1) * 512], sg[0:3, :], [sr], sr)
                fd = cffns[0] if sample else cffnp[s]
                for q in range(12):
                    def trf(e, q=q):
                        ins = None
                        for j in range(4):
                            ins = e.transpose(out=bank(1)[0:2, j * 128:(j + 1) * 128], in_=uhalo[:, 4 * q + j, :], identity=identf[:, :])
                        return ins
                    A('pe', trf, reads=['uhalo%d' % (4 * q + j) for j in range(4)] + ['identf'], writes=['ps1'])
                    sg, sr = stage()
                    A('act', lambda e, sg=sg: e.activation(out=sg[0:2, :], in_=bank(1)[0:2, :], func=AF.Copy), reads=['ps1'], writes=[sr])
                    out_dma(fd[:, q * 512:(q + 1) * 512], sg[0:2, :], [sr], sr)

        try:
            chk(0.05)
            for s in range(NSEQ):
                for ti in range(NT):
                    do_tile('p', s, ti)
            if SAMPLE:
                do_tile('s', 0, 0)
        except StopBuild:
            pass
        A('sp', None, reads=list(out_res))
        S.emit(nc, st)
    return nc


def _consts(g_mix, g_ffn, g_ple, g_q, g_k, b_gate, conv_ssm_w, conv_ssm_b, ffn_conv_w, ffn_conv_b, dt_bias, a_log, d_skip, g_ssm, rel_bias):
    cst = np.zeros((128, NCST), np.float32)

    def fm(v, nch):
        return np.ascontiguousarray(np.asarray(v, np.float32).reshape(nch, 128).T)

    cst[:, _off['gmix']:_off['gmix'] + 8] = fm(g_mix, 8)
    cst[:, _off['gffn']:_off['gffn'] + 8] = fm(g_ffn, 8)
    cst[:, _off['gple']:_off['gple'] + 8] = fm(g_ple, 8)
    cst[:, _off['gq']] = np.tile(np.asarray(g_q, np.float32), 2)
    cst[:, _off['gk']] = np.tile(np.asarray(g_k, np.float32), 2)
    cst[:, _off['bgate']:_off['bgate'] + 16] = fm(b_gate, 16)
    cw = np.asarray(conv_ssm_w, np.float32)
    cst[:, _off['cw']:_off['cw'] + 96] = cw.T.reshape(24, 128, 4).transpose(1, 0, 2).reshape(128, 96)
    cst[:, _off['cb']:_off['cb'] + 24] = fm(conv_ssm_b, 24)
    fw = np.asarray(ffn_conv_w, np.float32)
    cst[:, _off['fw']:_off['fw'] + 144] = fw.T.reshape(48, 128, 3).transpose(1, 0, 2).reshape(128, 144)
    cst[:, _off['fb']:_off['fb'] + 48] = fm(ffn_conv_b, 48)
    cst[:, _off['dtb']:_off['dtb'] + 32] = np.asarray(dt_bias, np.float32)[None, :]
    cst[:, _off['alog']:_off['alog'] + 32] = np.asarray(a_log, np.float32)[None, :]
    cst[:, _off['dskip']:_off['dskip'] + 32] = np.asarray(d_skip, np.float32)[None, :]
    cst[:, _off['gssm']:_off['gssm'] + 16] = fm(g_ssm, 16)
    tab = np.asarray(rel_bias, np.float32)
    cst[:, _off['ch']:_off['ch'] + 16] = tab[:, 256][None, :]
    j = np.arange(128)[:, None]
    i = np.arange(128)[None, :]
    bx = np.zeros((128, 16, 2, 128), np.float32)
    for t, base in enumerate((128, 0)):
        rel = np.clip(base + i - j, -128, 128) + 128
        bx[:, :, t, :] = tab[:, rel].transpose(1, 0, 2)
    mk = np.zeros((128, 5, 128), np.float32)
    mk[:, 0, :] = np.eye(128)
    mk[:, 1, :] = (j <= i)
    mk[:, 2, :] = (j > i)
    mk[:, 3, :] = 1.0
    mk[:, 4, :] = ((j // 64) == (i // 64))
    return cst, bx.reshape(128, 16 * 256), mk.reshape(128, 5 * 128)


_NC_CACHE = {}


def _run(inputs, SEQ, n_cores, SAMPLE=True):
    f = lambda k: np.asarray(inputs[k], np.float32)
    cst, bx, mk = _consts(f('g_mix')[0], f('g_ffn')[0], f('g_ple')[0], f('g_q')[0], f('g_k')[0], f('b_gate')[0], f('conv_ssm_w')[0],
                          f('conv_ssm_b')[0], f('ffn_conv_w')[0], f('ffn_conv_b')[0], f('dt_bias')[0], f('a_log')[0], f('d_skip')[0],
                          f('g_ssm')[0], f('rel_bias')[0])
    key = (SEQ, SAMPLE)
    if key not in _NC_CACHE:
        _NC_CACHE[key] = build(SEQ, 2, SAMPLE)
    nc = _NC_CACHE[key]
    shared = dict(cst=cst, biasx=bx, mk=mk, w_in=np.ascontiguousarray(f('w_in')[0]), w_pa=np.ascontiguousarray(f('w_proj_a')[0]),
                  w_pb=np.ascontiguousarray(f('w_proj_b')[0]), w_out=np.ascontiguousarray(f('w_out')[0]), w_up=np.ascontiguousarray(f('w_up')[0]),
                  w_down=np.ascontiguousarray(f('w_down')[0]), w_pg=np.ascontiguousarray(f('w_ple_gate')[0]), w_ple=np.ascontiguousarray(f('w_ple')[0]))
    xpr, ppr = f('x_prompt'), f('p_prompt')[0]
    in_maps = []
    for c in range(n_cores):
        m = dict(shared)
        m['xp'] = np.ascontiguousarray(xpr[2 * c:2 * c + 2])
        m['ppT'] = np.ascontiguousarray(ppr[2 * c:2 * c + 2].transpose(0, 2, 1))
        m['xsm'] = np.ascontiguousarray(f('x_sample')[c])
        m['psT'] = np.ascontiguousarray(f('p_sample')[0, c].T)
        ck = f('cache_k')[0, c]
        m['ckT'] = np.ascontiguousarray(ck.transpose(1, 2, 0).reshape(8, 128, 512).transpose(1, 0, 2))
        m['cv'] = np.ascontiguousarray(f('cache_v')[0, c].reshape(512, 1024))
        m['st_ssmT'] = np.ascontiguousarray(f('state_ssm')[0, c].reshape(2048, 128).T)
        m['st_convT'] = np.ascontiguousarray(f('state_conv_ssm')[0, c].T.reshape(24, 128, 3).transpose(1, 0, 2))
        m['st_ffnT'] = np.ascontiguousarray(f('state_conv_ffn')[0, c].T.reshape(48, 128, 2).transpose(1, 0, 2))
        in_maps.append(m)
    res = run_bass_kernel_spmd(nc, in_maps, core_ids=list(range(n_cores)))
    R = res.results
    LK = min(512, SEQ)
    cat = lambda k: np.concatenate([np.asarray(r[k]) for r in R], axis=0)
    stk = lambda k: np.stack([np.asarray(r[k]) for r in R], axis=0)
    yp = cat('yp')
    ys = stk('ysm')
    kp = cat('kp').reshape(-1, LK, 16, 64)[None]
    vp = cat('vp').reshape(-1, LK, 16, 64)[None]
    ssmp = cat('ssmp').reshape(-1, 32, 64, 128)[None]
    cssmp = cat('cssmp')[None]
    cffnp = cat('cffnp')[None]
    ks = stk('ksm').reshape(-1, 32, 16, 64)[None]
    vs = stk('vsm').reshape(-1, 32, 16, 64)[None]
    ssms = cat('ssms').reshape(-1, 32, 64, 128)[None]
    cssms = cat('cssms')[None]
    cffns = cat('cffns')[None]
    return tuple(np.ascontiguousarray(a, dtype=np.float32) for a in (yp, ys, kp, vp, ssmp, cssmp, cffnp, ks, vs, ssms, cssms, cffns))


def kernel(**inputs):
    SEQ = int(np.asarray(inputs['x_prompt']).shape[1])
    n_cores = int(np.asarray(inputs['x_prompt']).shape[0]) // 2
    return _run(inputs, SEQ, n_cores)
```

```python
import numpy as np
from contextlib import ExitStack
import concourse.bass as bass
import concourse.mybir as mybir
from concourse.bass_utils import run_bass_kernel_spmd

F32 = mybir.dt.float32
BF16 = mybir.dt.bfloat16
AF = mybir.ActivationFunctionType
ALU = mybir.AluOpType
AX = mybir.AxisListType
ENGS = ('pe', 'act', 'dve', 'pool', 'sp')
RHS_ENG = 'pool'
POOL_CONV = True
WSCRATCH = True
WLOAD_ENG = 'sp'
EPS = 1e-6
D = 1024
NIN = 10272
NSLOT = 3


class StopBuild(Exception):
    pass


class Sched:
    def __init__(self):
        self.ops = []
        self.last_w = {}
        self.readers = {}
        self.eng_count = {e: 0 for e in ENGS}
        self.dma_count = {}
        self.last_dma = {}

    def add(self, eng, fn, reads=(), writes=(), dma=None):
        writes = list(writes) + [r for r in reads if r.startswith('ps')]
        reads = [r for r in reads if not r.startswith('ps')]
        idx = len(self.ops)
        deps = set()
        for r in reads:
            w = self.last_w.get(r)
            if w is not None:
                deps.add(w)
        for w_ in writes:
            w = self.last_w.get(w_)
            if w is not None:
                deps.add(w)
            deps.update(self.readers.get(w_, ()))
        if fn is None:
            assert not writes
            sig = None
        elif dma is None:
            self.eng_count[eng] += 1
            sig = ('e:' + eng, self.eng_count[eng])
        else:
            self.dma_count[dma] = self.dma_count.get(dma, 0) + 1
            sig = ('d:' + dma, 16 * self.dma_count[dma])
            prev = self.last_dma.get(dma)
            if prev is not None:
                deps.add(prev)
            self.last_dma[dma] = idx
        deps.discard(idx)
        self.ops.append(dict(eng=eng, fn=fn, deps=sorted(deps), sig=sig, dma=dma))
        for r in reads:
            self.readers.setdefault(r, []).append(idx)
        for w_ in writes:
            self.last_w[w_] = idx
            self.readers[w_] = []
        return idx

    def emit(self, nc, stack):
        names = ['e:' + e for e in ENGS] + ['d:' + k for k in self.dma_count]
        sems = {}
        for n in names:
            sems[n] = stack.enter_context(nc.semaphore(n.replace(':', '_')))
        know = {e: {} for e in ENGS}
        opknow = [None] * len(self.ops)
        waits = [None] * len(self.ops)
        for i, op in enumerate(self.ops):
            e = op['eng']
            k = know[e]
            m = {}
            for d in op['deps']:
                dop = self.ops[d]
                sn, sv = dop['sig']
                if k.get(sn, 0) >= sv:
                    continue
                if sn == 'e:pe' and e == 'pe':
                    continue
                m[sn] = max(m.get(sn, 0), sv)
                for a, b in opknow[d].items():
                    if k.get(a, 0) < b:
                        k[a] = b
                if k.get(sn, 0) < sv:
                    k[sn] = sv
            waits[i] = list(m.items())
            opknow[i] = dict(k)
        block = stack.enter_context(nc.Block())
        per_eng = {e: [i for i, op in enumerate(self.ops) if op['eng'] == e] for e in ENGS}

        def make(e):
            def body(engine):
                for i in per_eng[e]:
                    op = self.ops[i]
                    for sn, sv in waits[i]:
                        engine.wait_ge(sems[sn], sv)
                    if op['fn'] is None:
                        continue
                    ins = op['fn'](engine)
                    sn, sv = op['sig']
                    ins.then_inc(sems[sn], 16 if op['dma'] is not None else 1)
            return body

        block.tensor(make('pe'))
        block.scalar(make('act'))
        block.vector(make('dve'))
        block.gpsimd(make('pool'))
        block.sync(make('sp'))


_off = {}
_n = 0
for _name, _w in [('gmix', 8), ('gffn', 8), ('gple', 8), ('gq', 1), ('gk', 1), ('bgate', 16), ('cw', 96), ('cb', 24),
                  ('fw', 144), ('fb', 48), ('dtb', 32), ('alog', 32), ('dskip', 32), ('gssm', 16), ('ch', 16)]:
    _off[_name] = _n
    _n += _w
NCST = _n


def build(SEQ, NSEQ=2, SAMPLE=True, DBG=False, STOP=99):
    NT = SEQ // 512
    LK = min(512, SEQ)
    nc = bass.Bass("TRN2", target_bir_lowering=False)
    S = Sched()
    A = S.add

    def din(name, shape):
        return nc.dram_tensor(name, list(shape), F32, kind="ExternalInput").ap()

    def dout(name, shape):
        return nc.dram_tensor(name, list(shape), F32, kind="ExternalOutput").ap()

    xp = din("xp", [NSEQ, SEQ, D])
    ppT = din("ppT", [NSEQ, 256, SEQ])
    xsm = din("xsm", [32, D])
    psT = din("psT", [256, 32])
    ckT = din("ckT", [128, 8, 512])
    cv = din("cv", [512, D])
    st_ssmT = din("st_ssmT", [128, 2048])
    st_convT = din("st_convT", [128, 24, 3])
    st_ffnT = din("st_ffnT", [128, 48, 2])
    cst_d = din("cst", [128, NCST])
    biasx = din("biasx", [128, 16 * 256])
    mk_d = din("mk", [128, 5 * 128])
    w_in = din("w_in", [D, NIN])
    w_pa = din("w_pa", [D, D])
    w_pb = din("w_pb", [2048, D])
    w_out = din("w_out", [D, D])
    w_up = din("w_up", [D, 6144])
    w_down = din("w_down", [3072, D])
    w_pg = din("w_pg", [D, D])
    w_ple = din("w_ple", [256, D])

    yp = dout("yp", [NSEQ, SEQ, D])
    kp = dout("kp", [NSEQ, LK, D])
    vp = dout("vp", [NSEQ, LK, D])
    ssmp = dout("ssmp", [NSEQ, 2048, 128])
    cssmp = dout("cssmp", [NSEQ, 3, 3072])
    cffnp = dout("cffnp", [NSEQ, 2, 6144])
    ysm = dout("ysm", [32, D])
    ksm = dout("ksm", [32, D])
    vsm = dout("vsm", [32, D])
    ssms = dout("ssms", [1, 2048, 128])
    cssms = dout("cssms", [1, 3, 3072])
    cffns = dout("cffns", [1, 2, 6144])
    out_res = []
    if DBG:
        ntl = NSEQ * (SEQ // 512) + 1
        d_ya = nc.dram_tensor("d_ya", [ntl, 128, 8 * 512], BF16, kind="ExternalOutput").ap()
        d_yb = nc.dram_tensor("d_yb", [ntl, 128, 16 * 512], BF16, kind="ExternalOutput").ap()
        d_x = nc.dram_tensor("d_x", [ntl, 128, 4 * 1024], F32, kind="ExternalOutput").ap()
        d_x2 = nc.dram_tensor("d_x2", [ntl, 128, 4 * 1024], F32, kind="ExternalOutput").ap()
        d_m = nc.dram_tensor("d_m", [ntl, 128, 8 * 512], BF16, kind="ExternalOutput").ap()
        d_g = nc.dram_tensor("d_g", [ntl, 128, 16 * 512], BF16, kind="ExternalOutput").ap()
    tcount = [0]

    with ExitStack() as st:
        def sb(name, shape, dt):
            return st.enter_context(nc.sbuf_tensor(name, list(shape), dt))

        xt = sb("xt", [128, 4, D], F32)
        hT = sb("hT", [128, 8, 512], BF16)
        R3 = sb("R3", [128, 8192], BF16)
        kTb = [sb("kTb%d" % i, [128, 8, 512], BF16) for i in range(2)]
        vb = [sb("vb%d" % i, [128, 4, D], BF16) for i in range(2)]
        yaT = sb("yaT", [128, 8, 512], BF16)
        R2 = sb("R2", [128, 8192], BF16)
        R1 = sb("R1", [128, 24, 512], BF16)
        SCR = sb("SCR", [128, 6656], F32)
        hst = sb("hst", [128, 2048], F32)
        hbf = sb("hbf", [128, 2048], BF16)
        Et = sb("Et", [128, 16, 256], BF16)
        wsl = [sb("wsl%d" % i, [128, 8, 512], BF16) for i in range(NSLOT)]
        xhalo = sb("xhalo", [128, 24, 3], F32)
        uhalo = sb("uhalo", [128, 48, 2], F32)
        cst = sb("cst_sb", [128, NCST], F32)
        mk = sb("mk_sb", [128, 5, 128], BF16)
        identf = sb("identf", [128, 128], F32)
        aneg = sb("aneg", [128, 32], F32)
        xn = sb("xn", [128, D], BF16)
        stg = sb("stg", [128, 2, 512], F32)
        pet = sb("pet", [128, 2, 512], BF16)
        dtr = sb("dtr", [128, 4, 32], F32)
        ssb = sb("ssb", [128, 16], F32)
        PS = st.enter_context(nc.psum_tensor("PS", [128, 4096], F32))

        def bank(i):
            return PS[:, i * 512:(i + 1) * 512]

        def bankb(i):
            return PS[:, i * 512:(i + 1) * 512].bitcast(BF16)

        identb = mk[:, 0, :]
        Vm = mk[:, 1, :]
        Um = mk[:, 2, :]
        onesb = mk[:, 3, :]
        blk1 = mk[:, 4, :]

        def C(name, i=0, n=1):
            o = _off[name] + i
            return cst[:, o:o + n]

        def scr_f(off_b, ncols):
            return SCR[:, off_b // 4: off_b // 4 + ncols]

        def scr_b(off_b, ncols):
            return SCR[:, off_b // 4: off_b // 4 + ncols // 2].bitcast(BF16)

        qT = R3[:, 0:4096].rearrange("p (c t) -> p c t", c=8)
        ybT = R3[:, :].rearrange("p (c t) -> p c t", c=16)
        pT = [R3[:, 4096 + i * 1024: 4096 + i * 1024 + 640] for i in range(3)]
        rden = [R3[:, 7168 + i * 512: 7168 + i * 512 + 256].bitcast(F32) for i in range(2)]
        zs = R2[:, :].rearrange("p (b c) -> p b c", b=4)
        t1 = R2[:, :].bitcast(F32).rearrange("p (c t) -> p c t", c=8)
        tmpE = R2[:, :].bitcast(F32).rearrange("p (h t) -> p h t", h=16)
        xin = R2[:, :].bitcast(F32).rearrange("p (b d) -> p b d", b=4)
        mT = yaT

        def r3q(hp):
            return 'R3_%d' % hp

        def r3pt(i):
            return ['R3_%d' % (8 + 2 * i), 'R3_%d' % (9 + 2 * i)]

        def r3rd(i):
            return ['R3_%d' % (14 + i)]

        SCRN = ['scr%d' % i for i in range(12)]

        dummy = sb("dummy_sb", [128, 8], F32)
        rhs_x = sb("rhs_x", [128, 2048], BF16)
        dg = sb("dg", [128, 128], BF16)

        def phase_barrier():
            A('dve', lambda e: e.memset(dummy[0:1, 0:1], 0.0), writes=SCRN)

        wplan = []

        def tile_plan():
            p = []
            for i in range(16):
                p.append((w_in, 0, 8, 512 * i, 512))
            p.append((w_in, 0, 8, 8192, 32))
            for i in range(4):
                p.append((w_in, 0, 8, 8224 + 512 * i, 512))
            for i in range(2):
                p.append((w_pa, 0, 8, 512 * i, 512))
            for i in range(2):
                for kg in range(2):
                    p.append((w_pb, kg * 1024, 8, 512 * i, 512))
            for i in range(2):
                p.append((w_out, 0, 8, 512 * i, 512))
            for i in range(6):
                p.append((w_up, 0, 8, 512 * i, 512))
                p.append((w_up, 0, 8, 3072 + 512 * i, 512))
            for i in range(2):
                for kg in range(3):
                    p.append((w_down, kg * 1024, 8, 512 * i, 512))
            for i in range(2):
                p.append((w_pg, 0, 8, 512 * i, 512))
                p.append((w_ple, 0, 2, 512 * i, 512))
            return p

        ntiles = NSEQ * NT + (1 if SAMPLE else 0)
        for _ in range(ntiles):
            wplan.extend(tile_plan())
        wstate = dict(issued=0, used=0)

        NPLAN = len(tile_plan())
        wscr = nc.dram_tensor("wscr", [NPLAN, 128, 4096], BF16, kind="Internal").ap() if WSCRATCH else None

        def w_issue(upto):
            while wstate['issued'] < min(upto, len(wplan)):
                i = wstate['issued']
                wap, r0, nk, c0, ncol = wplan[i]
                s = i % NSLOT
                j = i % NPLAN
                if WSCRATCH and i >= NPLAN:
                    src = wscr[j, :, 0:nk * ncol].rearrange("p (k n) -> p k n", k=nk)
                    A(WLOAD_ENG, lambda e, s=s, nk=nk, ncol=ncol, src=src: e.dma_start(out=wsl[s][:, 0:nk, 0:ncol], in_=src),
                      reads=['wscr%d' % j], writes=['w%d' % s], dma=('wh%d' if WLOAD_ENG == 'sp' else 'w%d') % s)
                else:
                    src = wap[r0:r0 + nk * 128, c0:c0 + ncol].rearrange("(k p) n -> p k n", p=128)
                    A('pool', lambda e, s=s, nk=nk, ncol=ncol, src=src: e.dma_start(out=wsl[s][:, 0:nk, 0:ncol], in_=src),
                      writes=['w%d' % s], dma='w%d' % s)
                    if WSCRATCH and ntiles > 1:
                        dst = wscr[j, :, 0:nk * ncol].rearrange("p (k n) -> p k n", k=nk)
                        A('sp', lambda e, s=s, nk=nk, ncol=ncol, dst=dst: e.dma_start(out=dst, in_=wsl[s][:, 0:nk, 0:ncol]),
                          reads=['w%d' % s], writes=['wscr%d' % j], dma='ws%d' % s)
                wstate['issued'] += 1

        def w_get(wap, c0, prefetch=True):
            i = wstate['used']
            assert wplan[i][0] is wap and wplan[i][3] == c0, (i, c0)
            w_issue(i + NSLOT if prefetch else i + 1)
            wstate['used'] += 1
            s = i % NSLOT
            return wsl[s], 'w%d' % s

        A('sp', lambda e: e.dma_start(out=cst[:], in_=cst_d), writes=['cst'], dma='c0')
        A('pool', lambda e: e.dma_start(out=mk[:].rearrange("p a b -> p (a b)"), in_=mk_d), writes=['mk'], dma='c1')
        A('sp', lambda e: e.dma_start(out=identf[:], in_=mk_d[:, 0:128]), writes=['identf'], dma='c2')
        A('sp', lambda e: e.dma_start(out=tmpE.rearrange("p h t -> p (h t)"), in_=biasx), writes=['tmpE'], dma='c3')
        A('dve', lambda e: e.tensor_tensor(out=tmpE, in0=tmpE, in1=C('ch', 0, 16).unsqueeze(2).to_broadcast([128, 16, 256]),
                                           op=ALU.subtract), reads=['cst', 'tmpE'], writes=['tmpE'])
        A('act', lambda e: e.activation(out=Et[:], in_=tmpE, func=AF.Exp), reads=['tmpE'], writes=['Et'])
        A('dve', lambda e: e.memset(Et[64:128, :, 128:192], 0.0), writes=['Et'])
        A('act', lambda e: e.activation(out=aneg[:], in_=C('alog', 0, 32), func=AF.Exp), reads=['cst'], writes=['aneg'])
        A('dve', lambda e: e.tensor_scalar(out=aneg[:], in0=aneg[:], scalar1=-1.0, scalar2=None, op0=ALU.mult),
          reads=['aneg'], writes=['aneg'])
        A('dve', lambda e: e.memset(dummy[0:1, 1:2], 0.0), reads=['tmpE', 'Et'], writes=['R2_%d' % i for i in range(16)])

        xnb = [xn, scr_b(24576, 1024)]

        def rms_A(b, bs, junk, from_xin=False):
            k = b % 2
            xb = (xin if from_xin else xt)[0:bs, b, :]
            xr = 'xt%d' % b
            sr_ = 'ssb%d' % k
            sc_ = ssb[:, 4 * k:4 * k + 4]
            xrs = ['R2_%d' % (4 * b + j_) for j_ in range(4)] if from_xin else [xr]
            A('act', lambda e: e.activation(out=junk[0:bs, :], in_=xb, func=AF.Square, accum_out=sc_[0:bs, 0:1]),
              reads=xrs, writes=['scr0', sr_])
            A('act', lambda e: e.activation(out=sc_[0:bs, 1:2], in_=sc_[0:bs, 0:1], func=AF.Sqrt, scale=1.0 / D, bias=EPS),
              reads=[sr_], writes=[sr_])
            A('dve', lambda e: e.reciprocal(out=sc_[0:bs, 2:3], in_=sc_[0:bs, 1:2]), reads=[sr_], writes=[sr_])
            A('dve', lambda e: e.tensor_scalar(out=xnb[k][0:bs, :], in0=xb, scalar1=sc_[0:bs, 2:3], scalar2=None, op0=ALU.mult),
              reads=xrs + [sr_], writes=['xn%d' % k])

        def rms_B(b, bs, gname):
            k = b % 2
            pb = 3

            def tr(e):
                ins = None
                for kc in range(8):
                    ins = e.transpose(out=bankb(pb)[:, kc * 128: kc * 128 + bs], in_=xnb[k][0:bs, kc * 128:(kc + 1) * 128],
                                      identity=identb[0:bs, 0:bs])
                return ins
            A('pe', tr, reads=['xn%d' % k, 'mk'], writes=['ps%d' % pb])
            A('dve', lambda e: e.tensor_tensor(
                out=hT[:, :, b * 128: b * 128 + bs],
                in0=bankb(pb).rearrange("p (k t) -> p k t", k=8)[:, :, 0:bs],
                in1=C(gname, 0, 8).unsqueeze(2).to_broadcast([128, 8, bs]), op=ALU.mult),
              reads=['ps%d' % pb, 'cst'], writes=['hT'])

        def rmsnorm_T(nb, bs, gname, junk, from_xin=False):
            for b in range(nb):
                rms_A(b, bs, junk, from_xin)
                if b >= 1:
                    rms_B(b - 1, bs, gname)
            rms_B(nb - 1, bs, gname)

        def mm_fm(e, out, slot, c, srcT, nk, ncol, koff=0, start=True, stop=True):
            ins = None
            for kc in range(nk):
                ins = e.matmul(out, lhsT=slot[:, kc, c * 128:(c + 1) * 128], rhs=srcT[:, koff + kc, 0:ncol],
                               start=(start and kc == 0), stop=(stop and kc == nk - 1))
            return ins

        def mm_tm(e, out, slot, srcT, b, bs, nk, ncols, koff=0, start=True, stop=True):
            ins = None
            for kc in range(nk):
                ins = e.matmul(out, lhsT=srcT[:, koff + kc, b * 128: b * 128 + bs], rhs=slot[:, kc, 0:ncols],
                               start=(start and kc == 0), stop=(stop and kc == nk - 1))
            return ins

        def out_dma(dst, src, reads, key):
            rn = 'out%d' % len(out_res)
            out_res.append(rn)
            A('sp', lambda e: e.dma_start(out=dst, in_=src), reads=reads, writes=[rn], dma=key)

        stgc = [0]

        def stage():
            i = stgc[0] % 2
            stgc[0] += 1
            return stg[:, i, :], 'stg%d' % i

        stopped = [False]

        def chk(x):
            if STOP <= x:
                raise StopBuild()

        tiles = [('p', s_, ti_) for s_ in range(NSEQ) for ti_ in range(NT)] + ([('s', 0, 0)] if SAMPLE else [])

        def emit_xprefetch(tidx):
            kind_, s_, ti_ = tiles[tidx]
            if kind_ == 's':
                A('sp', lambda e: e.dma_start(out=xin[0:32, 0, :], in_=xsm), writes=['R2_%d' % j_ for j_ in range(4)], dma='xin0')
            else:
                for b_ in range(4):
                    src = xp[s_, ti_ * 512 + b_ * 128: ti_ * 512 + (b_ + 1) * 128, :]
                    A('sp', lambda e, b_=b_, src=src: e.dma_start(out=xin[:, b_, :], in_=src),
                      writes=['R2_%d' % (4 * b_ + j_) for j_ in range(4)], dma='xin%d' % b_)

        def do_tile(kind, s, ti):
            if stopped[0]:
                return
            tidx = tiles.index((kind, s, ti))
            sample = (kind == 's')
            bs = 32 if sample else 128
            nb = 1 if sample else 4
            ncol = nb * bs
            first = sample or ti == 0
            last = sample or ti == NT - 1
            t0 = 0 if sample else ti * 512
            cur = 0 if sample else ti % 2
            prv = 1 - cur
            xsrc = xsm if sample else xp[s, t0:t0 + ncol, :]
            ydst = ysm if sample else yp[s, t0:t0 + ncol, :]
            k_dst = ksm if sample else kp[s]
            v_dst = vsm if sample else vp[s]
            kv_out = sample or (SEQ - (t0 + ncol) < LK)
            kv_row0 = 0 if sample else t0 - (SEQ - LK)
            xres = ['xt%d' % b for b in range(nb)]

            def cols(b):
                return slice(b * 128, b * 128 + bs)

            if first:
                if sample:
                    A('sp', lambda e: e.dma_start(out=hst[:], in_=st_ssmT), writes=['hst%d' % g for g in range(4)], dma='c0')
                    A('act', lambda e: e.activation(out=hbf[:], in_=hst[:], func=AF.Copy),
                      reads=['hst%d' % g for g in range(4)], writes=['hbf%d' % g for g in range(4)])
                    A('sp', lambda e: e.dma_start(out=xhalo[:], in_=st_convT), writes=['xhalo%d' % k for k in range(24)], dma='c2')
                    A('sp', lambda e: e.dma_start(out=uhalo[:], in_=st_ffnT), writes=['uhalo%d' % k for k in range(48)], dma='c3')
                    A('pool', lambda e: e.dma_start(out=kTb[1][:], in_=ckT), writes=['kT1_%d' % h for h in range(8)], dma='c1')
                    A('pool', lambda e: e.dma_start(out=vb[1][:], in_=cv.rearrange("(b p) d -> p b d", p=128)),
                      writes=['v1_%d' % b for b in range(4)], dma='c4')
                else:
                    A('dve', lambda e: e.memset(hst[:], 0.0), writes=['hst%d' % g for g in range(4)])
                    A('dve', lambda e: e.memset(hbf[:], 0.0), writes=['hbf%d' % g for g in range(4)])
                    A('dve', lambda e: e.memset(xhalo[:], 0.0), writes=['xhalo%d' % k for k in range(24)])
                    A('dve', lambda e: e.memset(uhalo[:], 0.0), writes=['uhalo%d' % k for k in range(48)])

            chk(0.1)
            if tidx == 0:
                emit_xprefetch(0)
            if sample:
                A('pool', lambda e: e.dma_start(out=pet[:, :, 0:32], in_=psT.rearrange("(c p) t -> p c t", p=128)),
                  writes=['pet'], dma='pet')
            else:
                A('pool', lambda e: e.dma_start(out=pet[:], in_=ppT[s, :, t0:t0 + 512].rearrange("(c p) t -> p c t", p=128)),
                  writes=['pet'], dma='pet')

            chk(0.2)
            phase_barrier()
            junk = scr_b(0, 1024)
            sqb = [scr_b(2048 + i * 1024, 512) for i in range(2)]
            qraw = [scr_f(4096 + i * 2048, 512) for i in range(2)]
            rs = [scr_f(8192 + i * 2048, 512) for i in range(2)]
            knf = scr_f(12288, 512)
            xraw = [scr_f(14336 + i * 2064, 516) for i in range(2)]
            cacc = [scr_f(18464 + i * 2048, 512) for i in range(2)]
            rmsnorm_T(nb, bs, 'gmix', junk, from_xin=True)

            chk(0.3)
            pcount = [0]

            def nextbank():
                i = pcount[0] % 2
                pcount[0] += 1
                return i

            def qk_chunk(slot, wr, c, hp, is_k):
                pb = nextbank()
                i = hp % 2
                A('pe', lambda e: mm_fm(e, bank(pb)[:, 0:ncol], slot, c, hT, 8, ncol), reads=[wr, 'hT'], writes=['ps%d' % pb])
                A('act', lambda e: e.activation(out=sqb[i][:, 0:ncol], in_=bank(pb)[:, 0:ncol], func=AF.Square),
                  reads=['ps%d' % pb], writes=['scr%d' % (1 + i)])
                gcol = C('gk') if is_k else C('gq')
                A('dve', lambda e: e.tensor_scalar(out=qraw[i][:, 0:ncol], in0=bank(pb)[:, 0:ncol], scalar1=gcol, scalar2=None, op0=ALU.mult),
                  reads=['ps%d' % pb, 'cst'], writes=['scr%d' % (3 + i)])
                return lambda: qk_chunk_B(pb, i, hp, is_k)

            def qk_chunk_B(pb, i, hp, is_k):
                A('pe', lambda e: e.matmul(bank(2)[:, 0:ncol], lhsT=blk1, rhs=sqb[i][:, 0:ncol], start=True, stop=True),
                  reads=['scr%d' % (1 + i), 'mk'], writes=['ps2'])
                A('act', lambda e: e.activation(out=rs[i][:, 0:ncol], in_=bank(2)[:, 0:ncol], func=AF.Ln, scale=1.0 / 64, bias=EPS),
                  reads=['ps2'], writes=['scr%d' % (5 + i)])
                A('act', lambda e: e.activation(out=rs[i][:, 0:ncol], in_=rs[i][:, 0:ncol], func=AF.Exp, scale=-0.5),
                  reads=['scr%d' % (5 + i)], writes=['scr%d' % (5 + i)])
                if not is_k:
                    A('dve', lambda e: e.tensor_tensor(out=qT[:, hp, 0:ncol], in0=qraw[i][:, 0:ncol], in1=rs[i][:, 0:ncol], op=ALU.mult),
                      reads=['scr%d' % (3 + i), 'scr%d' % (5 + i)], writes=[r3q(hp)])
                else:
                    A('dve', lambda e: e.tensor_tensor(out=kTb[cur][:, hp, 0:ncol], in0=qraw[i][:, 0:ncol], in1=rs[i][:, 0:ncol], op=ALU.mult),
                      reads=['scr%d' % (3 + i), 'scr%d' % (5 + i)], writes=['kT%d_%d' % (cur, hp)])
                    if kv_out:
                        A('dve', lambda e: e.tensor_tensor(out=knf[:, 0:ncol], in0=qraw[i][:, 0:ncol], in1=rs[i][:, 0:ncol], op=ALU.mult),
                          reads=['scr%d' % (3 + i), 'scr%d' % (5 + i)], writes=['scr7'])
                        sg, sr = stage()

                        def trk(e):
                            ins = None
                            for b in range(nb):
                                ins = e.transpose(out=bank(4)[0:bs, b * 128:(b + 1) * 128], in_=knf[:, cols(b)], identity=identf[:, :])
                            return ins
                        A('pe', trk, reads=['scr7', 'identf'], writes=['ps4'])
                        A('act', lambda e: e.activation(out=sg[0:bs, 0:nb * 128], in_=bank(4)[0:bs, 0:nb * 128], func=AF.Copy),
                          reads=['ps4'], writes=[sr])
                        dst = k_dst[kv_row0:kv_row0 + ncol, hp * 128:(hp + 1) * 128].rearrange("(b p) d -> p b d", p=bs)
                        out_dma(dst, sg[0:bs, 0:nb * 128].rearrange("p (b d) -> p b d", b=nb), [sr], sr)

            pend = []
            for is_k_ in (False, True):
                for blk in range(2):
                    slot, wr = w_get(w_in, (1024 if is_k_ else 0) + 512 * blk)
                    for c in range(4):
                        pB = qk_chunk(slot, wr, c, blk * 4 + c, is_k_)
                        if pend:
                            pend.pop(0)()
                        pend.append(pB)
            while pend:
                pend.pop(0)()
            chk(0.5)
            for blk in range(2):
                slot, wr = w_get(w_in, 2048 + 512 * blk)
                for b in range(nb):
                    pb = nextbank()
                    A('pe', lambda e, b=b, pb=pb, slot=slot: mm_tm(e, bank(pb)[0:bs, :], slot, hT, b, bs, 8, 512), reads=[wr, 'hT'], writes=['ps%d' % pb])
                    A('act', lambda e, b=b, pb=pb, blk=blk: e.activation(out=vb[cur][0:bs, b, blk * 512:(blk + 1) * 512], in_=bank(pb)[0:bs, :], func=AF.Copy),
                      reads=['ps%d' % pb], writes=['v%d_%d' % (cur, b)])
                    if kv_out:
                        sg, sr = stage()
                        A('dve', lambda e, pb=pb, sg=sg: e.tensor_copy(out=sg[0:bs, :], in_=bank(pb)[0:bs, :]), reads=['ps%d' % pb], writes=[sr])
                        r0 = kv_row0 + b * 128
                        out_dma(v_dst[r0:r0 + bs, blk * 512:(blk + 1) * 512], sg[0:bs, :], [sr], sr)
            chk(0.6)
            for b_ in range(nb):
                A('act', lambda e, b_=b_: e.activation(out=xt[0:bs, b_, :], in_=xin[0:bs, b_, :], func=AF.Copy),
                  reads=['R2_%d' % (4 * b_ + j_) for j_ in range(4)], writes=['xt%d' % b_])
            for blk in range(4):
                slot, wr = w_get(w_in, 3072 + 512 * blk)
                for b in range(nb):
                    pb = nextbank()
                    A('pe', lambda e, b=b, pb=pb, slot=slot: mm_tm(e, bank(pb)[0:bs, :], slot, hT, b, bs, 8, 512), reads=[wr, 'hT'], writes=['ps%d' % pb])
                    A('act', lambda e, b=b, pb=pb, blk=blk: e.activation(out=zs[0:bs, b, blk * 512:(blk + 1) * 512], in_=bank(pb)[0:bs, :], func=AF.Silu),
                      reads=['ps%d' % pb], writes=['R2_%d' % (b * 4 + blk)])
            chk(0.7)
            pend = []
            for blk in range(6):
                slot, wr = w_get(w_in, 5120 + 512 * blk)
                for c in range(4):
                    def xbc_A(c=c, cc=blk * 4 + c, slot=slot, wr=wr):
                        i = cc % 2
                        pb = nextbank()
                        xr = 'scr%d' % (8 + i)
                        ar = 'scr%d' % (10 + i)
                        hr = 'xhalo%d' % cc
                        A('pe', lambda e: mm_fm(e, bank(pb)[:, 0:ncol], slot, c, hT, 8, ncol), reads=[wr, 'hT'], writes=['ps%d' % pb])
                        A('act', lambda e: e.activation(out=xraw[i][:, 3:3 + ncol], in_=bank(pb)[:, 0:ncol], func=AF.Copy),
                          reads=['ps%d' % pb], writes=[xr])
                        xh_ = 'xrh%d' % i
                        A('act', lambda e: e.activation(out=cacc[i][:, 0:ncol], in_=bank(pb)[:, 0:ncol], func=AF.Identity,
                                                        scale=C('cw', cc * 4 + 3), bias=C('cb', cc)),
                          reads=['ps%d' % pb, 'cst'], writes=[ar])
                        A('dve', lambda e: e.tensor_copy(out=xraw[i][:, 0:3], in_=xhalo[:, cc, :]), reads=[hr], writes=[xh_])
                        for tap in range(0, 3):
                            A('dve', lambda e, tap=tap: e.scalar_tensor_tensor(
                                out=cacc[i][:, 0:ncol], in0=xraw[i][:, tap:tap + ncol], scalar=C('cw', cc * 4 + tap),
                                in1=cacc[i][:, 0:ncol], op0=ALU.mult, op1=ALU.add), reads=[xr, xh_, ar, 'cst'], writes=[ar])
                        A('act', lambda e: e.activation(out=xhalo[:, cc, :], in_=xraw[i][:, ncol:ncol + 3], func=AF.Copy),
                          reads=[xr], writes=[hr])

                        def xbc_B():
                            A('act', lambda e: e.activation(out=R1[:, cc, 0:ncol], in_=cacc[i][:, 0:ncol], func=AF.Silu),
                              reads=[ar], writes=['R1_%d' % cc])
                        return xbc_B
                    pB = xbc_A()
                    if pend:
                        pend.pop(0)()
                    pend.append(pB)
            while pend:
                pend.pop(0)()
            chk(0.8)
            slot, wr = w_get(w_in, 8192)
            for b in range(nb):
                pb = nextbank()
                A('pe', lambda e, b=b, pb=pb, slot=slot: mm_tm(e, bank(pb)[0:bs, 0:32], slot, hT, b, bs, 8, 32), reads=[wr, 'hT'], writes=['ps%d' % pb])
                A('dve', lambda e, b=b, pb=pb: e.tensor_tensor(out=dtr[0:bs, b, :], in0=bank(pb)[0:bs, 0:32], in1=C('dtb', 0, 32)[0:bs, :], op=ALU.add),
                  reads=['ps%d' % pb, 'cst'], writes=['dtr'])

            if STOP <= 1:
                stopped[0] = True
                return
            LA = 2
            jobs = []
            for qb in range(nb):
                gb = (0 if sample else ti * 4) + qb
                if sample:
                    kl = [(1, t, t, 128) for t in range(4)] + [(0, 0, 4, 32)]
                else:
                    kl = []
                    for t in range(5):
                        kbi = gb - 4 + t
                        if kbi < 0:
                            continue
                        kl.append(((kbi // 4) % 2, kbi % 4, t, 128))
                for hp in range(8):
                    for half in range(2):
                        jobs.append((qb, hp, half, kl))
            nq = bs

            def att_sc(j):
                qb, hp, half, kl = jobs[j]
                si = j % 3
                bA, bB = 2 * si, 2 * si + 1
                p0, p1 = half * 64, half * 64 + 64

                def sc(e):
                    ins = None
                    for (sl, bk, t, nk) in kl:
                        o = bank(bA)[0:nk, t * 128: t * 128 + nq] if t < 4 else bank(bB)[0:nk, 0:nq]
                        ins = e.matmul(o, lhsT=kTb[sl][p0:p1, hp, bk * 128: bk * 128 + nk], rhs=qT[p0:p1, hp, qb * 128: qb * 128 + nq],
                                       start=True, stop=True)
                    return ins
                rds = [r3q(hp)] + ['kT%d_%d' % (sl, hp) for (sl, bk, t, nk) in kl]
                A('pe', sc, reads=rds, writes=['ps%d' % bA, 'ps%d' % bB])

            def att_rest(j):
                qb, hp, half, kl = jobs[j]
                si = j % 3
                h = hp * 2 + half
                bA, bB = 2 * si, 2 * si + 1
                p0, p1 = half * 64, half * 64 + 64
                pp = (j // 2) % 2
                nc0 = pp * 128
                far = [x for x in kl if x[2] < 4]
                if far:
                    tlo = far[0][2]
                    A('act', lambda e: e.activation(
                        out=pT[si][:, tlo * 128:512].rearrange("p (t q) -> p t q", q=128)[:, :, 0:nq],
                        in_=bank(bA)[:, tlo * 128:512].rearrange("p (t q) -> p t q", q=128)[:, :, 0:nq],
                        func=AF.Exp, scale=0.125), reads=['ps%d' % bA], writes=r3pt(si))
                nk4 = kl[-1][3]
                A('act', lambda e: e.activation(out=pT[si][0:nk4, 512:512 + nq], in_=bank(bB)[0:nk4, 0:nq],
                                                func=AF.Exp, scale=0.125), reads=['ps%d' % bB], writes=r3pt(si))
                has3 = any(x[2] == 3 for x in kl)
                if has3 and nk4 == 128:
                    A('dve', lambda e: e.tensor_tensor(
                        out=pT[si][:, 384:640].rearrange("p (t q) -> p t q", q=128)[:, :, 0:nq],
                        in0=pT[si][:, 384:640].rearrange("p (t q) -> p t q", q=128)[:, :, 0:nq],
                        in1=Et[:, h, :].rearrange("p (t q) -> p t q", q=128)[:, :, 0:nq], op=ALU.mult),
                      reads=r3pt(si) + ['Et'], writes=r3pt(si))
                else:
                    if has3:
                        A('dve', lambda e: e.tensor_tensor(out=pT[si][:, 384:384 + nq], in0=pT[si][:, 384:384 + nq],
                                                           in1=Et[:, h, 0:nq], op=ALU.mult),
                          reads=r3pt(si) + ['Et'], writes=r3pt(si))
                    A('dve', lambda e: e.tensor_tensor(out=pT[si][0:nk4, 512:512 + nq], in0=pT[si][0:nk4, 512:512 + nq],
                                                       in1=Et[0:nk4, h, 128:128 + nq], op=ALU.mult),
                      reads=r3pt(si) + ['Et'], writes=r3pt(si))
                if any(x[2] == 0 for x in kl) and nq > 64:
                    A('dve', lambda e: e.memset(pT[si][0:64, 64:128], 0.0), reads=r3pt(si), writes=r3pt(si))

                def pv(e):
                    ins = None
                    n = len(kl)
                    for jj, (sl, bk, t, nk) in enumerate(kl):
                        e.matmul(bank(6 + pp)[p0:p1, 0:nq], lhsT=vb[sl][0:nk, bk, h * 64:(h + 1) * 64], rhs=pT[si][0:nk, t * 128: t * 128 + nq],
                                 start=(jj == 0), stop=(jj == n - 1), skip_group_check=True)
                        ins = e.matmul(bank(6 + pp)[p0:p1, 128:128 + nq], lhsT=onesb[0:nk, 0:64], rhs=pT[si][0:nk, t * 128: t * 128 + nq],
                                       start=False, stop=(jj == n - 1), skip_group_check=True)
                    return ins
                A('pe', pv, reads=r3pt(si) + ['mk'] + ['v%d_%d' % (sl, bk) for (sl, bk, t, nk) in kl],
                  writes=['ps%d' % (6 + pp)])
                if half == 1:
                    fins.append(lambda: att_fin(qb, hp, pp, nc0))

            def att_fin(qb, hp, pp, nc0):
                if True:
                    ri = pp
                    A('act', lambda e: e.activation(out=rden[ri][:, 0:nq], in_=bank(6 + pp)[:, 128:128 + nq], func=AF.Ln),
                      reads=['ps%d' % (6 + pp)], writes=r3rd(ri))
                    A('act', lambda e: e.activation(out=rden[ri][:, 0:nq], in_=rden[ri][:, 0:nq], func=AF.Exp, scale=-1.0),
                      reads=r3rd(ri), writes=r3rd(ri))
                    A('dve', lambda e: e.tensor_tensor(out=yaT[:, hp, qb * 128: qb * 128 + nq], in0=bank(6 + pp)[:, 0:nq],
                                                       in1=rden[ri][:, 0:nq], op=ALU.mult),
                      reads=['ps%d' % (6 + pp)] + r3rd(ri), writes=['yaT'])

            nj = len(jobs)
            fins = []
            for j in range(min(LA, nj)):
                att_sc(j)
            for j in range(nj):
                if j + LA < nj:
                    att_sc(j + LA)
                npend = len(fins)
                att_rest(j)
                for _ in range(npend):
                    fins.pop(0)()
            while fins:
                fins.pop(0)()

            if STOP <= 2:
                stopped[0] = True
                return
            tl = tcount[0]
            tcount[0] += 1
            if DBG:
                out_dma(d_ya[tl], yaT[:].rearrange("p a b -> p (a b)"), ['yaT'], 'dbg')
            phase_barrier()
            rhs_hi = [scr_b(0, 1024), rhs_x[:, 0:1024]]
            rhs_lo = [scr_b(2048, 1024), rhs_x[:, 1024:2048]]
            dec = [scr_b(4096 + i * 2048, 1024) for i in range(2)]
            dxb = [scr_b(8192 + i * 1024, 512) for i in range(2)]
            xsd = [scr_b(10240 + i * 1024, 512) for i in range(2)]
            xwb = [scr_b(12288 + i * 1024, 512) for i in range(2)]
            bm_tm = scr_b(14336, 512)
            cbm = scr_b(15360, 512)
            yg = scr_b(16384, 2048)
            tmpS = scr_f(20480, 512)
            yv = scr_f(22528, 512)
            sm = scr_f(24576, 512)
            exo_all = sm[:, 0:384].rearrange("p (b c) -> p b c", c=96)
            a_hi_all = sm[:, 384:448].bitcast(BF16)
            a_lo_all = sm[:, 448:512].bitcast(BF16)
            ssy = ssb[:, 8:16]
            nt = bs
            ex1v = yv[0:nt, 0:nb * 32].rearrange("p (b c) -> p b c", c=32)
            a_fv = tmpS[0:nt, 0:nb * 32].rearrange("p (b c) -> p b c", c=32)
            A('act', lambda e: e.activation(out=ex1v, in_=dtr[0:nt, 0:nb, :], func=AF.Exp), reads=['dtr'], writes=['yv'])
            A('act', lambda e: e.activation(out=dtr[0:nt, 0:nb, :], in_=ex1v, func=AF.Ln, bias=1.0), reads=['yv'], writes=['sm1', 'dtr'])
            A('dve', lambda e: e.tensor_tensor(out=a_fv, in0=dtr[0:nt, 0:nb, :], in1=aneg[0:nt, :].unsqueeze(1).to_broadcast([nt, nb, 32]), op=ALU.mult),
              reads=['sm1', 'aneg'], writes=['tmpS'])
            A('dve', lambda e: e.tensor_copy(out=a_hi_all[0:nt, 0:nb * 32], in_=tmpS[0:nt, 0:nb * 32]), reads=['tmpS'], writes=['sm3'])
            A('dve', lambda e: e.tensor_tensor(out=a_lo_all[0:nt, 0:nb * 32], in0=tmpS[0:nt, 0:nb * 32], in1=a_hi_all[0:nt, 0:nb * 32], op=ALU.subtract),
              reads=['tmpS', 'sm3'], writes=['sm4'])

            def cums(e):
                ins = None
                for b in range(nb):
                    for (m, c0, M) in ((Vm, 0, nt), (Um, 32, nt), (onesb, 64, 128)):
                        o = bank(0)[0:M, b * 96 + c0: b * 96 + c0 + 32]
                        e.matmul(o, lhsT=m[0:nt, 0:M], rhs=a_hi_all[0:nt, b * 32:(b + 1) * 32], start=True, stop=False, skip_group_check=True)
                        ins = e.matmul(o, lhsT=m[0:nt, 0:M], rhs=a_lo_all[0:nt, b * 32:(b + 1) * 32], start=False, stop=True, skip_group_check=True)
                return ins
            A('pe', cums, reads=['sm3', 'sm4', 'mk'], writes=['ps0'])
            A('act', lambda e: e.activation(out=sm[0:nt, 0:nb * 96], in_=bank(0)[0:nt, 0:nb * 96], func=AF.Exp), reads=['ps0'], writes=['sm5', 'sm6'])
            if nt < 128:
                A('act', lambda e: e.activation(out=exo_all[:, 0:nb, 64:96], in_=bank(0)[:, 0:nb * 96].rearrange("p (b c) -> p b c", c=96)[:, :, 64:96],
                                                func=AF.Exp), reads=['ps0'], writes=['sm6'])
            A('dve', lambda e: e.tensor_tensor(out=exo_all[0:nt, 0:nb, 32:64], in0=exo_all[0:nt, 0:nb, 32:64], in1=dtr[0:nt, 0:nb, :], op=ALU.mult),
              reads=['sm5', 'sm1'], writes=['sm7', 'sm5'])
            blkf = []
            for b in range(nb):
                cb_ = cols(b)
                dt_ = dtr[:, b, :]
                exo = exo_all[:, b, :]
                w2 = exo_all[:, b, 32:64]
                a_hi = a_hi_all[:, b * 32:(b + 1) * 32]
                a_lo = a_lo_all[:, b * 32:(b + 1) * 32]
                def ssd_pre(b=b, cb_=cb_):

                    def cbf(e, cb_=cb_):
                        ins = None
                        for g in range(4):
                            ins = e.matmul(bank(3)[0:nt, g * 128: g * 128 + nt], lhsT=R1[:, 16 + g, cb_], rhs=R1[:, 20 + g, cb_], start=True, stop=True)
                        return ins
                    A('pe', cbf, reads=['R1_%d' % c for c in range(16, 24)], writes=['ps3'])
                    A('dve', lambda e: e.tensor_tensor(out=cbm[0:nt, :].rearrange("p (g i) -> p g i", g=4)[:, :, 0:nt],
                                                       in0=bank(3)[0:nt, :].rearrange("p (g i) -> p g i", g=4)[:, :, 0:nt],
                                                       in1=Vm[0:nt, 0:nt].unsqueeze(1).to_broadcast([nt, 4, nt]), op=ALU.mult),
                      reads=['ps3', 'mk'], writes=['cbm'])

                    def trb(e, cb_=cb_):
                        ins = None
                        for g in range(4):
                            ins = e.transpose(out=bankb(4)[0:nt, g * 128:(g + 1) * 128], in_=R1[:, 16 + g, cb_], identity=identb)
                        return ins
                    A('pe', trb, reads=['R1_%d' % c for c in range(16, 20)] + ['mk'], writes=['ps4'])
                    A('act', lambda e: e.activation(out=bm_tm[0:nt, :], in_=bankb(4)[0:nt, 0:512], func=AF.Copy), reads=['ps4'], writes=['bm_tm'])
                def ssd_stage1(g, b=b, cb_=cb_, dt_=dt_, exo=exo, w2=w2, a_hi=a_hi, a_lo=a_lo):
                    i = g % 2
                    hs = slice(8 * g, 8 * g + 8)
                    W8 = 8 * nt
                    sb0, sb1 = ((1, 2), (0, 3))[i]
                    A(RHS_ENG, lambda e: e.tensor_tensor(out=rhs_hi[i][0:nt, 0:8 * nt].rearrange("p (h i) -> p h i", h=8),
                                                         in0=Vm[0:nt, 0:nt].unsqueeze(1).to_broadcast([nt, 8, nt]),
                                                         in1=a_hi[0:nt, hs].unsqueeze(2).to_broadcast([nt, 8, nt]), op=ALU.mult),
                      reads=['sm3', 'mk'], writes=['rhs_hi%d' % i])
                    A(RHS_ENG, lambda e: e.tensor_tensor(out=rhs_lo[i][0:nt, 0:8 * nt].rearrange("p (h i) -> p h i", h=8),
                                                         in0=Vm[0:nt, 0:nt].unsqueeze(1).to_broadcast([nt, 8, nt]),
                                                         in1=a_lo[0:nt, hs].unsqueeze(2).to_broadcast([nt, 8, nt]), op=ALU.mult),
                      reads=['sm4', 'mk'], writes=['rhs_lo%d' % i])

                    def segf(e):
                        ins = None
                        for k_, c0 in enumerate(range(0, W8, 512)):
                            c1 = min(W8, c0 + 512)
                            o = bank((sb0, sb1)[k_])[0:nt, 0:c1 - c0]
                            e.matmul(o, lhsT=Um[0:nt, 0:nt], rhs=rhs_hi[i][0:nt, c0:c1], start=True, stop=False)
                            ins = e.matmul(o, lhsT=Um[0:nt, 0:nt], rhs=rhs_lo[i][0:nt, c0:c1], start=False, stop=True)
                        return ins
                    A('pe', segf, reads=['rhs_hi%d' % i, 'rhs_lo%d' % i, 'mk'], writes=['ps%d' % sb0, 'ps%d' % sb1])
                    for k_, c0 in enumerate(range(0, W8, 512)):
                        c1 = min(W8, c0 + 512)
                        bk_ = (sb0, sb1)[k_]
                        A('act', lambda e, c0=c0, c1=c1, bk_=bk_: e.activation(out=dec[i][0:nt, c0:c1], in_=bank(bk_)[0:nt, 0:c1 - c0], func=AF.Exp),
                          reads=['ps%d' % bk_], writes=['dec%d' % i])

                    def trx(e):
                        ins = None
                        for j in range(4):
                            ins = e.transpose(out=bankb(4)[0:nt, i * 512 + j * 128: i * 512 + (j + 1) * 128], in_=R1[:, 4 * g + j, cb_], identity=identb)
                        return ins
                    A('pe', trx, reads=['R1_%d' % (4 * g + j) for j in range(4)] + ['mk'], writes=['ps4'])
                    xsv = bankb(4)[0:nt, i * 512:(i + 1) * 512].rearrange("p (h d) -> p h d", h=8)
                    for (dst, nm, sc_ap, rr) in ((dxb[i], 'dx%d' % i, dt_[0:nt, hs], 'sm1'), (xsd[i], 'xsd%d' % i, C('dskip', 0, 32)[0:nt, hs], 'cst'),
                                                 (xwb[i], 'xw%d' % i, w2[0:nt, hs], 'sm7')):
                        A('dve', lambda e, dst=dst, sc_ap=sc_ap: e.tensor_tensor(
                            out=dst[0:nt, :].rearrange("p (h d) -> p h d", h=8), in0=xsv,
                            in1=sc_ap.unsqueeze(2).to_broadcast([nt, 8, 64]), op=ALU.mult), reads=['ps4', rr], writes=[nm])
                    A('dve', lambda e: e.tensor_tensor(
                        out=dec[i][0:nt, 0:W8].rearrange("p (h i) -> p h i", h=8), in0=dec[i][0:nt, 0:W8].rearrange("p (h i) -> p h i", h=8),
                        in1=cbm[0:nt, g * 128: g * 128 + nt].unsqueeze(1).to_broadcast([nt, 8, nt]), op=ALU.mult),
                      reads=['dec%d' % i, 'cbm'], writes=['dec%d' % i])

                def ssd_stage2(g, b=b, cb_=cb_, dt_=dt_, exo=exo, w2=w2, a_hi=a_hi, a_lo=a_lo):
                    i = g % 2
                    hs = slice(8 * g, 8 * g + 8)

                    def yf(e):
                        e.matmul(bank(5)[0:nt, :], lhsT=identb[0:nt, 0:nt], rhs=xsd[i][0:nt, :], start=True, stop=False, skip_group_check=True)
                        ins = None
                        for hh in range(8):
                            ins = e.matmul(bank(5)[0:nt, hh * 64:(hh + 1) * 64], lhsT=dec[i][0:nt, hh * nt:(hh + 1) * nt],
                                           rhs=dxb[i][0:nt, hh * 64:(hh + 1) * 64], start=False, stop=(hh == 7), skip_group_check=True)
                        return ins
                    A('pe', yf, reads=['dec%d' % i, 'dx%d' % i, 'xsd%d' % i, 'mk'], writes=['ps5'])
                    A('pe', lambda e: e.matmul(bank(6)[0:nt, :], lhsT=R1[:, 20 + g, cb_], rhs=hbf[:, g * 512:(g + 1) * 512], start=True, stop=True),
                      reads=['R1_%d' % (20 + g), 'hbf%d' % g], writes=['ps6'])
                    A('pe', lambda e: e.matmul(bank(7)[:, :], lhsT=bm_tm[0:nt, g * 128:(g + 1) * 128], rhs=xwb[i][0:nt, :], start=True, stop=True),
                      reads=['bm_tm', 'xw%d' % i], writes=['ps7'])
                    A('dve', lambda e: e.tensor_tensor(out=tmpS[0:nt, :].rearrange("p (h d) -> p h d", h=8),
                                                       in0=bank(6)[0:nt, :].rearrange("p (h d) -> p h d", h=8),
                                                       in1=exo[0:nt, hs].unsqueeze(2).to_broadcast([nt, 8, 64]), op=ALU.mult),
                      reads=['ps6', 'sm5'], writes=['tmpS'])
                    A('dve', lambda e: e.tensor_tensor(out=yv[0:nt, :], in0=bank(5)[0:nt, :], in1=tmpS[0:nt, :], op=ALU.add),
                      reads=['ps5', 'tmpS'], writes=['yv'])
                    A('dve', lambda e: e.tensor_tensor(out=yv[0:nt, :], in0=yv[0:nt, :], in1=zs[0:nt, b, g * 512:(g + 1) * 512], op=ALU.mult),
                      reads=['yv', 'R2_%d' % (b * 4 + g)], writes=['yv'])
                    A('act', lambda e: e.activation(out=yg[0:nt, g * 512:(g + 1) * 512], in_=yv[0:nt, :], func=AF.Copy),
                      reads=['yv'], writes=['yg'])
                    A('act', lambda e: e.activation(out=tmpS[0:nt, :], in_=yv[0:nt, :], func=AF.Square, accum_out=ssy[0:nt, g:g + 1]),
                      reads=['yv', 'tmpS'], writes=['tmpS', 'ssy'])
                    hv = hst[:, g * 512:(g + 1) * 512]
                    A('dve', lambda e: e.tensor_tensor(out=hv.rearrange("p (h d) -> p h d", h=8), in0=hv.rearrange("p (h d) -> p h d", h=8),
                                                         in1=exo[:, 64:96][:, hs].unsqueeze(2).to_broadcast([128, 8, 64]), op=ALU.mult),
                      reads=['hst%d' % g, 'sm6'], writes=['hst%d' % g])
                    A('dve', lambda e: e.tensor_tensor(out=hv, in0=hv, in1=bank(7)[:, :], op=ALU.add), reads=['hst%d' % g, 'ps7'], writes=['hst%d' % g])
                    A('act', lambda e: e.activation(out=hbf[:, g * 512:(g + 1) * 512], in_=hv, func=AF.Copy),
                      reads=['hst%d' % g], writes=['hbf%d' % g])

                def ssd_post(b=b, cb_=cb_):
                    A('dve', lambda e: e.reduce_sum(out=ssy[0:nt, 4:5], in_=ssy[0:nt, 0:4], axis=AX.X), reads=['ssy'], writes=['ssy'])
                    A('act', lambda e: e.activation(out=ssy[0:nt, 5:6], in_=ssy[0:nt, 4:5], func=AF.Ln, scale=1.0 / 2048, bias=EPS), reads=['ssy'], writes=['ssy'])
                    A('act', lambda e: e.activation(out=ssy[0:nt, 6:7], in_=ssy[0:nt, 5:6], func=AF.Exp, scale=-0.5), reads=['ssy'], writes=['ssy'])
                    A('dve', lambda e: e.tensor_scalar(out=dg[0:nt, 0:nt], in0=identb[0:nt, 0:nt], scalar1=ssy[0:nt, 6:7], scalar2=None, op0=ALU.mult),
                      reads=['ssy', 'mk'], writes=['dg'])
                    for j in range(4):
                        bk_ = 5 + (j % 3)

                        def tryb(e, j=j, bk_=bk_):
                            ins = None
                            for c in range(4):
                                cc = 4 * j + c
                                ins = e.matmul(bank(bk_)[:, c * 128: c * 128 + nt], lhsT=yg[0:nt, cc * 128:(cc + 1) * 128], rhs=dg[0:nt, 0:nt],
                                               start=True, stop=True)
                            return ins
                        A('pe', tryb, reads=['yg', 'dg'], writes=['ps%d' % bk_])
                        A('dve', lambda e, j=j, bk_=bk_: e.tensor_tensor(out=ybT[:, 4 * j:4 * j + 4, cb_],
                                                                        in0=bank(bk_).rearrange("p (k t) -> p k t", k=4)[:, :, 0:nt],
                                                                        in1=C('gssm', 4 * j, 4).unsqueeze(2).to_broadcast([128, 4, nt]), op=ALU.mult),
                          reads=['ps%d' % bk_, 'cst'], writes=['R3_%d' % c for c in range(4 * j, 4 * j + 4)])

                blkf.append((ssd_pre, ssd_stage1, ssd_stage2, ssd_post))

            blkf[0][0]()
            blkf[0][1](0)
            blkf[0][1](1)
            for b in range(nb):
                pre_, s1_, s2_, post_ = blkf[b]
                s2_(0)
                s1_(2)
                s2_(1)
                s1_(3)
                s2_(2)
                s2_(3)
                if b + 1 < nb:
                    blkf[b + 1][0]()
                    blkf[b + 1][1](0)
                    blkf[b + 1][1](1)
                post_()

            if STOP <= 3:
                stopped[0] = True
                return
            if DBG:
                out_dma(d_yb[tl], R3[:, :], ['R3_%d' % k for k in range(16)], 'dbg')
            phase_barrier()
            tmpB = [scr_f(i * 2048, 512) for i in range(2)]
            for blk in range(4):
                slot, wr = w_get(w_in, 8224 + 512 * blk)
                for c in range(4):
                    cc = blk * 4 + c
                    pb = nextbank()
                    A('pe', lambda e, c=c, pb=pb, slot=slot: mm_fm(e, bank(pb)[:, 0:ncol], slot, c, hT, 8, ncol), reads=[wr, 'hT'], writes=['ps%d' % pb])
                    A('act', lambda e, cc=cc, pb=pb: e.activation(out=R1[:, cc, 0:ncol], in_=bank(pb)[:, 0:ncol], func=AF.Sigmoid, bias=C('bgate', cc)),
                      reads=['ps%d' % pb, 'cst'], writes=['R1_%d' % cc])
            for blk in range(2):
                slot, wr = w_get(w_pa, 512 * blk)
                for c in range(4):
                    cc = blk * 4 + c
                    pb = nextbank()
                    A('pe', lambda e, c=c, pb=pb, slot=slot: mm_fm(e, bank(pb)[:, 0:ncol], slot, c, yaT, 8, ncol), reads=[wr, 'yaT'], writes=['ps%d' % pb])
                    A('dve', lambda e, cc=cc, pb=pb: e.tensor_tensor(out=t1[:, cc, 0:ncol], in0=bank(pb)[:, 0:ncol], in1=R1[:, cc, 0:ncol], op=ALU.mult),
                      reads=['ps%d' % pb, 'R1_%d' % cc], writes=['R2_%d' % (2 * cc), 'R2_%d' % (2 * cc + 1)])
            for blk in range(2):
                slot0, wr0 = w_get(w_pb, 512 * blk)
                slot1, wr1 = w_get(w_pb, 512 * blk, prefetch=False)
                for c in range(4):
                    cc = blk * 4 + c
                    pb = nextbank()
                    i = cc % 2

                    def pbf(e, c=c, pb=pb, slot0=slot0, slot1=slot1):
                        mm_fm(e, bank(pb)[:, 0:ncol], slot0, c, ybT, 8, ncol, koff=0, start=True, stop=False)
                        return mm_fm(e, bank(pb)[:, 0:ncol], slot1, c, ybT, 8, ncol, koff=8, start=False, stop=True)
                    A('pe', pbf, reads=[wr0, wr1] + ['R3_%d' % k for k in range(16)], writes=['ps%d' % pb])
                    A('dve', lambda e, cc=cc, pb=pb, i=i: e.tensor_tensor(out=tmpB[i][:, 0:ncol], in0=bank(pb)[:, 0:ncol], in1=R1[:, 8 + cc, 0:ncol], op=ALU.mult),
                      reads=['ps%d' % pb, 'R1_%d' % (8 + cc)], writes=['scr%d' % i])
                    A('dve', lambda e, cc=cc, i=i: e.tensor_tensor(out=mT[:, cc, 0:ncol], in0=t1[:, cc, 0:ncol], in1=tmpB[i][:, 0:ncol], op=ALU.add),
                      reads=['scr%d' % i, 'R2_%d' % (2 * cc), 'R2_%d' % (2 * cc + 1)], writes=['yaT'])
            if DBG:
                out_dma(d_m[tl], yaT[:].rearrange("p a b -> p (a b)"), ['yaT'], 'dbg')
                out_dma(d_g[tl], R1[:, 0:16, :].rearrange("p a b -> p (a b)"), ['R1_%d' % k for k in range(16)], 'dbg')
            for half in range(2):
                slot, wr = w_get(w_out, 512 * half)
                for b in range(nb):
                    pb = nextbank()
                    A('pe', lambda e, b=b, pb=pb, slot=slot: mm_tm(e, bank(pb)[0:bs, :], slot, mT, b, bs, 8, 512), reads=[wr, 'yaT'], writes=['ps%d' % pb])
                    A('dve', lambda e, b=b, pb=pb, half=half: e.tensor_tensor(out=xt[0:bs, b, half * 512:(half + 1) * 512], in0=xt[0:bs, b, half * 512:(half + 1) * 512],
                                                                   in1=bank(pb)[0:bs, :], op=ALU.add),
                      reads=['ps%d' % pb, 'xt%d' % b], writes=['xt%d' % b])
                    if half == 1:
                        if b >= 2:
                            rms_B(b - 2, bs, 'gffn')
                        rms_A(b, bs, scr_b(0, 1024))
                if half == 1:
                    for b in range(max(0, nb - 2), nb):
                        rms_B(b, bs, 'gffn')

            if STOP <= 4:
                stopped[0] = True
                return
            if DBG:
                out_dma(d_x[tl], xt[:].rearrange("p a b -> p (a b)"), xres, 'dbg')
            if tidx + 1 < len(tiles):
                emit_xprefetch(tidx + 1)
            phase_barrier()
            junk = scr_b(0, 1024)
            ugr = [scr_f(2048 + i * 2064, 516) for i in range(2)]
            uvr = [scr_f(6176 + i * 2064, 516) for i in range(2)]
            cga = [scr_f(10304 + k * 2048, 512) for k in range(2)]
            cva = [scr_f(14400 + k * 2048, 512) for k in range(2)]
            gl = [scr_f(18496 + k * 2048, 512) for k in range(2)]
            pend = []
            for ub in range(6):
                slotg, wrg = w_get(w_up, 512 * ub)
                slotv, wrv = w_get(w_up, 3072 + 512 * ub, prefetch=False)
                for c in range(4):
                    def up_A(c=c, cg_=ub * 4 + c, slotg=slotg, wrg=wrg, slotv=slotv, wrv=wrv):
                        i = cg_ % 2
                        for (slot, wr, raw, rn, chn, acc, an, pb) in ((slotg, wrg, ugr[i], 'scr%d' % (1 + i), cg_, cga[i], 'scr%d' % (5 + i), 2 * i),
                                                                       (slotv, wrv, uvr[i], 'scr%d' % (3 + i), 24 + cg_, cva[i], 'scr%d' % (7 + i), 2 * i + 1)):
                            hr = 'uhalo%d' % chn
                            A('pe', lambda e, pb=pb, slot=slot: mm_fm(e, bank(pb)[:, 0:ncol], slot, c, hT, 8, ncol), reads=[wr, 'hT'], writes=['ps%d' % pb])
                            A('act', lambda e, raw=raw, pb=pb: e.activation(out=raw[:, 2:2 + ncol], in_=bank(pb)[:, 0:ncol], func=AF.Copy),
                              reads=['ps%d' % pb], writes=[rn])
                            rh_ = rn + 'h'
                            A('act', lambda e, chn=chn, acc=acc, pb=pb: e.activation(out=acc[:, 0:ncol], in_=bank(pb)[:, 0:ncol], func=AF.Identity,
                                                                                    scale=C('fw', chn * 3 + 2), bias=C('fb', chn)),
                              reads=['ps%d' % pb, 'cst'], writes=[an])
                            A('dve', lambda e, raw=raw, chn=chn: e.tensor_copy(out=raw[:, 0:2], in_=uhalo[:, chn, :]), reads=[hr], writes=[rh_])
                            for tap in range(0, 2):
                                A('dve', lambda e, raw=raw, chn=chn, acc=acc, tap=tap: e.scalar_tensor_tensor(
                                    out=acc[:, 0:ncol], in0=raw[:, tap:tap + ncol], scalar=C('fw', chn * 3 + tap), in1=acc[:, 0:ncol],
                                    op0=ALU.mult, op1=ALU.add), reads=[rn, rh_, an, 'cst'], writes=[an])
                            A('act', lambda e, raw=raw, chn=chn: e.activation(out=uhalo[:, chn, :], in_=raw[:, ncol:ncol + 2], func=AF.Copy),
                              reads=[rn], writes=[hr])

                        def up_B():
                            A('act', lambda e: e.activation(out=gl[i][:, 0:ncol], in_=cga[i][:, 0:ncol], func=AF.Gelu_apprx_tanh),
                              reads=['scr%d' % (5 + i)], writes=['scr%d' % (9 + i)])
                            A('dve', lambda e: e.tensor_tensor(out=R1[:, cg_, 0:ncol], in0=gl[i][:, 0:ncol], in1=cva[i][:, 0:ncol], op=ALU.mult),
                              reads=['scr%d' % (9 + i), 'scr%d' % (7 + i)], writes=['R1_%d' % cg_])
                        return up_B
                    pB = up_A()
                    if pend:
                        pend.pop(0)()
                    pend.append(pB)
            while pend:
                pend.pop(0)()
            for half in range(2):
                for kg in range(3):
                    slot, wr = w_get(w_down, 512 * half)
                    for b in range(nb):
                        A('pe', lambda e, b=b, slot=slot, kg=kg: mm_tm(e, bank(4 + b)[0:bs, :], slot, R1, b, bs, 8, 512, koff=8 * kg,
                                                                        start=(kg == 0), stop=(kg == 2)),
                          reads=[wr] + ['R1_%d' % k for k in range(8 * kg, 8 * kg + 8)], writes=['ps%d' % (4 + b)])
                for b in range(nb):
                    A('dve', lambda e, b=b, half=half: e.tensor_tensor(out=xt[0:bs, b, half * 512:(half + 1) * 512], in0=xt[0:bs, b, half * 512:(half + 1) * 512],
                                                                       in1=bank(4 + b)[0:bs, :], op=ALU.add),
                      reads=['ps%d' % (4 + b), 'xt%d' % b], writes=['xt%d' % b])
                    if half == 1:
                        if b >= 2:
                            rms_B(b - 2, bs, 'gple')
                        rms_A(b, bs, scr_b(0, 1024))
                if half == 1:
                    for b in range(max(0, nb - 2), nb):
                        rms_B(b, bs, 'gple')

            if STOP <= 5:
                stopped[0] = True
                return
            if DBG:
                out_dma(d_x2[tl], xt[:].rearrange("p a b -> p (a b)"), xres, 'dbg')
            phase_barrier()
            junk = scr_b(0, 1024)
            sgt = [scr_f(2048 + i * 2048, 512) for i in range(2)]
            for half in range(2):
                slotg, wrg = w_get(w_pg, 512 * half)
                for b in range(nb):
                    A('pe', lambda e, b=b, slotg=slotg: mm_tm(e, bank(b)[0:bs, :], slotg, hT, b, bs, 8, 512), reads=[wrg, 'hT'], writes=['ps%d' % b])
                slotp, wrp = w_get(w_ple, 512 * half)
                for b in range(nb):
                    A('pe', lambda e, b=b, slotp=slotp: mm_tm(e, bank(4 + b)[0:bs, :], slotp, pet, b, bs, 2, 512), reads=[wrp, 'pet'], writes=['ps%d' % (4 + b)])
                for b in range(nb):
                    i = b % 2
                    A('act', lambda e, i=i, b=b: e.activation(out=sgt[i][0:bs, :], in_=bank(b)[0:bs, :], func=AF.Sigmoid), reads=['ps%d' % b], writes=['scr%d' % (1 + i)])
                    A('dve', lambda e, i=i, b=b: e.tensor_tensor(out=sgt[i][0:bs, :], in0=sgt[i][0:bs, :], in1=bank(4 + b)[0:bs, :], op=ALU.mult),
                      reads=['ps%d' % (4 + b), 'scr%d' % (1 + i)], writes=['scr%d' % (1 + i)])
                    A('dve', lambda e, b=b, i=i, half=half: e.tensor_tensor(out=xt[0:bs, b, half * 512:(half + 1) * 512],
                                                                            in0=xt[0:bs, b, half * 512:(half + 1) * 512], in1=sgt[i][0:bs, :], op=ALU.add),
                      reads=['scr%d' % (1 + i), 'xt%d' % b], writes=['xt%d' % b])
                    if half == 1 and not sample:
                        out_dma(ydst[b * 128:(b + 1) * 128, :], xt[:, b, :], ['xt%d' % b], 'yout%d' % b)
            if sample:
                out_dma(ydst, xt[0:32, 0, :], xres, 'yout0')

            if last:
                sd = ssms[0] if sample else ssmp[s]
                for q in range(4):
                    def trs(e, q=q):
                        ins = None
                        for j in range(4):
                            c = 4 * q + j
                            ins = e.transpose(out=bank(0)[:, j * 128:(j + 1) * 128], in_=hst[:, c * 128:(c + 1) * 128], identity=identf[:, :])
                        return ins
                    A('pe', trs, reads=['hst%d' % q, 'identf'], writes=['ps0'])
                    sg, sr = stage()
                    A('act', lambda e, sg=sg: e.activation(out=sg[:, :], in_=bank(0)[:, :], func=AF.Copy), reads=['ps0'], writes=[sr])
                    out_dma(sd[q * 512:(q + 1) * 512, :].rearrange("(j p) n -> p j n", p=128), sg[:, :].rearrange("p (j n) -> p j n", j=4), [sr], sr)
                cd = cssms[0] if sample else cssmp[s]
                for q in range(6):
                    def trc(e, q=q):
                        ins = None
                        for j in range(4):
                            ins = e.transpose(out=bank(1)[0:3, j * 128:(j + 1) * 128], in_=xhalo[:, 4 * q + j, :], identity=identf[:, :])
                        return ins
                    A('pe', trc, reads=['xhalo%d' % (4 * q + j) for j in range(4)] + ['identf'], writes=['ps1'])
                    sg, sr = stage()
                    A('act', lambda e, sg=sg: e.activation(out=sg[0:3, :], in_=bank(1)[0:3, :], func=AF.Copy), reads=['ps1'], writes=[sr])
                    out_dma(cd[:, q * 512:(q + 1) * 512], sg[0:3, :], [sr], sr)
                fd = cffns[0] if sample else cffnp[s]
                for q in range(12):
                    def trf(e, q=q):
                        ins = None
                        for j in range(4):
                            ins = e.transpose(out=bank(1)[0:2, j * 128:(j + 1) * 128], in_=uhalo[:, 4 * q + j, :], identity=identf[:, :])
                        return ins
                    A('pe', trf, reads=['uhalo%d' % (4 * q + j) for j in range(4)] + ['identf'], writes=['ps1'])
                    sg, sr = stage()
                    A('act', lambda e, sg=sg: e.activation(out=sg[0:2, :], in_=bank(1)[0:2, :], func=AF.Copy), reads=['ps1'], writes=[sr])
                    out_dma(fd[:, q * 512:(q + 1) * 512], sg[0:2, :], [sr], sr)

        try:
            chk(0.05)
            for s in range(NSEQ):
                for ti in range(NT):
                    do_tile('p', s, ti)
            if SAMPLE:
                do_tile('s', 0, 0)
        except StopBuild:
            pass
        A('sp', None, reads=list(out_res))
        S.emit(nc, st)
    return nc


def _consts(g_mix, g_ffn, g_ple, g_q, g_k, b_gate, conv_ssm_w, conv_ssm_b, ffn_conv_w, ffn_conv_b, dt_bias, a_log, d_skip, g_ssm, rel_bias):
    cst = np.zeros((128, NCST), np.float32)

    def fm(v, nch):
        return np.ascontiguousarray(np.asarray(v, np.float32).reshape(nch, 128).T)

    cst[:, _off['gmix']:_off['gmix'] + 8] = fm(g_mix, 8)
    cst[:, _off['gffn']:_off['gffn'] + 8] = fm(g_ffn, 8)
    cst[:, _off['gple']:_off['gple'] + 8] = fm(g_ple, 8)
    cst[:, _off['gq']] = np.tile(np.asarray(g_q, np.float32), 2)
    cst[:, _off['gk']] = np.tile(np.asarray(g_k, np.float32), 2)
    cst[:, _off['bgate']:_off['bgate'] + 16] = fm(b_gate, 16)
    cw = np.asarray(conv_ssm_w, np.float32)
    cst[:, _off['cw']:_off['cw'] + 96] = cw.T.reshape(24, 128, 4).transpose(1, 0, 2).reshape(128, 96)
    cst[:, _off['cb']:_off['cb'] + 24] = fm(conv_ssm_b, 24)
    fw = np.asarray(ffn_conv_w, np.float32)
    cst[:, _off['fw']:_off['fw'] + 144] = fw.T.reshape(48, 128, 3).transpose(1, 0, 2).reshape(128, 144)
    cst[:, _off['fb']:_off['fb'] + 48] = fm(ffn_conv_b, 48)
    cst[:, _off['dtb']:_off['dtb'] + 32] = np.asarray(dt_bias, np.float32)[None, :]
    cst[:, _off['alog']:_off['alog'] + 32] = np.asarray(a_log, np.float32)[None, :]
    cst[:, _off['dskip']:_off['dskip'] + 32] = np.asarray(d_skip, np.float32)[None, :]
    cst[:, _off['gssm']:_off['gssm'] + 16] = fm(g_ssm, 16)
    tab = np.asarray(rel_bias, np.float32)
    cst[:, _off['ch']:_off['ch'] + 16] = tab[:, 256][None, :]
    j = np.arange(128)[:, None]
    i = np.arange(128)[None, :]
    bx = np.zeros((128, 16, 2, 128), np.float32)
    for t, base in enumerate((128, 0)):
        rel = np.clip(base + i - j, -128, 128) + 128
        bx[:, :, t, :] = tab[:, rel].transpose(1, 0, 2)
    mk = np.zeros((128, 5, 128), np.float32)
    mk[:, 0, :] = np.eye(128)
    mk[:, 1, :] = (j <= i)
    mk[:, 2, :] = (j > i)
    mk[:, 3, :] = 1.0
    mk[:, 4, :] = ((j // 64) == (i // 64))
    return cst, bx.reshape(128, 16 * 256), mk.reshape(128, 5 * 128)


_NC_CACHE = {}


def _run(inputs, SEQ, n_cores, SAMPLE=True):
    f = lambda k: np.asarray(inputs[k], np.float32)
    cst, bx, mk = _consts(f('g_mix')[0], f('g_ffn')[0], f('g_ple')[0], f('g_q')[0], f('g_k')[0], f('b_gate')[0], f('conv_ssm_w')[0],
                          f('conv_ssm_b')[0], f('ffn_conv_w')[0], f('ffn_conv_b')[0], f('dt_bias')[0], f('a_log')[0], f('d_skip')[0],
                          f('g_ssm')[0], f('rel_bias')[0])
    key = (SEQ, SAMPLE)
    if key not in _NC_CACHE:
        _NC_CACHE[key] = build(SEQ, 2, SAMPLE)
    nc = _NC_CACHE[key]
    shared = dict(cst=cst, biasx=bx, mk=mk, w_in=np.ascontiguousarray(f('w_in')[0]), w_pa=np.ascontiguousarray(f('w_proj_a')[0]),
                  w_pb=np.ascontiguousarray(f('w_proj_b')[0]), w_out=np.ascontiguousarray(f('w_out')[0]), w_up=np.ascontiguousarray(f('w_up')[0]),
                  w_down=np.ascontiguousarray(f('w_down')[0]), w_pg=np.ascontiguousarray(f('w_ple_gate')[0]), w_ple=np.ascontiguousarray(f('w_ple')[0]))
    xpr, ppr = f('x_prompt'), f('p_prompt')[0]
    in_maps = []
    for c in range(n_cores):
        m = dict(shared)
        m['xp'] = np.ascontiguousarray(xpr[2 * c:2 * c + 2])
        m['ppT'] = np.ascontiguousarray(ppr[2 * c:2 * c + 2].transpose(0, 2, 1))
        m['xsm'] = np.ascontiguousarray(f('x_sample')[c])
        m['psT'] = np.ascontiguousarray(f('p_sample')[0, c].T)
        ck = f('cache_k')[0, c]
        m['ckT'] = np.ascontiguousarray(ck.transpose(1, 2, 0).reshape(8, 128, 512).transpose(1, 0, 2))
        m['cv'] = np.ascontiguousarray(f('cache_v')[0, c].reshape(512, 1024))
        m['st_ssmT'] = np.ascontiguousarray(f('state_ssm')[0, c].reshape(2048, 128).T)
        m['st_convT'] = np.ascontiguousarray(f('state_conv_ssm')[0, c].T.reshape(24, 128, 3).transpose(1, 0, 2))
        m['st_ffnT'] = np.ascontiguousarray(f('state_conv_ffn')[0, c].T.reshape(48, 128, 2).transpose(1, 0, 2))
        in_maps.append(m)
    res = run_bass_kernel_spmd(nc, in_maps, core_ids=list(range(n_cores)))
    R = res.results
    LK = min(512, SEQ)
    cat = lambda k: np.concatenate([np.asarray(r[k]) for r in R], axis=0)
    stk = lambda k: np.stack([np.asarray(r[k]) for r in R], axis=0)
    yp = cat('yp')
    ys = stk('ysm')
    kp = cat('kp').reshape(-1, LK, 16, 64)[None]
    vp = cat('vp').reshape(-1, LK, 16, 64)[None]
    ssmp = cat('ssmp').reshape(-1, 32, 64, 128)[None]
    cssmp = cat('cssmp')[None]
    cffnp = cat('cffnp')[None]
    ks = stk('ksm').reshape(-1, 32, 16, 64)[None]
    vs = stk('vsm').reshape(-1, 32, 16, 64)[None]
    ssms = cat('ssms').reshape(-1, 32, 64, 128)[None]
    cssms = cat('cssms')[None]
    cffns = cat('cffns')[None]
    return tuple(np.ascontiguousarray(a, dtype=np.float32) for a in (yp, ys, kp, vp, ssmp, cssmp, cffnp, ks, vs, ssms, cssms, cffns))


def kernel(**inputs):
    SEQ = int(np.asarray(inputs['x_prompt']).shape[1])
    n_cores = int(np.asarray(inputs['x_prompt']).shape[0]) // 2
    return _run(inputs, SEQ, n_cores)
```

```python
import numpy as np
from contextlib import ExitStack
import concourse.bass as bass
import concourse.mybir as mybir
from concourse.bass_utils import run_bass_kernel_spmd

F32 = mybir.dt.float32
BF16 = mybir.dt.bfloat16
AF = mybir.ActivationFunctionType
ALU = mybir.AluOpType
AX = mybir.AxisListType
ENGS = ('pe', 'act', 'dve', 'pool', 'sp')
RHS_ENG = 'pool'
POOL_CONV = True
WSCRATCH = True
WLOAD_ENG = 'sp'
EPS = 1e-6
D = 1024
NIN = 10272
NSLOT = 3


class StopBuild(Exception):
    pass


class Sched:
    def __init__(self):
        self.ops = []
        self.last_w = {}
        self.readers = {}
        self.eng_count = {e: 0 for e in ENGS}
        self.dma_count = {}
        self.last_dma = {}

    def add(self, eng, fn, reads=(), writes=(), dma=None):
        writes = list(writes) + [r for r in reads if r.startswith('ps')]
        reads = [r for r in reads if not r.startswith('ps')]
        idx = len(self.ops)
        deps = set()
        for r in reads:
            w = self.last_w.get(r)
            if w is not None:
                deps.add(w)
        for w_ in writes:
            w = self.last_w.get(w_)
            if w is not None:
                deps.add(w)
            deps.update(self.readers.get(w_, ()))
        if fn is None:
            assert not writes
            sig = None
        elif dma is None:
            self.eng_count[eng] += 1
            sig = ('e:' + eng, self.eng_count[eng])
        else:
            self.dma_count[dma] = self.dma_count.get(dma, 0) + 1
            sig = ('d:' + dma, 16 * self.dma_count[dma])
            prev = self.last_dma.get(dma)
            if prev is not None:
                deps.add(prev)
            self.last_dma[dma] = idx
        deps.discard(idx)
        self.ops.append(dict(eng=eng, fn=fn, deps=sorted(deps), sig=sig, dma=dma))
        for r in reads:
            self.readers.setdefault(r, []).append(idx)
        for w_ in writes:
            self.last_w[w_] = idx
            self.readers[w_] = []
        return idx

    def emit(self, nc, stack):
        names = ['e:' + e for e in ENGS] + ['d:' + k for k in self.dma_count]
        sems = {}
        for n in names:
            sems[n] = stack.enter_context(nc.semaphore(n.replace(':', '_')))
        know = {e: {} for e in ENGS}
        opknow = [None] * len(self.ops)
        waits = [None] * len(self.ops)
        for i, op in enumerate(self.ops):
            e = op['eng']
            k = know[e]
            m = {}
            for d in op['deps']:
                dop = self.ops[d]
                sn, sv = dop['sig']
                if k.get(sn, 0) >= sv:
                    continue
                if sn == 'e:pe' and e == 'pe':
                    continue
                m[sn] = max(m.get(sn, 0), sv)
                for a, b in opknow[d].items():
                    if k.get(a, 0) < b:
                        k[a] = b
                if k.get(sn, 0) < sv:
                    k[sn] = sv
            waits[i] = list(m.items())
            opknow[i] = dict(k)
        block = stack.enter_context(nc.Block())
        per_eng = {e: [i for i, op in enumerate(self.ops) if op['eng'] == e] for e in ENGS}

        def make(e):
            def body(engine):
                for i in per_eng[e]:
                    op = self.ops[i]
                    for sn, sv in waits[i]:
                        engine.wait_ge(sems[sn], sv)
                    if op['fn'] is None:
                        continue
                    ins = op['fn'](engine)
                    sn, sv = op['sig']
                    ins.then_inc(sems[sn], 16 if op['dma'] is not None else 1)
            return body

        block.tensor(make('pe'))
        block.scalar(make('act'))
        block.vector(make('dve'))
        block.gpsimd(make('pool'))
        block.sync(make('sp'))


_off = {}
_n = 0
for _name, _w in [('gmix', 8), ('gffn', 8), ('gple', 8), ('gq', 1), ('gk', 1), ('bgate', 16), ('cw', 96), ('cb', 24),
                  ('fw', 144), ('fb', 48), ('dtb', 32), ('alog', 32), ('dskip', 32), ('gssm', 16), ('ch', 16)]:
    _off[_name] = _n
    _n += _w
NCST = _n


def build(SEQ, NSEQ=2, SAMPLE=True, DBG=False, STOP=99):
    NT = SEQ // 512
    LK = min(512, SEQ)
    nc = bass.Bass("TRN2", target_bir_lowering=False)
    S = Sched()
    A = S.add

    def din(name, shape):
        return nc.dram_tensor(name, list(shape), F32, kind="ExternalInput").ap()

    def dout(name, shape):
        return nc.dram_tensor(name, list(shape), F32, kind="ExternalOutput").ap()

    xp = din("xp", [NSEQ, SEQ, D])
    ppT = din("ppT", [NSEQ, 256, SEQ])
    xsm = din("xsm", [32, D])
    psT = din("psT", [256, 32])
    ckT = din("ckT", [128, 8, 512])
    cv = din("cv", [512, D])
    st_ssmT = din("st_ssmT", [128, 2048])
    st_convT = din("st_convT", [128, 24, 3])
    st_ffnT = din("st_ffnT", [128, 48, 2])
    cst_d = din("cst", [128, NCST])
    biasx = din("biasx", [128, 16 * 256])
    mk_d = din("mk", [128, 5 * 128])
    w_in = din("w_in", [D, NIN])
    w_pa = din("w_pa", [D, D])
    w_pb = din("w_pb", [2048, D])
    w_out = din("w_out", [D, D])
    w_up = din("w_up", [D, 6144])
    w_down = din("w_down", [3072, D])
    w_pg = din("w_pg", [D, D])
    w_ple = din("w_ple", [256, D])

    yp = dout("yp", [NSEQ, SEQ, D])
    kp = dout("kp", [NSEQ, LK, D])
    vp = dout("vp", [NSEQ, LK, D])
    ssmp = dout("ssmp", [NSEQ, 2048, 128])
    cssmp = dout("cssmp", [NSEQ, 3, 3072])
    cffnp = dout("cffnp", [NSEQ, 2, 6144])
    ysm = dout("ysm", [32, D])
    ksm = dout("ksm", [32, D])
    vsm = dout("vsm", [32, D])
    ssms = dout("ssms", [1, 2048, 128])
    cssms = dout("cssms", [1, 3, 3072])
    cffns = dout("cffns", [1, 2, 6144])
    out_res = []
    if DBG:
        ntl = NSEQ * (SEQ // 512) + 1
        d_ya = nc.dram_tensor("d_ya", [ntl, 128, 8 * 512], BF16, kind="ExternalOutput").ap()
        d_yb = nc.dram_tensor("d_yb", [ntl, 128, 16 * 512], BF16, kind="ExternalOutput").ap()
        d_x = nc.dram_tensor("d_x", [ntl, 128, 4 * 1024], F32, kind="ExternalOutput").ap()
        d_x2 = nc.dram_tensor("d_x2", [ntl, 128, 4 * 1024], F32, kind="ExternalOutput").ap()
        d_m = nc.dram_tensor("d_m", [ntl, 128, 8 * 512], BF16, kind="ExternalOutput").ap()
        d_g = nc.dram_tensor("d_g", [ntl, 128, 16 * 512], BF16, kind="ExternalOutput").ap()
    tcount = [0]

    with ExitStack() as st:
        def sb(name, shape, dt):
            return st.enter_context(nc.sbuf_tensor(name, list(shape), dt))

        xt = sb("xt", [128, 4, D], F32)
        hT = sb("hT", [128, 8, 512], BF16)
        R3 = sb("R3", [128, 8192], BF16)
        kTb = [sb("kTb%d" % i, [128, 8, 512], BF16) for i in range(2)]
        vb = [sb("vb%d" % i, [128, 4, D], BF16) for i in range(2)]
        yaT = sb("yaT", [128, 8, 512], BF16)
        R2 = sb("R2", [128, 8192], BF16)
        R1 = sb("R1", [128, 24, 512], BF16)
        SCR = sb("SCR", [128, 6656], F32)
        hst = sb("hst", [128, 2048], F32)
        hbf = sb("hbf", [128, 2048], BF16)
        Et = sb("Et", [128, 16, 256], BF16)
        wsl = [sb("wsl%d" % i, [128, 8, 512], BF16) for i in range(NSLOT)]
        xhalo = sb("xhalo", [128, 24, 3], F32)
        uhalo = sb("uhalo", [128, 48, 2], F32)
        cst = sb("cst_sb", [128, NCST], F32)
        mk = sb("mk_sb", [128, 5, 128], BF16)
        identf = sb("identf", [128, 128], F32)
        aneg = sb("aneg", [128, 32], F32)
        xn = sb("xn", [128, D], BF16)
        stg = sb("stg", [128, 2, 512], F32)
        pet = sb("pet", [128, 2, 512], BF16)
        dtr = sb("dtr", [128, 4, 32], F32)
        ssb = sb("ssb", [128, 16], F32)
        PS = st.enter_context(nc.psum_tensor("PS", [128, 4096], F32))

        def bank(i):
            return PS[:, i * 512:(i + 1) * 512]

        def bankb(i):
            return PS[:, i * 512:(i + 1) * 512].bitcast(BF16)

        identb = mk[:, 0, :]
        Vm = mk[:, 1, :]
        Um = mk[:, 2, :]
        onesb = mk[:, 3, :]
        blk1 = mk[:, 4, :]

        def C(name, i=0, n=1):
            o = _off[name] + i
            return cst[:, o:o + n]

        def scr_f(off_b, ncols):
            return SCR[:, off_b // 4: off_b // 4 + ncols]

        def scr_b(off_b, ncols):
            return SCR[:, off_b // 4: off_b // 4 + ncols // 2].bitcast(BF16)

        qT = R3[:, 0:4096].rearrange("p (c t) -> p c t", c=8)
        ybT = R3[:, :].rearrange("p (c t) -> p c t", c=16)
        pT = [R3[:, 4096 + i * 1024: 4096 + i * 1024 + 640] for i in range(3)]
        rden = [R3[:, 7168 + i * 512: 7168 + i * 512 + 256].bitcast(F32) for i in range(2)]
        zs = R2[:, :].rearrange("p (b c) -> p b c", b=4)
        t1 = R2[:, :].bitcast(F32).rearrange("p (c t) -> p c t", c=8)
        tmpE = R2[:, :].bitcast(F32).rearrange("p (h t) -> p h t", h=16)
        xin = R2[:, :].bitcast(F32).rearrange("p (b d) -> p b d", b=4)
        mT = yaT

        def r3q(hp):
            return 'R3_%d' % hp

        def r3pt(i):
            return ['R3_%d' % (8 + 2 * i), 'R3_%d' % (9 + 2 * i)]

        def r3rd(i):
            return ['R3_%d' % (14 + i)]

        SCRN = ['scr%d' % i for i in range(12)]

        dummy = sb("dummy_sb", [128, 8], F32)
        rhs_x = sb("rhs_x", [128, 2048], BF16)
        dg = sb("dg", [128, 128], BF16)

        def phase_barrier():
            A('dve', lambda e: e.memset(dummy[0:1, 0:1], 0.0), writes=SCRN)

        wplan = []

        def tile_plan():
            p = []
            for i in range(16):
                p.append((w_in, 0, 8, 512 * i, 512))
            p.append((w_in, 0, 8, 8192, 32))
            for i in range(4):
                p.append((w_in, 0, 8, 8224 + 512 * i, 512))
            for i in range(2):
                p.append((w_pa, 0, 8, 512 * i, 512))
            for i in range(2):
                for kg in range(2):
                    p.append((w_pb, kg * 1024, 8, 512 * i, 512))
            for i in range(2):
                p.append((w_out, 0, 8, 512 * i, 512))
            for i in range(6):
                p.append((w_up, 0, 8, 512 * i, 512))
                p.append((w_up, 0, 8, 3072 + 512 * i, 512))
            for i in range(2):
                for kg in range(3):
                    p.append((w_down, kg * 1024, 8, 512 * i, 512))
            for i in range(2):
                p.append((w_pg, 0, 8, 512 * i, 512))
                p.append((w_ple, 0, 2, 512 * i, 512))
            return p

        ntiles = NSEQ * NT + (1 if SAMPLE else 0)
        for _ in range(ntiles):
            wplan.extend(tile_plan())
        wstate = dict(issued=0, used=0)

        NPLAN = len(tile_plan())
        wscr = nc.dram_tensor("wscr", [NPLAN, 128, 4096], BF16, kind="Internal").ap() if WSCRATCH else None

        def w_issue(upto):
            while wstate['issued'] < min(upto, len(wplan)):
                i = wstate['issued']
                wap, r0, nk, c0, ncol = wplan[i]
                s = i % NSLOT
                j = i % NPLAN
                if WSCRATCH and i >= NPLAN:
                    src = wscr[j, :, 0:nk * ncol].rearrange("p (k n) -> p k n", k=nk)
                    A(WLOAD_ENG, lambda e, s=s, nk=nk, ncol=ncol, src=src: e.dma_start(out=wsl[s][:, 0:nk, 0:ncol], in_=src),
                      reads=['wscr%d' % j], writes=['w%d' % s], dma=('wh%d' if WLOAD_ENG == 'sp' else 'w%d') % s)
                else:
                    src = wap[r0:r0 + nk * 128, c0:c0 + ncol].rearrange("(k p) n -> p k n", p=128)
                    A('pool', lambda e, s=s, nk=nk, ncol=ncol, src=src: e.dma_start(out=wsl[s][:, 0:nk, 0:ncol], in_=src),
                      writes=['w%d' % s], dma='w%d' % s)
                    if WSCRATCH and ntiles > 1:
                        dst = wscr[j, :, 0:nk * ncol].rearrange("p (k n) -> p k n", k=nk)
                        A('sp', lambda e, s=s, nk=nk, ncol=ncol, dst=dst: e.dma_start(out=dst, in_=wsl[s][:, 0:nk, 0:ncol]),
                          reads=['w%d' % s], writes=['wscr%d' % j], dma='ws%d' % s)
                wstate['issued'] += 1

        def w_get(wap, c0, prefetch=True):
            i = wstate['used']
            assert wplan[i][0] is wap and wplan[i][3] == c0, (i, c0)
            w_issue(i + NSLOT if prefetch else i + 1)
            wstate['used'] += 1
            s = i % NSLOT
            return wsl[s], 'w%d' % s

        A('sp', lambda e: e.dma_start(out=cst[:], in_=cst_d), writes=['cst'], dma='c0')
        A('pool', lambda e: e.dma_start(out=mk[:].rearrange("p a b -> p (a b)"), in_=mk_d), writes=['mk'], dma='c1')
        A('sp', lambda e: e.dma_start(out=identf[:], in_=mk_d[:, 0:128]), writes=['identf'], dma='c2')
        A('sp', lambda e: e.dma_start(out=tmpE.rearrange("p h t -> p (h t)"), in_=biasx), writes=['tmpE'], dma='c3')
        A('dve', lambda e: e.tensor_tensor(out=tmpE, in0=tmpE, in1=C('ch', 0, 16).unsqueeze(2).to_broadcast([128, 16, 256]),
                                           op=ALU.subtract), reads=['cst', 'tmpE'], writes=['tmpE'])
        A('act', lambda e: e.activation(out=Et[:], in_=tmpE, func=AF.Exp), reads=['tmpE'], writes=['Et'])
        A('dve', lambda e: e.memset(Et[64:128, :, 128:192], 0.0), writes=['Et'])
        A('act', lambda e: e.activation(out=aneg[:], in_=C('alog', 0, 32), func=AF.Exp), reads=['cst'], writes=['aneg'])
        A('dve', lambda e: e.tensor_scalar(out=aneg[:], in0=aneg[:], scalar1=-1.0, scalar2=None, op0=ALU.mult),
          reads=['aneg'], writes=['aneg'])
        A('dve', lambda e: e.memset(dummy[0:1, 1:2], 0.0), reads=['tmpE', 'Et'], writes=['R2_%d' % i for i in range(16)])

        xnb = [xn, scr_b(24576, 1024)]

        def rms_A(b, bs, junk, from_xin=False):
            k = b % 2
            xb = (xin if from_xin else xt)[0:bs, b, :]
            xr = 'xt%d' % b
            sr_ = 'ssb%d' % k
            sc_ = ssb[:, 4 * k:4 * k + 4]
            xrs = ['R2_%d' % (4 * b + j_) for j_ in range(4)] if from_xin else [xr]
            A('act', lambda e: e.activation(out=junk[0:bs, :], in_=xb, func=AF.Square, accum_out=sc_[0:bs, 0:1]),
              reads=xrs, writes=['scr0', sr_])
            A('act', lambda e: e.activation(out=sc_[0:bs, 1:2], in_=sc_[0:bs, 0:1], func=AF.Sqrt, scale=1.0 / D, bias=EPS),
              reads=[sr_], writes=[sr_])
            A('dve', lambda e: e.reciprocal(out=sc_[0:bs, 2:3], in_=sc_[0:bs, 1:2]), reads=[sr_], writes=[sr_])
            A('dve', lambda e: e.tensor_scalar(out=xnb[k][0:bs, :], in0=xb, scalar1=sc_[0:bs, 2:3], scalar2=None, op0=ALU.mult),
              reads=xrs + [sr_], writes=['xn%d' % k])

        def rms_B(b, bs, gname):
            k = b % 2
            pb = 3

            def tr(e):
                ins = None
                for kc in range(8):
                    ins = e.transpose(out=bankb(pb)[:, kc * 128: kc * 128 + bs], in_=xnb[k][0:bs, kc * 128:(kc + 1) * 128],
                                      identity=identb[0:bs, 0:bs])
                return ins
            A('pe', tr, reads=['xn%d' % k, 'mk'], writes=['ps%d' % pb])
            A('dve', lambda e: e.tensor_tensor(
                out=hT[:, :, b * 128: b * 128 + bs],
                in0=bankb(pb).rearrange("p (k t) -> p k t", k=8)[:, :, 0:bs],
                in1=C(gname, 0, 8).unsqueeze(2).to_broadcast([128, 8, bs]), op=ALU.mult),
              reads=['ps%d' % pb, 'cst'], writes=['hT'])

        def rmsnorm_T(nb, bs, gname, junk, from_xin=False):
            for b in range(nb):
                rms_A(b, bs, junk, from_xin)
                if b >= 1:
                    rms_B(b - 1, bs, gname)
            rms_B(nb - 1, bs, gname)

        def mm_fm(e, out, slot, c, srcT, nk, ncol, koff=0, start=True, stop=True):
            ins = None
            for kc in range(nk):
                ins = e.matmul(out, lhsT=slot[:, kc, c * 128:(c + 1) * 128], rhs=srcT[:, koff + kc, 0:ncol],
                               start=(start and kc == 0), stop=(stop and kc == nk - 1))
            return ins

        def mm_tm(e, out, slot, srcT, b, bs, nk, ncols, koff=0, start=True, stop=True):
            ins = None
            for kc in range(nk):
                ins = e.matmul(out, lhsT=srcT[:, koff + kc, b * 128: b * 128 + bs], rhs=slot[:, kc, 0:ncols],
                               start=(start and kc == 0), stop=(stop and kc == nk - 1))
            return ins

        def out_dma(dst, src, reads, key):
            rn = 'out%d' % len(out_res)
            out_res.append(rn)
            A('sp', lambda e: e.dma_start(out=dst, in_=src), reads=reads, writes=[rn], dma=key)

        stgc = [0]

        def stage():
            i = stgc[0] % 2
            stgc[0] += 1
            return stg[:, i, :], 'stg%d' % i

        stopped = [False]

        def chk(x):
            if STOP <= x:
                raise StopBuild()

        tiles = [('p', s_, ti_) for s_ in range(NSEQ) for ti_ in range(NT)] + ([('s', 0, 0)] if SAMPLE else [])

        def emit_xprefetch(tidx):
            kind_, s_, ti_ = tiles[tidx]
            if kind_ == 's':
                A('sp', lambda e: e.dma_start(out=xin[0:32, 0, :], in_=xsm), writes=['R2_%d' % j_ for j_ in range(4)], dma='xin0')
            else:
                for b_ in range(4):
                    src = xp[s_, ti_ * 512 + b_ * 128: ti_ * 512 + (b_ + 1) * 128, :]
                    A('sp', lambda e, b_=b_, src=src: e.dma_start(out=xin[:, b_, :], in_=src),
                      writes=['R2_%d' % (4 * b_ + j_) for j_ in range(4)], dma='xin%d' % b_)

        def do_tile(kind, s, ti):
            if stopped[0]:
                return
            tidx = tiles.index((kind, s, ti))
            sample = (kind == 's')
            bs = 32 if sample else 128
            nb = 1 if sample else 4
            ncol = nb * bs
            first = sample or ti == 0
            last = sample or ti == NT - 1
            t0 = 0 if sample else ti * 512
            cur = 0 if sample else ti % 2
            prv = 1 - cur
            xsrc = xsm if sample else xp[s, t0:t0 + ncol, :]
            ydst = ysm if sample else yp[s, t0:t0 + ncol, :]
            k_dst = ksm if sample else kp[s]
            v_dst = vsm if sample else vp[s]
            kv_out = sample or (SEQ - (t0 + ncol) < LK)
            kv_row0 = 0 if sample else t0 - (SEQ - LK)
            xres = ['xt%d' % b for b in range(nb)]

            def cols(b):
                return slice(b * 128, b * 128 + bs)

            if first:
                if sample:
                    A('sp', lambda e: e.dma_start(out=hst[:], in_=st_ssmT), writes=['hst%d' % g for g in range(4)], dma='c0')
                    A('act', lambda e: e.activation(out=hbf[:], in_=hst[:], func=AF.Copy),
                      reads=['hst%d' % g for g in range(4)], writes=['hbf%d' % g for g in range(4)])
                    A('sp', lambda e: e.dma_start(out=xhalo[:], in_=st_convT), writes=['xhalo%d' % k for k in range(24)], dma='c2')
                    A('sp', lambda e: e.dma_start(out=uhalo[:], in_=st_ffnT), writes=['uhalo%d' % k for k in range(48)], dma='c3')
                    A('pool', lambda e: e.dma_start(out=kTb[1][:], in_=ckT), writes=['kT1_%d' % h for h in range(8)], dma='c1')
                    A('pool', lambda e: e.dma_start(out=vb[1][:], in_=cv.rearrange("(b p) d -> p b d", p=128)),
                      writes=['v1_%d' % b for b in range(4)], dma='c4')
                else:
                    A('dve', lambda e: e.memset(hst[:], 0.0), writes=['hst%d' % g for g in range(4)])
                    A('dve', lambda e: e.memset(hbf[:], 0.0), writes=['hbf%d' % g for g in range(4)])
                    A('dve', lambda e: e.memset(xhalo[:], 0.0), writes=['xhalo%d' % k for k in range(24)])
                    A('dve', lambda e: e.memset(uhalo[:], 0.0), writes=['uhalo%d' % k for k in range(48)])

            chk(0.1)
            if tidx == 0:
                emit_xprefetch(0)
            if sample:
                A('pool', lambda e: e.dma_start(out=pet[:, :, 0:32], in_=psT.rearrange("(c p) t -> p c t", p=128)),
                  writes=['pet'], dma='pet')
            else:
                A('pool', lambda e: e.dma_start(out=pet[:], in_=ppT[s, :, t0:t0 + 512].rearrange("(c p) t -> p c t", p=128)),
                  writes=['pet'], dma='pet')

            chk(0.2)
            phase_barrier()
            junk = scr_b(0, 1024)
            sqb = [scr_b(2048 + i * 1024, 512) for i in range(2)]
            qraw = [scr_f(4096 + i * 2048, 512) for i in range(2)]
            rs = [scr_f(8192 + i * 2048, 512) for i in range(2)]
            knf = scr_f(12288, 512)
            xraw = [scr_f(14336 + i * 2064, 516) for i in range(2)]
            cacc = [scr_f(18464 + i * 2048, 512) for i in range(2)]
            rmsnorm_T(nb, bs, 'gmix', junk, from_xin=True)

            chk(0.3)
            pcount = [0]

            def nextbank():
                i = pcount[0] % 2
                pcount[0] += 1
                return i

            def qk_chunk(slot, wr, c, hp, is_k):
                pb = nextbank()
                i = hp % 2
                A('pe', lambda e: mm_fm(e, bank(pb)[:, 0:ncol], slot, c, hT, 8, ncol), reads=[wr, 'hT'], writes=['ps%d' % pb])
                A('act', lambda e: e.activation(out=sqb[i][:, 0:ncol], in_=bank(pb)[:, 0:ncol], func=AF.Square),
                  reads=['ps%d' % pb], writes=['scr%d' % (1 + i)])
                gcol = C('gk') if is_k else C('gq')
                A('dve', lambda e: e.tensor_scalar(out=qraw[i][:, 0:ncol], in0=bank(pb)[:, 0:ncol], scalar1=gcol, scalar2=None, op0=ALU.mult),
                  reads=['ps%d' % pb, 'cst'], writes=['scr%d' % (3 + i)])
                return lambda: qk_chunk_B(pb, i, hp, is_k)

            def qk_chunk_B(pb, i, hp, is_k):
                A('pe', lambda e: e.matmul(bank(2)[:, 0:ncol], lhsT=blk1, rhs=sqb[i][:, 0:ncol], start=True, stop=True),
                  reads=['scr%d' % (1 + i), 'mk'], writes=['ps2'])
                A('act', lambda e: e.activation(out=rs[i][:, 0:ncol], in_=bank(2)[:, 0:ncol], func=AF.Ln, scale=1.0 / 64, bias=EPS),
                  reads=['ps2'], writes=['scr%d' % (5 + i)])
                A('act', lambda e: e.activation(out=rs[i][:, 0:ncol], in_=rs[i][:, 0:ncol], func=AF.Exp, scale=-0.5),
                  reads=['scr%d' % (5 + i)], writes=['scr%d' % (5 + i)])
                if not is_k:
                    A('dve', lambda e: e.tensor_tensor(out=qT[:, hp, 0:ncol], in0=qraw[i][:, 0:ncol], in1=rs[i][:, 0:ncol], op=ALU.mult),
                      reads=['scr%d' % (3 + i), 'scr%d' % (5 + i)], writes=[r3q(hp)])
                else:
                    A('dve', lambda e: e.tensor_tensor(out=kTb[cur][:, hp, 0:ncol], in0=qraw[i][:, 0:ncol], in1=rs[i][:, 0:ncol], op=ALU.mult),
                      reads=['scr%d' % (3 + i), 'scr%d' % (5 + i)], writes=['kT%d_%d' % (cur, hp)])
                    if kv_out:
                        A('dve', lambda e: e.tensor_tensor(out=knf[:, 0:ncol], in0=qraw[i][:, 0:ncol], in1=rs[i][:, 0:ncol], op=ALU.mult),
                          reads=['scr%d' % (3 + i), 'scr%d' % (5 + i)], writes=['scr7'])
                        sg, sr = stage()

                        def trk(e):
                            ins = None
                            for b in range(nb):
                                ins = e.transpose(out=bank(4)[0:bs, b * 128:(b + 1) * 128], in_=knf[:, cols(b)], identity=identf[:, :])
                            return ins
                        A('pe', trk, reads=['scr7', 'identf'], writes=['ps4'])
                        A('act', lambda e: e.activation(out=sg[0:bs, 0:nb * 128], in_=bank(4)[0:bs, 0:nb * 128], func=AF.Copy),
                          reads=['ps4'], writes=[sr])
                        dst = k_dst[kv_row0:kv_row0 + ncol, hp * 128:(hp + 1) * 128].rearrange("(b p) d -> p b d", p=bs)
                        out_dma(dst, sg[0:bs, 0:nb * 128].rearrange("p (b d) -> p b d", b=nb), [sr], sr)

            pend = []
            for is_k_ in (False, True):
                for blk in range(2):
                    slot, wr = w_get(w_in, (1024 if is_k_ else 0) + 512 * blk)
                    for c in range(4):
                        pB = qk_chunk(slot, wr, c, blk * 4 + c, is_k_)
                        if pend:
                            pend.pop(0)()
                        pend.append(pB)
            while pend:
                pend.pop(0)()
            chk(0.5)
            for blk in range(2):
                slot, wr = w_get(w_in, 2048 + 512 * blk)
                for b in range(nb):
                    pb = nextbank()
                    A('pe', lambda e, b=b, pb=pb, slot=slot: mm_tm(e, bank(pb)[0:bs, :], slot, hT, b, bs, 8, 512), reads=[wr, 'hT'], writes=['ps%d' % pb])
                    A('act', lambda e, b=b, pb=pb, blk=blk: e.activation(out=vb[cur][0:bs, b, blk * 512:(blk + 1) * 512], in_=bank(pb)[0:bs, :], func=AF.Copy),
                      reads=['ps%d' % pb], writes=['v%d_%d' % (cur, b)])
                    if kv_out:
                        sg, sr = stage()
                        A('dve', lambda e, pb=pb, sg=sg: e.tensor_copy(out=sg[0:bs, :], in_=bank(pb)[0:bs, :]), reads=['ps%d' % pb], writes=[sr])
                        r0 = kv_row0 + b * 128
                        out_dma(v_dst[r0:r0 + bs, blk * 512:(blk + 1) * 512], sg[0:bs, :], [sr], sr)
            chk(0.6)
            for b_ in range(nb):
                A('act', lambda e, b_=b_: e.activation(out=xt[0:bs, b_, :], in_=xin[0:bs, b_, :], func=AF.Copy),
                  reads=['R2_%d' % (4 * b_ + j_) for j_ in range(4)], writes=['xt%d' % b_])
            for blk in range(4):
                slot, wr = w_get(w_in, 3072 + 512 * blk)
                for b in range(nb):
                    pb = nextbank()
                    A('pe', lambda e, b=b, pb=pb, slot=slot: mm_tm(e, bank(pb)[0:bs, :], slot, hT, b, bs, 8, 512), reads=[wr, 'hT'], writes=['ps%d' % pb])
                    A('act', lambda e, b=b, pb=pb, blk=blk: e.activation(out=zs[0:bs, b, blk * 512:(blk + 1) * 512], in_=bank(pb)[0:bs, :], func=AF.Silu),
                      reads=['ps%d' % pb], writes=['R2_%d' % (b * 4 + blk)])
            chk(0.7)
            pend = []
            for blk in range(6):
                slot, wr = w_get(w_in, 5120 + 512 * blk)
                for c in range(4):
                    def xbc_A(c=c, cc=blk * 4 + c, slot=slot, wr=wr):
                        i = cc % 2
                        pb = nextbank()
                        xr = 'scr%d' % (8 + i)
                        ar = 'scr%d' % (10 + i)
                        hr = 'xhalo%d' % cc
                        A('pe', lambda e: mm_fm(e, bank(pb)[:, 0:ncol], slot, c, hT, 8, ncol), reads=[wr, 'hT'], writes=['ps%d' % pb])
                        A('act', lambda e: e.activation(out=xraw[i][:, 3:3 + ncol], in_=bank(pb)[:, 0:ncol], func=AF.Copy),
                          reads=['ps%d' % pb], writes=[xr])
                        xh_ = 'xrh%d' % i
                        A('act', lambda e: e.activation(out=cacc[i][:, 0:ncol], in_=bank(pb)[:, 0:ncol], func=AF.Identity,
                                                        scale=C('cw', cc * 4 + 3), bias=C('cb', cc)),
                          reads=['ps%d' % pb, 'cst'], writes=[ar])
                        A('dve', lambda e: e.tensor_copy(out=xraw[i][:, 0:3], in_=xhalo[:, cc, :]), reads=[hr], writes=[xh_])
                        for tap in range(0, 3):
                            A('dve', lambda e, tap=tap: e.scalar_tensor_tensor(
                                out=cacc[i][:, 0:ncol], in0=xraw[i][:, tap:tap + ncol], scalar=C('cw', cc * 4 + tap),
                                in1=cacc[i][:, 0:ncol], op0=ALU.mult, op1=ALU.add), reads=[xr, xh_, ar, 'cst'], writes=[ar])
                        A('act', lambda e: e.activation(out=xhalo[:, cc, :], in_=xraw[i][:, ncol:ncol + 3], func=AF.Copy),
                          reads=[xr], writes=[hr])

                        def xbc_B():
                            A('act', lambda e: e.activation(out=R1[:, cc, 0:ncol], in_=cacc[i][:, 0:ncol], func=AF.Silu),
                              reads=[ar], writes=['R1_%d' % cc])
                        return xbc_B
                    pB = xbc_A()
                    if pend:
                        pend.pop(0)()
                    pend.append(pB)
            while pend:
                pend.pop(0)()
            chk(0.8)
            slot, wr = w_get(w_in, 8192)
            for b in range(nb):
                pb = nextbank()
                A('pe', lambda e, b=b, pb=pb, slot=slot: mm_tm(e, bank(pb)[0:bs, 0:32], slot, hT, b, bs, 8, 32), reads=[wr, 'hT'], writes=['ps%d' % pb])
                A('dve', lambda e, b=b, pb=pb: e.tensor_tensor(out=dtr[0:bs, b, :], in0=bank(pb)[0:bs, 0:32], in1=C('dtb', 0, 32)[0:bs, :], op=ALU.add),
                  reads=['ps%d' % pb, 'cst'], writes=['dtr'])

            if STOP <= 1:
                stopped[0] = True
                return
            LA = 2
            jobs = []
            for qb in range(nb):
                gb = (0 if sample else ti * 4) + qb
                if sample:
                    kl = [(1, t, t, 128) for t in range(4)] + [(0, 0, 4, 32)]
                else:
                    kl = []
                    for t in range(5):
                        kbi = gb - 4 + t
                        if kbi < 0:
                            continue
                        kl.append(((kbi // 4) % 2, kbi % 4, t, 128))
                for hp in range(8):
                    for half in range(2):
                        jobs.append((qb, hp, half, kl))
            nq = bs

            def att_sc(j):
                qb, hp, half, kl = jobs[j]
                si = j % 3
                bA, bB = 2 * si, 2 * si + 1
                p0, p1 = half * 64, half * 64 + 64

                def sc(e):
                    ins = None
                    for (sl, bk, t, nk) in kl:
                        o = bank(bA)[0:nk, t * 128: t * 128 + nq] if t < 4 else bank(bB)[0:nk, 0:nq]
                        ins = e.matmul(o, lhsT=kTb[sl][p0:p1, hp, bk * 128: bk * 128 + nk], rhs=qT[p0:p1, hp, qb * 128: qb * 128 + nq],
                                       start=True, stop=True)
                    return ins
                rds = [r3q(hp)] + ['kT%d_%d' % (sl, hp) for (sl, bk, t, nk) in kl]
                A('pe', sc, reads=rds, writes=['ps%d' % bA, 'ps%d' % bB])

            def att_rest(j):
                qb, hp, half, kl = jobs[j]
                si = j % 3
                h = hp * 2 + half
                bA, bB = 2 * si, 2 * si + 1
                p0, p1 = half * 64, half * 64 + 64
                pp = (j // 2) % 2
                nc0 = pp * 128
                far = [x for x in kl if x[2] < 4]
                if far:
                    tlo = far[0][2]
                    A('act', lambda e: e.activation(
                        out=pT[si][:, tlo * 128:512].rearrange("p (t q) -> p t q", q=128)[:, :, 0:nq],
                        in_=bank(bA)[:, tlo * 128:512].rearrange("p (t q) -> p t q", q=128)[:, :, 0:nq],
                        func=AF.Exp, scale=0.125), reads=['ps%d' % bA], writes=r3pt(si))
                nk4 = kl[-1][3]
                A('act', lambda e: e.activation(out=pT[si][0:nk4, 512:512 + nq], in_=bank(bB)[0:nk4, 0:nq],
                                                func=AF.Exp, scale=0.125), reads=['ps%d' % bB], writes=r3pt(si))
                has3 = any(x[2] == 3 for x in kl)
                if has3 and nk4 == 128:
                    A('dve', lambda e: e.tensor_tensor(
                        out=pT[si][:, 384:640].rearrange("p (t q) -> p t q", q=128)[:, :, 0:nq],
                        in0=pT[si][:, 384:640].rearrange("p (t q) -> p t q", q=128)[:, :, 0:nq],
                        in1=Et[:, h, :].rearrange("p (t q) -> p t q", q=128)[:, :, 0:nq], op=ALU.mult),
                      reads=r3pt(si) + ['Et'], writes=r3pt(si))
                else:
                    if has3:
                        A('dve', lambda e: e.tensor_tensor(out=pT[si][:, 384:384 + nq], in0=pT[si][:, 384:384 + nq],
                                                           in1=Et[:, h, 0:nq], op=ALU.mult),
                          reads=r3pt(si) + ['Et'], writes=r3pt(si))
                    A('dve', lambda e: e.tensor_tensor(out=pT[si][0:nk4, 512:512 + nq], in0=pT[si][0:nk4, 512:512 + nq],
                                                       in1=Et[0:nk4, h, 128:128 + nq], op=ALU.mult),
                      reads=r3pt(si) + ['Et'], writes=r3pt(si))
                if any(x[2] == 0 for x in kl) and nq > 64:
                    A('dve', lambda e: e.memset(pT[si][0:64, 64:128], 0.0), reads=r3pt(si), writes=r3pt(si))

                def pv(e):
                    ins = None
                    n = len(kl)
                    for jj, (sl, bk, t, nk) in enumerate(kl):
                        e.matmul(bank(6 + pp)[p0:p1, 0:nq], lhsT=vb[sl][0:nk, bk, h * 64:(h + 1) * 64], rhs=pT[si][0:nk, t * 128: t * 128 + nq],
                                 start=(jj == 0), stop=(jj == n - 1), skip_group_check=True)
                        ins = e.matmul(bank(6 + pp)[p0:p1, 128:128 + nq], lhsT=onesb[0:nk, 0:64], rhs=pT[si][0:nk, t * 128: t * 128 + nq],
                                       start=False, stop=(jj == n - 1), skip_group_check=True)
                    return ins
                A('pe', pv, reads=r3pt(si) + ['mk'] + ['v%d_%d' % (sl, bk) for (sl, bk, t, nk) in kl],
                  writes=['ps%d' % (6 + pp)])
                if half == 1:
                    fins.append(lambda: att_fin(qb, hp, pp, nc0))

            def att_fin(qb, hp, pp, nc0):
                if True:
                    ri = pp
                    A('act', lambda e: e.activation(out=rden[ri][:, 0:nq], in_=bank(6 + pp)[:, 128:128 + nq], func=AF.Ln),
                      reads=['ps%d' % (6 + pp)], writes=r3rd(ri))
                    A('act', lambda e: e.activation(out=rden[ri][:, 0:nq], in_=rden[ri][:, 0:nq], func=AF.Exp, scale=-1.0),
                      reads=r3rd(ri), writes=r3rd(ri))
                    A('dve', lambda e: e.tensor_tensor(out=yaT[:, hp, qb * 128: qb * 128 + nq], in0=bank(6 + pp)[:, 0:nq],
                                                       in1=rden[ri][:, 0:nq], op=ALU.mult),
                      reads=['ps%d' % (6 + pp)] + r3rd(ri), writes=['yaT'])

            nj = len(jobs)
            fins = []
            for j in range(min(LA, nj)):
                att_sc(j)
            for j in range(nj):
                if j + LA < nj:
                    att_sc(j + LA)
                npend = len(fins)
                att_rest(j)
                for _ in range(npend):
                    fins.pop(0)()
            while fins:
                fins.pop(0)()

            if STOP <= 2:
                stopped[0] = True
                return
            tl = tcount[0]
            tcount[0] += 1
            if DBG:
                out_dma(d_ya[tl], yaT[:].rearrange("p a b -> p (a b)"), ['yaT'], 'dbg')
            phase_barrier()
            rhs_hi = [scr_b(0, 1024), rhs_x[:, 0:1024]]
            rhs_lo = [scr_b(2048, 1024), rhs_x[:, 1024:2048]]
            dec = [scr_b(4096 + i * 2048, 1024) for i in range(2)]
            dxb = [scr_b(8192 + i * 1024, 512) for i in range(2)]
            xsd = [scr_b(10240 + i * 1024, 512) for i in range(2)]
            xwb = [scr_b(12288 + i * 1024, 512) for i in range(2)]
            bm_tm = scr_b(14336, 512)
            cbm = scr_b(15360, 512)
            yg = scr_b(16384, 2048)
            tmpS = scr_f(20480, 512)
            yv = scr_f(22528, 512)
            sm = scr_f(24576, 512)
            exo_all = sm[:, 0:384].rearrange("p (b c) -> p b c", c=96)
            a_hi_all = sm[:, 384:448].bitcast(BF16)
            a_lo_all = sm[:, 448:512].bitcast(BF16)
            ssy = ssb[:, 8:16]
            nt = bs
            ex1v = yv[0:nt, 0:nb * 32].rearrange("p (b c) -> p b c", c=32)
            a_fv = tmpS[0:nt, 0:nb * 32].rearrange("p (b c) -> p b c", c=32)
            A('act', lambda e: e.activation(out=ex1v, in_=dtr[0:nt, 0:nb, :], func=AF.Exp), reads=['dtr'], writes=['yv'])
            A('act', lambda e: e.activation(out=dtr[0:nt, 0:nb, :], in_=ex1v, func=AF.Ln, bias=1.0), reads=['yv'], writes=['sm1', 'dtr'])
            A('dve', lambda e: e.tensor_tensor(out=a_fv, in0=dtr[0:nt, 0:nb, :], in1=aneg[0:nt, :].unsqueeze(1).to_broadcast([nt, nb, 32]), op=ALU.mult),
              reads=['sm1', 'aneg'], writes=['tmpS'])
            A('dve', lambda e: e.tensor_copy(out=a_hi_all[0:nt, 0:nb * 32], in_=tmpS[0:nt, 0:nb * 32]), reads=['tmpS'], writes=['sm3'])
            A('dve', lambda e: e.tensor_tensor(out=a_lo_all[0:nt, 0:nb * 32], in0=tmpS[0:nt, 0:nb * 32], in1=a_hi_all[0:nt, 0:nb * 32], op=ALU.subtract),
              reads=['tmpS', 'sm3'], writes=['sm4'])

            def cums(e):
                ins = None
                for b in range(nb):
                    for (m, c0, M) in ((Vm, 0, nt), (Um, 32, nt), (onesb, 64, 128)):
                        o = bank(0)[0:M, b * 96 + c0: b * 96 + c0 + 32]
                        e.matmul(o, lhsT=m[0:nt, 0:M], rhs=a_hi_all[0:nt, b * 32:(b + 1) * 32], start=True, stop=False, skip_group_check=True)
                        ins = e.matmul(o, lhsT=m[0:nt, 0:M], rhs=a_lo_all[0:nt, b * 32:(b + 1) * 32], start=False, stop=True, skip_group_check=True)
                return ins
            A('pe', cums, reads=['sm3', 'sm4', 'mk'], writes=['ps0'])
            A('act', lambda e: e.activation(out=sm[0:nt, 0:nb * 96], in_=bank(0)[0:nt, 0:nb * 96], func=AF.Exp), reads=['ps0'], writes=['sm5', 'sm6'])
            if nt < 128:
                A('act', lambda e: e.activation(out=exo_all[:, 0:nb, 64:96], in_=bank(0)[:, 0:nb * 96].rearrange("p (b c) -> p b c", c=96)[:, :, 64:96],
                                                func=AF.Exp), reads=['ps0'], writes=['sm6'])
            A('dve', lambda e: e.tensor_tensor(out=exo_all[0:nt, 0:nb, 32:64], in0=exo_all[0:nt, 0:nb, 32:64], in1=dtr[0:nt, 0:nb, :], op=ALU.mult),
              reads=['sm5', 'sm1'], writes=['sm7', 'sm5'])
            blkf = []
            for b in range(nb):
                cb_ = cols(b)
                dt_ = dtr[:, b, :]
                exo = exo_all[:, b, :]
                w2 = exo_all[:, b, 32:64]
                a_hi = a_hi_all[:, b * 32:(b + 1) * 32]
                a_lo = a_lo_all[:, b * 32:(b + 1) * 32]
                def ssd_pre(b=b, cb_=cb_):

                    def cbf(e, cb_=cb_):
                        ins = None
                        for g in range(4):
                            ins = e.matmul(bank(3)[0:nt, g * 128: g * 128 + nt], lhsT=R1[:, 16 + g, cb_], rhs=R1[:, 20 + g, cb_], start=True, stop=True)
                        return ins
                    A('pe', cbf, reads=['R1_%d' % c for c in range(16, 24)], writes=['ps3'])
                    A('dve', lambda e: e.tensor_tensor(out=cbm[0:nt, :].rearrange("p (g i) -> p g i", g=4)[:, :, 0:nt],
                                                       in0=bank(3)[0:nt, :].rearrange("p (g i) -> p g i", g=4)[:, :, 0:nt],
                                                       in1=Vm[0:nt, 0:nt].unsqueeze(1).to_broadcast([nt, 4, nt]), op=ALU.mult),
                      reads=['ps3', 'mk'], writes=['cbm'])

                    def trb(e, cb_=cb_):
                        ins = None
                        for g in range(4):
                            ins = e.transpose(out=bankb(4)[0:nt, g * 128:(g + 1) * 128], in_=R1[:, 16 + g, cb_], identity=identb)
                        return ins
                    A('pe', trb, reads=['R1_%d' % c for c in range(16, 20)] + ['mk'], writes=['ps4'])
                    A('act', lambda e: e.activation(out=bm_tm[0:nt, :], in_=bankb(4)[0:nt, 0:512], func=AF.Copy), reads=['ps4'], writes=['bm_tm'])
                def ssd_stage1(g, b=b, cb_=cb_, dt_=dt_, exo=exo, w2=w2, a_hi=a_hi, a_lo=a_lo):
                    i = g % 2
                    hs = slice(8 * g, 8 * g + 8)
                    W8 = 8 * nt
                    sb0, sb1 = ((1, 2), (0, 3))[i]
                    A(RHS_ENG, lambda e: e.tensor_tensor(out=rhs_hi[i][0:nt, 0:8 * nt].rearrange("p (h i) -> p h i", h=8),
                                                         in0=Vm[0:nt, 0:nt].unsqueeze(1).to_broadcast([nt, 8, nt]),
                                                         in1=a_hi[0:nt, hs].unsqueeze(2).to_broadcast([nt, 8, nt]), op=ALU.mult),
                      reads=['sm3', 'mk'], writes=['rhs_hi%d' % i])
                    A(RHS_ENG, lambda e: e.tensor_tensor(out=rhs_lo[i][0:nt, 0:8 * nt].rearrange("p (h i) -> p h i", h=8),
                                                         in0=Vm[0:nt, 0:nt].unsqueeze(1).to_broadcast([nt, 8, nt]),
                                                         in1=a_lo[0:nt, hs].unsqueeze(2).to_broadcast([nt, 8, nt]), op=ALU.mult),
                      reads=['sm4', 'mk'], writes=['rhs_lo%d' % i])

                    def segf(e):
                        ins = None
                        for k_, c0 in enumerate(range(0, W8, 512)):
                            c1 = min(W8, c0 + 512)
                            o = bank((sb0, sb1)[k_])[0:nt, 0:c1 - c0]
                            e.matmul(o, lhsT=Um[0:nt, 0:nt], rhs=rhs_hi[i][0:nt, c0:c1], start=True, stop=False)
                            ins = e.matmul(o, lhsT=Um[0:nt, 0:nt], rhs=rhs_lo[i][0:nt, c0:c1], start=False, stop=True)
                        return ins
                    A('pe', segf, reads=['rhs_hi%d' % i, 'rhs_lo%d' % i, 'mk'], writes=['ps%d' % sb0, 'ps%d' % sb1])
                    for k_, c0 in enumerate(range(0, W8, 512)):
                        c1 = min(W8, c0 + 512)
                        bk_ = (sb0, sb1)[k_]
                        A('act', lambda e, c0=c0, c1=c1, bk_=bk_: e.activation(out=dec[i][0:nt, c0:c1], in_=bank(bk_)[0:nt, 0:c1 - c0], func=AF.Exp),
                          reads=['ps%d' % bk_], writes=['dec%d' % i])

                    def trx(e):
                        ins = None
                        for j in range(4):
                            ins = e.transpose(out=bankb(4)[0:nt, i * 512 + j * 128: i * 512 + (j + 1) * 128], in_=R1[:, 4 * g + j, cb_], identity=identb)
                        return ins
                    A('pe', trx, reads=['R1_%d' % (4 * g + j) for j in range(4)] + ['mk'], writes=['ps4'])
                    xsv = bankb(4)[0:nt, i * 512:(i + 1) * 512].rearrange("p (h d) -> p h d", h=8)
                    for (dst, nm, sc_ap, rr) in ((dxb[i], 'dx%d' % i, dt_[0:nt, hs], 'sm1'), (xsd[i], 'xsd%d' % i, C('dskip', 0, 32)[0:nt, hs], 'cst'),
                                                 (xwb[i], 'xw%d' % i, w2[0:nt, hs], 'sm7')):
                        A('dve', lambda e, dst=dst, sc_ap=sc_ap: e.tensor_tensor(
                            out=dst[0:nt, :].rearrange("p (h d) -> p h d", h=8), in0=xsv,
                            in1=sc_ap.unsqueeze(2).to_broadcast([nt, 8, 64]), op=ALU.mult), reads=['ps4', rr], writes=[nm])
                    A('dve', lambda e: e.tensor_tensor(
                        out=dec[i][0:nt, 0:W8].rearrange("p (h i) -> p h i", h=8), in0=dec[i][0:nt, 0:W8].rearrange("p (h i) -> p h i", h=8),
                        in1=cbm[0:nt, g * 128: g * 128 + nt].unsqueeze(1).to_broadcast([nt, 8, nt]), op=ALU.mult),
                      reads=['dec%d' % i, 'cbm'], writes=['dec%d' % i])

                def ssd_stage2(g, b=b, cb_=cb_, dt_=dt_, exo=exo, w2=w2, a_hi=a_hi, a_lo=a_lo):
                    i = g % 2
                    hs = slice(8 * g, 8 * g + 8)

                    def yf(e):
                        e.matmul(bank(5)[0:nt, :], lhsT=identb[0:nt, 0:nt], rhs=xsd[i][0:nt, :], start=True, stop=False, skip_group_check=True)
                        ins = None
                        for hh in range(8):
                            ins = e.matmul(bank(5)[0:nt, hh * 64:(hh + 1) * 64], lhsT=dec[i][0:nt, hh * nt:(hh + 1) * nt],
                                           rhs=dxb[i][0:nt, hh * 64:(hh + 1) * 64], start=False, stop=(hh == 7), skip_group_check=True)
                        return ins
                    A('pe', yf, reads=['dec%d' % i, 'dx%d' % i, 'xsd%d' % i, 'mk'], writes=['ps5'])
                    A('pe', lambda e: e.matmul(bank(6)[0:nt, :], lhsT=R1[:, 20 + g, cb_], rhs=hbf[:, g * 512:(g + 1) * 512], start=True, stop=True),
                      reads=['R1_%d' % (20 + g), 'hbf%d' % g], writes=['ps6'])
                    A('pe', lambda e: e.matmul(bank(7)[:, :], lhsT=bm_tm[0:nt, g * 128:(g + 1) * 128], rhs=xwb[i][0:nt, :], start=True, stop=True),
                      reads=['bm_tm', 'xw%d' % i], writes=['ps7'])
                    A('dve', lambda e: e.tensor_tensor(out=tmpS[0:nt, :].rearrange("p (h d) -> p h d", h=8),
                                                       in0=bank(6)[0:nt, :].rearrange("p (h d) -> p h d", h=8),
                                                       in1=exo[0:nt, hs].unsqueeze(2).to_broadcast([nt, 8, 64]), op=ALU.mult),
                      reads=['ps6', 'sm5'], writes=['tmpS'])
                    A('dve', lambda e: e.tensor_tensor(out=yv[0:nt, :], in0=bank(5)[0:nt, :], in1=tmpS[0:nt, :], op=ALU.add),
                      reads=['ps5', 'tmpS'], writes=['yv'])
                    A('dve', lambda e: e.tensor_tensor(out=yv[0:nt, :], in0=yv[0:nt, :], in1=zs[0:nt, b, g * 512:(g + 1) * 512], op=ALU.mult),
                      reads=['yv', 'R2_%d' % (b * 4 + g)], writes=['yv'])
                    A('act', lambda e: e.activation(out=yg[0:nt, g * 512:(g + 1) * 512], in_=yv[0:nt, :], func=AF.Copy),
                      reads=['yv'], writes=['yg'])
                    A('act', lambda e: e.activation(out=tmpS[0:nt, :], in_=yv[0:nt, :], func=AF.Square, accum_out=ssy[0:nt, g:g + 1]),
                      reads=['yv', 'tmpS'], writes=['tmpS', 'ssy'])
                    hv = hst[:, g * 512:(g + 1) * 512]
                    A('dve', lambda e: e.tensor_tensor(out=hv.rearrange("p (h d) -> p h d", h=8), in0=hv.rearrange("p (h d) -> p h d", h=8),
                                                         in1=exo[:, 64:96][:, hs].unsqueeze(2).to_broadcast([128, 8, 64]), op=ALU.mult),
                      reads=['hst%d' % g, 'sm6'], writes=['hst%d' % g])
                    A('dve', lambda e: e.tensor_tensor(out=hv, in0=hv, in1=bank(7)[:, :], op=ALU.add), reads=['hst%d' % g, 'ps7'], writes=['hst%d' % g])
                    A('act', lambda e: e.activation(out=hbf[:, g * 512:(g + 1) * 512], in_=hv, func=AF.Copy),
                      reads=['hst%d' % g], writes=['hbf%d' % g])

                def ssd_post(b=b, cb_=cb_):
                    A('dve', lambda e: e.reduce_sum(out=ssy[0:nt, 4:5], in_=ssy[0:nt, 0:4], axis=AX.X), reads=['ssy'], writes=['ssy'])
                    A('act', lambda e: e.activation(out=ssy[0:nt, 5:6], in_=ssy[0:nt, 4:5], func=AF.Ln, scale=1.0 / 2048, bias=EPS), reads=['ssy'], writes=['ssy'])
                    A('act', lambda e: e.activation(out=ssy[0:nt, 6:7], in_=ssy[0:nt, 5:6], func=AF.Exp, scale=-0.5), reads=['ssy'], writes=['ssy'])
                    A('dve', lambda e: e.tensor_scalar(out=dg[0:nt, 0:nt], in0=identb[0:nt, 0:nt], scalar1=ssy[0:nt, 6:7], scalar2=None, op0=ALU.mult),
                      reads=['ssy', 'mk'], writes=['dg'])
                    for j in range(4):
                        bk_ = 5 + (j % 3)

                        def tryb(e, j=j, bk_=bk_):
                            ins = None
                            for c in range(4):
                                cc = 4 * j + c
                                ins = e.matmul(bank(bk_)[:, c * 128: c * 128 + nt], lhsT=yg[0:nt, cc * 128:(cc + 1) * 128], rhs=dg[0:nt, 0:nt],
                                               start=True, stop=True)
                            return ins
                        A('pe', tryb, reads=['yg', 'dg'], writes=['ps%d' % bk_])
                        A('dve', lambda e, j=j, bk_=bk_: e.tensor_tensor(out=ybT[:, 4 * j:4 * j + 4, cb_],
                                                                        in0=bank(bk_).rearrange("p (k t) -> p k t", k=4)[:, :, 0:nt],
                                                                        in1=C('gssm', 4 * j, 4).unsqueeze(2).to_broadcast([128, 4, nt]), op=ALU.mult),
                          reads=['ps%d' % bk_, 'cst'], writes=['R3_%d' % c for c in range(4 * j, 4 * j + 4)])

                blkf.append((ssd_pre, ssd_stage1, ssd_stage2, ssd_post))

            blkf[0][0]()
            blkf[0][1](0)
            blkf[0][1](1)
            for b in range(nb):
                pre_, s1_, s2_, post_ = blkf[b]
                s2_(0)
                s1_(2)
                s2_(1)
                s1_(3)
                s2_(2)
                s2_(3)
                if b + 1 < nb:
                    blkf[b + 1][0]()
                    blkf[b + 1][1](0)
                    blkf[b + 1][1](1)
                post_()

            if STOP <= 3:
                stopped[0] = True
                return
            if DBG:
                out_dma(d_yb[tl], R3[:, :], ['R3_%d' % k for k in range(16)], 'dbg')
            phase_barrier()
            tmpB = [scr_f(i * 2048, 512) for i in range(2)]
            for blk in range(4):
                slot, wr = w_get(w_in, 8224 + 512 * blk)
                for c in range(4):
                    cc = blk * 4 + c
                    pb = nextbank()
                    A('pe', lambda e, c=c, pb=pb, slot=slot: mm_fm(e, bank(pb)[:, 0:ncol], slot, c, hT, 8, ncol), reads=[wr, 'hT'], writes=['ps%d' % pb])
                    A('act', lambda e, cc=cc, pb=pb: e.activation(out=R1[:, cc, 0:ncol], in_=bank(pb)[:, 0:ncol], func=AF.Sigmoid, bias=C('bgate', cc)),
                      reads=['ps%d' % pb, 'cst'], writes=['R1_%d' % cc])
            for blk in range(2):
                slot, wr = w_get(w_pa, 512 * blk)
                for c in range(4):
                    cc = blk * 4 + c
                    pb = nextbank()
                    A('pe', lambda e, c=c, pb=pb, slot=slot: mm_fm(e, bank(pb)[:, 0:ncol], slot, c, yaT, 8, ncol), reads=[wr, 'yaT'], writes=['ps%d' % pb])
                    A('dve', lambda e, cc=cc, pb=pb: e.tensor_tensor(out=t1[:, cc, 0:ncol], in0=bank(pb)[:, 0:ncol], in1=R1[:, cc, 0:ncol], op=ALU.mult),
                      reads=['ps%d' % pb, 'R1_%d' % cc], writes=['R2_%d' % (2 * cc), 'R2_%d' % (2 * cc + 1)])
            for blk in range(2):
                r3n = ['R3_%d' % k for k in range(16)]
                slot0, wr0 = w_get(w_pb, 512 * blk)
                for c in range(4):
                    A('pe', lambda e, c=c, slot0=slot0: mm_fm(e, bank(4 + c)[:, 0:ncol], slot0, c, ybT, 8, ncol, koff=0, start=True, stop=False),
                      reads=[wr0] + r3n, writes=['ps%d' % (4 + c)])
                slot1, wr1 = w_get(w_pb, 512 * blk)
                for c in range(4):
                    A('pe', lambda e, c=c, slot1=slot1: mm_fm(e, bank(4 + c)[:, 0:ncol], slot1, c, ybT, 8, ncol, koff=8, start=False, stop=True),
                      reads=[wr1] + r3n, writes=['ps%d' % (4 + c)])
                for c in range(4):
                    cc = blk * 4 + c
                    i = cc % 2
                    pb = 4 + c
                    A('dve', lambda e, cc=cc, pb=pb, i=i: e.tensor_tensor(out=tmpB[i][:, 0:ncol], in0=bank(pb)[:, 0:ncol], in1=R1[:, 8 + cc, 0:ncol], op=ALU.mult),
                      reads=['ps%d' % pb, 'R1_%d' % (8 + cc)], writes=['scr%d' % i])
                    A('dve', lambda e, cc=cc, i=i: e.tensor_tensor(out=mT[:, cc, 0:ncol], in0=t1[:, cc, 0:ncol], in1=tmpB[i][:, 0:ncol], op=ALU.add),
                      reads=['scr%d' % i, 'R2_%d' % (2 * cc), 'R2_%d' % (2 * cc + 1)], writes=['yaT'])
            if DBG:
                out_dma(d_m[tl], yaT[:].rearrange("p a b -> p (a b)"), ['yaT'], 'dbg')
                out_dma(d_g[tl], R1[:, 0:16, :].rearrange("p a b -> p (a b)"), ['R1_%d' % k for k in range(16)], 'dbg')
            for half in range(2):
                slot, wr = w_get(w_out, 512 * half)
                for b in range(nb):
                    pb = nextbank()
                    A('pe', lambda e, b=b, pb=pb, slot=slot: mm_tm(e, bank(pb)[0:bs, :], slot, mT, b, bs, 8, 512), reads=[wr, 'yaT'], writes=['ps%d' % pb])
                    A('dve', lambda e, b=b, pb=pb, half=half: e.tensor_tensor(out=xt[0:bs, b, half * 512:(half + 1) * 512], in0=xt[0:bs, b, half * 512:(half + 1) * 512],
                                                                   in1=bank(pb)[0:bs, :], op=ALU.add),
                      reads=['ps%d' % pb, 'xt%d' % b], writes=['xt%d' % b])
                    if half == 1:
                        if b >= 2:
                            rms_B(b - 2, bs, 'gffn')
                        rms_A(b, bs, scr_b(0, 1024))
                if half == 1:
                    for b in range(max(0, nb - 2), nb):
                        rms_B(b, bs, 'gffn')

            if STOP <= 4:
                stopped[0] = True
                return
            if DBG:
                out_dma(d_x[tl], xt[:].rearrange("p a b -> p (a b)"), xres, 'dbg')
            if tidx + 1 < len(tiles):
                emit_xprefetch(tidx + 1)
            phase_barrier()
            junk = scr_b(0, 1024)
            ugr = [scr_f(2048 + i * 2064, 516) for i in range(2)]
            uvr = [scr_f(6176 + i * 2064, 516) for i in range(2)]
            cga = [scr_f(10304 + k * 2048, 512) for k in range(2)]
            cva = [scr_f(14400 + k * 2048, 512) for k in range(2)]
            gl = [scr_f(18496 + k * 2048, 512) for k in range(2)]
            pend = []
            for ub in range(6):
                slotg, wrg = w_get(w_up, 512 * ub)
                slotv, wrv = w_get(w_up, 3072 + 512 * ub, prefetch=False)
                for c in range(4):
                    def up_A(c=c, cg_=ub * 4 + c, slotg=slotg, wrg=wrg, slotv=slotv, wrv=wrv):
                        i = cg_ % 2
                        for (slot, wr, raw, rn, chn, acc, an, pb) in ((slotg, wrg, ugr[i], 'scr%d' % (1 + i), cg_, cga[i], 'scr%d' % (5 + i), 2 * i),
                                                                       (slotv, wrv, uvr[i], 'scr%d' % (3 + i), 24 + cg_, cva[i], 'scr%d' % (7 + i), 2 * i + 1)):
                            hr = 'uhalo%d' % chn
                            A('pe', lambda e, pb=pb, slot=slot: mm_fm(e, bank(pb)[:, 0:ncol], slot, c, hT, 8, ncol), reads=[wr, 'hT'], writes=['ps%d' % pb])
                            A('act', lambda e, raw=raw, pb=pb: e.activation(out=raw[:, 2:2 + ncol], in_=bank(pb)[:, 0:ncol], func=AF.Copy),
                              reads=['ps%d' % pb], writes=[rn])
                            rh_ = rn + 'h'
                            A('act', lambda e, chn=chn, acc=acc, pb=pb: e.activation(out=acc[:, 0:ncol], in_=bank(pb)[:, 0:ncol], func=AF.Identity,
                                                                                    scale=C('fw', chn * 3 + 2), bias=C('fb', chn)),
                              reads=['ps%d' % pb, 'cst'], writes=[an])
                            A('dve', lambda e, raw=raw, chn=chn: e.tensor_copy(out=raw[:, 0:2], in_=uhalo[:, chn, :]), reads=[hr], writes=[rh_])
                            for tap in range(0, 2):
                                A('dve', lambda e, raw=raw, chn=chn, acc=acc, tap=tap: e.scalar_tensor_tensor(
                                    out=acc[:, 0:ncol], in0=raw[:, tap:tap + ncol], scalar=C('fw', chn * 3 + tap), in1=acc[:, 0:ncol],
                                    op0=ALU.mult, op1=ALU.add), reads=[rn, rh_, an, 'cst'], writes=[an])
                            A('act', lambda e, raw=raw, chn=chn: e.activation(out=uhalo[:, chn, :], in_=raw[:, ncol:ncol + 2], func=AF.Copy),
                              reads=[rn], writes=[hr])

                        def up_B():
                            A('act', lambda e: e.activation(out=gl[i][:, 0:ncol], in_=cga[i][:, 0:ncol], func=AF.Gelu_apprx_tanh),
                              reads=['scr%d' % (5 + i)], writes=['scr%d' % (9 + i)])
                            A('dve', lambda e: e.tensor_tensor(out=R1[:, cg_, 0:ncol], in0=gl[i][:, 0:ncol], in1=cva[i][:, 0:ncol], op=ALU.mult),
                              reads=['scr%d' % (9 + i), 'scr%d' % (7 + i)], writes=['R1_%d' % cg_])
                        return up_B
                    pB = up_A()
                    if pend:
                        pend.pop(0)()
                    pend.append(pB)
            while pend:
                pend.pop(0)()
            for half in range(2):
                for kg in range(3):
                    slot, wr = w_get(w_down, 512 * half)
                    for b in range(nb):
                        A('pe', lambda e, b=b, slot=slot, kg=kg: mm_tm(e, bank(4 + b)[0:bs, :], slot, R1, b, bs, 8, 512, koff=8 * kg,
                                                                        start=(kg == 0), stop=(kg == 2)),
                          reads=[wr] + ['R1_%d' % k for k in range(8 * kg, 8 * kg + 8)], writes=['ps%d' % (4 + b)])
                for b in range(nb):
                    A('dve', lambda e, b=b, half=half: e.tensor_tensor(out=xt[0:bs, b, half * 512:(half + 1) * 512], in0=xt[0:bs, b, half * 512:(half + 1) * 512],
                                                                       in1=bank(4 + b)[0:bs, :], op=ALU.add),
                      reads=['ps%d' % (4 + b), 'xt%d' % b], writes=['xt%d' % b])
                    if half == 1:
                        if b >= 2:
                            rms_B(b - 2, bs, 'gple')
                        rms_A(b, bs, scr_b(0, 1024))
                if half == 1:
                    for b in range(max(0, nb - 2), nb):
                        rms_B(b, bs, 'gple')

            if STOP <= 5:
                stopped[0] = True
                return
            if DBG:
                out_dma(d_x2[tl], xt[:].rearrange("p a b -> p (a b)"), xres, 'dbg')
            phase_barrier()
            junk = scr_b(0, 1024)
            sgt = [scr_f(2048 + i * 2048, 512) for i in range(2)]
            for half in range(2):
                slotg, wrg = w_get(w_pg, 512 * half)
                for b in range(nb):
                    A('pe', lambda e, b=b, slotg=slotg: mm_tm(e, bank(b)[0:bs, :], slotg, hT, b, bs, 8, 512), reads=[wrg, 'hT'], writes=['ps%d' % b])
                slotp, wrp = w_get(w_ple, 512 * half)
                for b in range(nb):
                    A('pe', lambda e, b=b, slotp=slotp: mm_tm(e, bank(4 + b)[0:bs, :], slotp, pet, b, bs, 2, 512), reads=[wrp, 'pet'], writes=['ps%d' % (4 + b)])
                for b in range(nb):
                    i = b % 2
                    A('act', lambda e, i=i, b=b: e.activation(out=sgt[i][0:bs, :], in_=bank(b)[0:bs, :], func=AF.Sigmoid), reads=['ps%d' % b], writes=['scr%d' % (1 + i)])
                    A('dve', lambda e, i=i, b=b: e.tensor_tensor(out=sgt[i][0:bs, :], in0=sgt[i][0:bs, :], in1=bank(4 + b)[0:bs, :], op=ALU.mult),
                      reads=['ps%d' % (4 + b), 'scr%d' % (1 + i)], writes=['scr%d' % (1 + i)])
                    A('dve', lambda e, b=b, i=i, half=half: e.tensor_tensor(out=xt[0:bs, b, half * 512:(half + 1) * 512],
                                                                            in0=xt[0:bs, b, half * 512:(half + 1) * 512], in1=sgt[i][0:bs, :], op=ALU.add),
                      reads=['scr%d' % (1 + i), 'xt%d' % b], writes=['xt%d' % b])
                    if half == 1 and not sample:
                        out_dma(ydst[b * 128:(b + 1) * 128, :], xt[:, b, :], ['xt%d' % b], 'yout%d' % b)
            if sample:
                out_dma(ydst, xt[0:32, 0, :], xres, 'yout0')

            if last:
                sd = ssms[0] if sample else ssmp[s]
                for q in range(4):
                    def trs(e, q=q):
                        ins = None
                        for j in range(4):
                            c = 4 * q + j
                            ins = e.transpose(out=bank(0)[:, j * 128:(j + 1) * 128], in_=hst[:, c * 128:(c + 1) * 128], identity=identf[:, :])
                        return ins
                    A('pe', trs, reads=['hst%d' % q, 'identf'], writes=['ps0'])
                    sg, sr = stage()
                    A('act', lambda e, sg=sg: e.activation(out=sg[:, :], in_=bank(0)[:, :], func=AF.Copy), reads=['ps0'], writes=[sr])
                    out_dma(sd[q * 512:(q + 1) * 512, :].rearrange("(j p) n -> p j n", p=128), sg[:, :].rearrange("p (j n) -> p j n", j=4), [sr], sr)
                cd = cssms[0] if sample else cssmp[s]
                for q in range(6):
                    def trc(e, q=q):
                        ins = None
                        for j in range(4):
                            ins = e.transpose(out=bank(1)[0:3, j * 128:(j + 1) * 128], in_=xhalo[:, 4 * q + j, :], identity=identf[:, :])
                        return ins
                    A('pe', trc, reads=['xhalo%d' % (4 * q + j) for j in range(4)] + ['identf'], writes=['ps1'])
                    sg, sr = stage()
                    A('act', lambda e, sg=sg: e.activation(out=sg[0:3, :], in_=bank(1)[0:3, :], func=AF.Copy), reads=['ps1'], writes=[sr])
                    out_dma(cd[:, q * 512:(q + 1) * 512], sg[0:3, :], [sr], sr)
                fd = cffns[0] if sample else cffnp[s]
                for q in range(12):
                    def trf(e, q=q):
                        ins = None
                        for j in range(4):
                            ins = e.transpose(out=bank(1)[0:2, j * 128:(j + 1) * 128], in_=uhalo[:, 4 * q + j, :], identity=identf[:, :])
                        return ins
                    A('pe', trf, reads=['uhalo%d' % (4 * q + j) for j in range(4)] + ['identf'], writes=['ps1'])
                    sg, sr = stage()
                    A('act', lambda e, sg=sg: e.activation(out=sg[0:2, :], in_=bank(1)[0:2, :], func=AF.Copy), reads=['ps1'], writes=[sr])
                    out_dma(fd[:, q * 512:(q + 1) * 512], sg[0:2, :], [sr], sr)

        try:
            chk(0.05)
            for s in range(NSEQ):
                for ti in range(NT):
                    do_tile('p', s, ti)
            if SAMPLE:
                do_tile('s', 0, 0)
        except StopBuild:
            pass
        A('sp', None, reads=list(out_res))
        S.emit(nc, st)
    return nc


def _consts(g_mix, g_ffn, g_ple, g_q, g_k, b_gate, conv_ssm_w, conv_ssm_b, ffn_conv_w, ffn_conv_b, dt_bias, a_log, d_skip, g_ssm, rel_bias):
    cst = np.zeros((128, NCST), np.float32)

    def fm(v, nch):
        return np.ascontiguousarray(np.asarray(v, np.float32).reshape(nch, 128).T)

    cst[:, _off['gmix']:_off['gmix'] + 8] = fm(g_mix, 8)
    cst[:, _off['gffn']:_off['gffn'] + 8] = fm(g_ffn, 8)
    cst[:, _off['gple']:_off['gple'] + 8] = fm(g_ple, 8)
    cst[:, _off['gq']] = np.tile(np.asarray(g_q, np.float32), 2)
    cst[:, _off['gk']] = np.tile(np.asarray(g_k, np.float32), 2)
    cst[:, _off['bgate']:_off['bgate'] + 16] = fm(b_gate, 16)
    cw = np.asarray(conv_ssm_w, np.float32)
    cst[:, _off['cw']:_off['cw'] + 96] = cw.T.reshape(24, 128, 4).transpose(1, 0, 2).reshape(128, 96)
    cst[:, _off['cb']:_off['cb'] + 24] = fm(conv_ssm_b, 24)
    fw = np.asarray(ffn_conv_w, np.float32)
    cst[:, _off['fw']:_off['fw'] + 144] = fw.T.reshape(48, 128, 3).transpose(1, 0, 2).reshape(128, 144)
    cst[:, _off['fb']:_off['fb'] + 48] = fm(ffn_conv_b, 48)
    cst[:, _off['dtb']:_off['dtb'] + 32] = np.asarray(dt_bias, np.float32)[None, :]
    cst[:, _off['alog']:_off['alog'] + 32] = np.asarray(a_log, np.float32)[None, :]
    cst[:, _off['dskip']:_off['dskip'] + 32] = np.asarray(d_skip, np.float32)[None, :]
    cst[:, _off['gssm']:_off['gssm'] + 16] = fm(g_ssm, 16)
    tab = np.asarray(rel_bias, np.float32)
    cst[:, _off['ch']:_off['ch'] + 16] = tab[:, 256][None, :]
    j = np.arange(128)[:, None]
    i = np.arange(128)[None, :]
    bx = np.zeros((128, 16, 2, 128), np.float32)
    for t, base in enumerate((128, 0)):
        rel = np.clip(base + i - j, -128, 128) + 128
        bx[:, :, t, :] = tab[:, rel].transpose(1, 0, 2)
    mk = np.zeros((128, 5, 128), np.float32)
    mk[:, 0, :] = np.eye(128)
    mk[:, 1, :] = (j <= i)
    mk[:, 2, :] = (j > i)
    mk[:, 3, :] = 1.0
    mk[:, 4, :] = ((j // 64) == (i // 64))
    return cst, bx.reshape(128, 16 * 256), mk.reshape(128, 5 * 128)


_NC_CACHE = {}


def _run(inputs, SEQ, n_cores, SAMPLE=True):
    f = lambda k: np.asarray(inputs[k], np.float32)
    cst, bx, mk = _consts(f('g_mix')[0], f('g_ffn')[0], f('g_ple')[0], f('g_q')[0], f('g_k')[0], f('b_gate')[0], f('conv_ssm_w')[0],
                          f('conv_ssm_b')[0], f('ffn_conv_w')[0], f('ffn_conv_b')[0], f('dt_bias')[0], f('a_log')[0], f('d_skip')[0],
                          f('g_ssm')[0], f('rel_bias')[0])
    key = (SEQ, SAMPLE)
    if key not in _NC_CACHE:
        _NC_CACHE[key] = build(SEQ, 2, SAMPLE)
    nc = _NC_CACHE[key]
    shared = dict(cst=cst, biasx=bx, mk=mk, w_in=np.ascontiguousarray(f('w_in')[0]), w_pa=np.ascontiguousarray(f('w_proj_a')[0]),
                  w_pb=np.ascontiguousarray(f('w_proj_b')[0]), w_out=np.ascontiguousarray(f('w_out')[0]), w_up=np.ascontiguousarray(f('w_up')[0]),
                  w_down=np.ascontiguousarray(f('w_down')[0]), w_pg=np.ascontiguousarray(f('w_ple_gate')[0]), w_ple=np.ascontiguousarray(f('w_ple')[0]))
    xpr, ppr = f('x_prompt'), f('p_prompt')[0]
    in_maps = []
    for c in range(n_cores):
        m = dict(shared)
        m['xp'] = np.ascontiguousarray(xpr[2 * c:2 * c + 2])
        m['ppT'] = np.ascontiguousarray(ppr[2 * c:2 * c + 2].transpose(0, 2, 1))
        m['xsm'] = np.ascontiguousarray(f('x_sample')[c])
        m['psT'] = np.ascontiguousarray(f('p_sample')[0, c].T)
        ck = f('cache_k')[0, c]
        m['ckT'] = np.ascontiguousarray(ck.transpose(1, 2, 0).reshape(8, 128, 512).transpose(1, 0, 2))
        m['cv'] = np.ascontiguousarray(f('cache_v')[0, c].reshape(512, 1024))
        m['st_ssmT'] = np.ascontiguousarray(f('state_ssm')[0, c].reshape(2048, 128).T)
        m['st_convT'] = np.ascontiguousarray(f('state_conv_ssm')[0, c].T.reshape(24, 128, 3).transpose(1, 0, 2))
        m['st_ffnT'] = np.ascontiguousarray(f('state_conv_ffn')[0, c].T.reshape(48, 128, 2).transpose(1, 0, 2))
        in_maps.append(m)
    res = run_bass_kernel_spmd(nc, in_maps, core_ids=list(range(n_cores)))
    R = res.results
    LK = min(512, SEQ)
    cat = lambda k: np.concatenate([np.asarray(r[k]) for r in R], axis=0)
    stk = lambda k: np.stack([np.asarray(r[k]) for r in R], axis=0)
    yp = cat('yp')
    ys = stk('ysm')
    kp = cat('kp').reshape(-1, LK, 16, 64)[None]
    vp = cat('vp').reshape(-1, LK, 16, 64)[None]
    ssmp = cat('ssmp').reshape(-1, 32, 64, 128)[None]
    cssmp = cat('cssmp')[None]
    cffnp = cat('cffnp')[None]
    ks = stk('ksm').reshape(-1, 32, 16, 64)[None]
    vs = stk('vsm').reshape(-1, 32, 16, 64)[None]
    ssms = cat('ssms').reshape(-1, 32, 64, 128)[None]
    cssms = cat('cssms')[None]
    cffns = cat('cffns')[None]
    return tuple(np.ascontiguousarray(a, dtype=np.float32) for a in (yp, ys, kp, vp, ssmp, cssmp, cffnp, ks, vs, ssms, cssms, cffns))


def kernel(**inputs):
    SEQ = int(np.asarray(inputs['x_prompt']).shape[1])
    n_cores = int(np.asarray(inputs['x_prompt']).shape[0]) // 2
    return _run(inputs, SEQ, n_cores)
```
